# Optimizing a Trainium2 kernel written in Bass

```python
import math
import jax
import jax.numpy as jnp
from jax import lax
import numpy as np

D_MODEL = 2048
BATCH = 4
SEQ = 4096
DEPTH = 4

GRID_W = 64
CTX_LEN = 256
Q_BLOCK = 128
ROPE_BASE = 10000.0
NORM_EPS = 1e-6

DIFF_HEADS = 8
DIFF_HEAD_DIM = 64
DIFF_V_DIM = 2 * DIFF_HEAD_DIM
DIFF_SUBLN_EPS = 1e-5

GDN_HEADS = 8
GDN_DK = 128
GDN_DV = 128
GDN_CONV_K = 3
GDN_CHUNK = 64
GDN_CONV_DIM = GDN_HEADS * (2 * GDN_DK + GDN_DV)

MLA_HEADS = 8
MLA_Q_RANK = 512
MLA_KV_RANK = 512
MLA_NOPE_DIM = 128
MLA_ROPE_DIM = 64
MLA_V_DIM = 128

N_BRANCHES = 3
BRANCH_WIDTH = DIFF_HEADS * DIFF_V_DIM
FFN_HIDDEN = ((8 * D_MODEL + 3 * 256 - 1) // (3 * 256)) * 256

IN_LAYOUT = (
    ('diff_q', DIFF_HEADS * 2 * DIFF_HEAD_DIM),
    ('diff_k', DIFF_HEADS * 2 * DIFF_HEAD_DIM),
    ('diff_v', DIFF_HEADS * DIFF_V_DIM),
    ('gdn_q', GDN_HEADS * GDN_DK),
    ('gdn_k', GDN_HEADS * GDN_DK),
    ('gdn_v', GDN_HEADS * GDN_DV),
    ('gdn_z', GDN_HEADS * GDN_DV),
    ('gdn_a', 2 * GDN_HEADS),
    ('gdn_b', 2 * GDN_HEADS),
    ('mla_q', MLA_Q_RANK),
    ('mla_kv', MLA_KV_RANK),
    ('mla_kr', MLA_ROPE_DIM),
    ('gates', N_BRANCHES * D_MODEL),
)
IN_WIDTH = sum(width for _, width in IN_LAYOUT)

kernel_name = 'hybrid_diff_gdn_mla_prefix_dit'


def rms_norm(x, g, eps=NORM_EPS):
    xf = x.astype(jnp.float32)
    y = xf * lax.rsqrt(jnp.mean(xf * xf, axis=-1, keepdims=True) + eps)
    return (y * g.astype(jnp.float32)).astype(x.dtype)


def l2_normalize(x, eps=1e-6):
    xf = x.astype(jnp.float32)
    return xf * lax.rsqrt(jnp.sum(xf * xf, axis=-1, keepdims=True) + eps)


def modulation(cond, w_ada, b_ada):
    return jnp.split(jax.nn.silu(cond) @ w_ada + b_ada, 6, axis=-1)


def modulate(x, g, shift, scale):
    return rms_norm(x, g) * (1.0 + scale) + shift


def split_columns(p):
    out, start = {}, 0
    for name, width in IN_LAYOUT:
        out[name] = p[..., start:start + width]
        start += width
    return out


def axial_rope(rows, rot_dim):
    row_id = jnp.repeat(jnp.arange(rows), GRID_W).astype(jnp.float32)
    col_id = jnp.tile(jnp.arange(GRID_W), rows).astype(jnp.float32)
    n_freq = rot_dim // 4
    inv_freq = ROPE_BASE ** (-jnp.arange(n_freq, dtype=jnp.float32) / n_freq)
    ang = jnp.concatenate([row_id[:, None] * inv_freq, col_id[:, None] * inv_freq], axis=-1)
    return jnp.cos(ang), jnp.sin(ang)


def apply_rope(x, cos, sin):
    half = x.shape[-1] // 2
    shape = (cos.shape[0],) + (1,) * (x.ndim - 3) + (half,)
    c = cos.reshape(shape).astype(x.dtype)
    s = sin.reshape(shape).astype(x.dtype)
    x1, x2 = x[..., :half], x[..., half:]
    return jnp.concatenate([x1 * c - x2 * s, x2 * c + x1 * s], axis=-1)


def block_attention(q, k, v, scale):
    b, tq, h, m, d = q.shape
    nblk = tq // Q_BLOCK
    qb = jnp.moveaxis(q.reshape(b, nblk, Q_BLOCK, h, m, d), 1, 0)

    def one_block(qi):
        s = jnp.einsum('bqhmd,bkhmd->bhmqk', qi, k, preferred_element_type=jnp.float32) * scale
        p = jax.nn.softmax(s, axis=-1).astype(v.dtype)
        return jnp.einsum('bhmqk,bkhv->bqhmv', p, v)

    o = lax.map(one_block, qb)
    return jnp.moveaxis(o, 0, 1).reshape(b, tq, h, m, v.shape[-1])


def depthwise_conv(x, w):
    k = w.shape[0]
    return lax.conv_general_dilated(x, w[:, None, :], window_strides=(1,), padding=[(k // 2, k // 2)],
                                    dimension_numbers=('NWC', 'WIO', 'NWC'), feature_group_count=x.shape[-1])


def gated_delta_chunked(q, k, v, g, beta, state0):
    b, t, h, dk = q.shape
    dv = v.shape[-1]
    cs = GDN_CHUNK
    n = t // cs

    def chunks(a):
        return jnp.swapaxes(a.reshape((b, n, cs) + a.shape[2:]), 2, 3)

    qc, kc, vc, gc, bc = (chunks(a) for a in (q, k, v, g, beta))
    gcum = jnp.cumsum(gc, axis=-1)
    incl = jnp.tril(jnp.ones((cs, cs), bool))
    strict = jnp.tril(jnp.ones((cs, cs), bool), -1)
    rel = gcum[..., :, None] - gcum[..., None, :]
    decay = jnp.where(incl, jnp.exp(jnp.where(incl, rel, 0.0)), 0.0)
    kb = kc * bc[..., None]
    m = jnp.where(strict, jnp.einsum('bnhid,bnhjd->bnhij', kb, kc) * decay, 0.0)
    eye = jnp.eye(cs, dtype=jnp.float32)
    tinv = lax.linalg.triangular_solve(eye + m, jnp.broadcast_to(eye, m.shape), left_side=True, lower=True)
    u = jnp.einsum('bnhij,bnhjv->bnhiv', tinv, vc * bc[..., None])
    w = jnp.einsum('bnhij,bnhjd->bnhid', tinv, kb * jnp.exp(gcum)[..., None])
    a_intra = jnp.where(incl, jnp.einsum('bnhid,bnhjd->bnhij', qc, kc) * decay, 0.0)
    q_dec = qc * jnp.exp(gcum)[..., None]
    g_last = gcum[..., -1]
    k_tail = kc * jnp.exp(g_last[..., None] - gcum)[..., None]

    def step(state, xs):
        w_i, u_i, q_i, k_i, a_i, gl_i = xs
        v_new = u_i - jnp.einsum('bhcd,bhdv->bhcv', w_i, state)
        o_i = jnp.einsum('bhcd,bhdv->bhcv', q_i, state) + jnp.einsum('bhij,bhjv->bhiv', a_i, v_new)
        state = state * jnp.exp(gl_i)[..., None, None] + jnp.einsum('bhcd,bhcv->bhdv', k_i, v_new)
        return state, o_i

    xs = tuple(jnp.moveaxis(a, 1, 0) for a in (w, u, q_dec, k_tail, a_intra, g_last))
    state, o = lax.scan(step, state0, xs)
    o = jnp.swapaxes(jnp.moveaxis(o, 0, 1), 2, 3).reshape(b, t, h, dv)
    return o, state


def gdn_prepare(cols, conv_w, a_log, dt_bias):
    qkv = jnp.concatenate([cols['gdn_q'], cols['gdn_k'], cols['gdn_v']], axis=-1)
    qkv = jax.nn.silu(depthwise_conv(qkv, conv_w))
    b, t, _ = qkv.shape
    nq = GDN_HEADS * GDN_DK
    q = l2_normalize(qkv[..., :nq].reshape(b, t, GDN_HEADS, GDN_DK)) * (GDN_DK ** -0.5)
    k = l2_normalize(qkv[..., nq:2 * nq].reshape(b, t, GDN_HEADS, GDN_DK))
    v = qkv[..., 2 * nq:].reshape(b, t, GDN_HEADS, GDN_DV).astype(jnp.float32)
    a = cols['gdn_a'].astype(jnp.float32).reshape(b, t, 2, GDN_HEADS)
    g = -jnp.exp(a_log.astype(jnp.float32)) * jax.nn.softplus(a + dt_bias.astype(jnp.float32))
    beta = jax.nn.sigmoid(cols['gdn_b'].astype(jnp.float32).reshape(b, t, 2, GDN_HEADS))
    return q, k, v, g, beta


def gdn_mixer(cols_lat, cols_ctx, conv_w, a_log, dt_bias, out_norm, need_ctx):
    lat = gdn_prepare(cols_lat, conv_w, a_log, dt_bias)
    ctx = gdn_prepare(cols_ctx, conv_w, a_log, dt_bias)
    b = lat[0].shape[0]
    o_lat, o_ctx = 0.0, 0.0
    for direction in range(2):
        orient = (lambda a: jnp.flip(a, axis=1)) if direction == 1 else (lambda a: a)

        def inputs(prep):
            q, k, v, g, beta = prep
            return (orient(q), orient(k), orient(v), orient(g[:, :, direction]), orient(beta[:, :, direction]))

        state0 = jnp.zeros((b, GDN_HEADS, GDN_DK, GDN_DV), jnp.float32)
        oc, state_ctx = gated_delta_chunked(*inputs(ctx), state0)
        ol, _ = gated_delta_chunked(*inputs(lat), state_ctx)
        o_lat = o_lat + orient(ol)
        if need_ctx:
            o_ctx = o_ctx + orient(oc)

    def out_gate(o, z):
        bb, tt = z.shape[:2]
        o = rms_norm(o, out_norm).astype(z.dtype)
        return (o * jax.nn.silu(z.reshape(bb, tt, GDN_HEADS, GDN_DV))).reshape(bb, tt, BRANCH_WIDTH)

    y_lat = out_gate(o_lat, cols_lat['gdn_z'])
    y_ctx = out_gate(o_ctx, cols_ctx['gdn_z']) if need_ctx else None
    return y_lat, y_ctx


def diff_project(cols, rope):
    b, t, _ = cols['diff_q'].shape
    q = cols['diff_q'].reshape(b, t, DIFF_HEADS, 2, DIFF_HEAD_DIM)
    k = cols['diff_k'].reshape(b, t, DIFF_HEADS, 2, DIFF_HEAD_DIM)
    v = cols['diff_v'].reshape(b, t, DIFF_HEADS, DIFF_V_DIM)
    if rope is not None:
        q = apply_rope(q, *rope)
        k = apply_rope(k, *rope)
    return q, k, v


def mla_project(cols, g_q_lat, w_q_up, g_kv_lat, w_kv_up, rope):
    b, t, _ = cols['mla_q'].shape
    q = (rms_norm(cols['mla_q'], g_q_lat) @ w_q_up).reshape(b, t, MLA_HEADS, MLA_NOPE_DIM + MLA_ROPE_DIM)
    kv = (rms_norm(cols['mla_kv'], g_kv_lat) @ w_kv_up).reshape(b, t, MLA_HEADS, MLA_NOPE_DIM + MLA_V_DIM)
    q_nope, q_rope = q[..., :MLA_NOPE_DIM], q[..., MLA_NOPE_DIM:]
    k_nope, v = kv[..., :MLA_NOPE_DIM], kv[..., MLA_NOPE_DIM:]
    k_rope = cols['mla_kr']
    if rope is not None:
        q_rope = apply_rope(q_rope, *rope)
        k_rope = apply_rope(k_rope, *rope)
    k_rope = jnp.broadcast_to(k_rope[:, :, None, :], (b, t, MLA_HEADS, MLA_ROPE_DIM))
    q = jnp.concatenate([q_nope, q_rope], axis=-1)[:, :, :, None, :]
    k = jnp.concatenate([k_nope, k_rope], axis=-1)[:, :, :, None, :]
    return q, k, v


def token_mix(h_lat, h_ctx, p, lam_init, rope_diff, rope_mla, need_ctx):
    cl = split_columns(h_lat @ p['w_in'])
    cc = split_columns(h_ctx @ p['w_in'])

    ql, kl, vl = diff_project(cl, rope_diff)
    qc, kc, vc = diff_project(cc, None)
    f32 = jnp.float32
    lam = (jnp.exp(jnp.sum(p['lam_q1'].astype(f32) * p['lam_k1'].astype(f32)))
           - jnp.exp(jnp.sum(p['lam_q2'].astype(f32) * p['lam_k2'].astype(f32))) + lam_init)

    def diff_out(o):
        a = o[:, :, :, 0] - lam.astype(o.dtype) * o[:, :, :, 1]
        a = rms_norm(a, p['diff_subln'], DIFF_SUBLN_EPS) * (1.0 - lam_init)
        return a.reshape(a.shape[0], a.shape[1], BRANCH_WIDTH)

    scale_d = DIFF_HEAD_DIM ** -0.5
    ya_lat = diff_out(block_attention(ql, jnp.concatenate([kl, kc], axis=1), jnp.concatenate([vl, vc], axis=1), scale_d))
    ya_ctx = diff_out(block_attention(qc, kc, vc, scale_d)) if need_ctx else None

    yb_lat, yb_ctx = gdn_mixer(cl, cc, p['gdn_conv'], p['gdn_a_log'], p['gdn_dt_bias'], p['gdn_out_norm'], need_ctx)

    mla_w = (p['mla_q_norm'], p['mla_q_up'], p['mla_kv_norm'], p['mla_kv_up'])
    ql, kl, vl = mla_project(cl, *mla_w, rope_mla)
    qc, kc, vc = mla_project(cc, *mla_w, None)
    scale_m = (MLA_NOPE_DIM + MLA_ROPE_DIM) ** -0.5
    ol = block_attention(ql, jnp.concatenate([kl, kc], axis=1), jnp.concatenate([vl, vc], axis=1), scale_m)
    yc_lat = ol[:, :, :, 0].reshape(ol.shape[0], ol.shape[1], BRANCH_WIDTH)
    yc_ctx = None
    if need_ctx:
        oc = block_attention(qc, kc, vc, scale_m)
        yc_ctx = oc[:, :, :, 0].reshape(oc.shape[0], oc.shape[1], BRANCH_WIDTH)

    def merge(cols, ys):
        b, t, _ = cols['gates'].shape
        gates = jax.nn.sigmoid(cols['gates'].astype(f32)).astype(ys[0].dtype).reshape(b, t, N_BRANCHES, D_MODEL)
        y = gates[:, :, 0] * (ys[0] @ p['w_branch'][0])
        for i in range(1, N_BRANCHES):
            y = y + gates[:, :, i] * (ys[i] @ p['w_branch'][i])
        return y @ p['w_out']

    y_lat = merge(cl, (ya_lat, yb_lat, yc_lat))
    y_ctx = merge(cc, (ya_ctx, yb_ctx, yc_ctx)) if need_ctx else None
    return y_lat, y_ctx


def swiglu(h, w_gate, w_up, w_down):
    return (jax.nn.silu(h @ w_gate) * (h @ w_up)) @ w_down


def setup_inputs(seed: int = 0) -> dict:
    key = jax.random.key(seed)
    ks = jax.random.split(key, 40)
    counter = iter(range(40))
    f32 = jnp.float32
    L, D = DEPTH, D_MODEL

    def nk():
        return ks[next(counter)]

    def normal(shape, scale):
        return jax.random.normal(nk(), shape, f32) * scale

    def gain(shape):
        return 1.0 + 0.1 * jax.random.normal(nk(), shape, f32)

    dt = jnp.exp(jax.random.uniform(nk(), (L, 2, GDN_HEADS), f32, math.log(1e-3), math.log(1e-1)))
    dt_bias = dt + jnp.log(-jnp.expm1(-dt))
    a_log = jnp.log(jax.random.uniform(nk(), (L, 2, GDN_HEADS), f32, 1.0, 16.0))
    return {
        'x': normal((BATCH, SEQ, D), 1.0),
        'c': normal((BATCH, D), 1.0),
        'ctx': normal((BATCH, CTX_LEN, D), 1.0),
        'c_ctx': normal((D,), 1.0),
        'w_ada': normal((L, D, 6 * D), D ** -0.5),
        'b_ada': normal((L, 6 * D), 0.02),
        'g_pre_mix': gain((L, D)),
        'g_post_mix': gain((L, D)),
        'g_pre_ffn': gain((L, D)),
        'g_post_ffn': gain((L, D)),
        'w_in': normal((L, D, IN_WIDTH), D ** -0.5),
        'diff_lam_q1': normal((L, DIFF_HEAD_DIM), 0.1),
        'diff_lam_k1': normal((L, DIFF_HEAD_DIM), 0.1),
        'diff_lam_q2': normal((L, DIFF_HEAD_DIM), 0.1),
        'diff_lam_k2': normal((L, DIFF_HEAD_DIM), 0.1),
        'diff_subln': gain((L, DIFF_V_DIM)),
        'gdn_conv': normal((L, GDN_CONV_K, GDN_CONV_DIM), GDN_CONV_K ** -0.5),
        'gdn_a_log': a_log,
        'gdn_dt_bias': dt_bias,
        'gdn_out_norm': gain((L, GDN_DV)),
        'mla_q_norm': gain((L, MLA_Q_RANK)),
        'mla_q_up': normal((L, MLA_Q_RANK, MLA_HEADS * (MLA_NOPE_DIM + MLA_ROPE_DIM)), MLA_Q_RANK ** -0.5),
        'mla_kv_norm': gain((L, MLA_KV_RANK)),
        'mla_kv_up': normal((L, MLA_KV_RANK, MLA_HEADS * (MLA_NOPE_DIM + MLA_V_DIM)), MLA_KV_RANK ** -0.5),
        'w_branch': normal((L, N_BRANCHES, BRANCH_WIDTH, D), BRANCH_WIDTH ** -0.5),
        'w_out': normal((L, D, D), D ** -0.5),
        'w_ffn_gate': normal((L, D, FFN_HIDDEN), D ** -0.5),
        'w_ffn_up': normal((L, D, FFN_HIDDEN), D ** -0.5),
        'w_ffn_down': normal((L, FFN_HIDDEN, D), FFN_HIDDEN ** -0.5),
    }


def reference(x, c, ctx, c_ctx, w_ada, b_ada, g_pre_mix, g_post_mix, g_pre_ffn, g_post_ffn, w_in,
              diff_lam_q1, diff_lam_k1, diff_lam_q2, diff_lam_k2, diff_subln,
              gdn_conv, gdn_a_log, gdn_dt_bias, gdn_out_norm,
              mla_q_norm, mla_q_up, mla_kv_norm, mla_kv_up,
              w_branch, w_out, w_ffn_gate, w_ffn_up, w_ffn_down):
    seq = x.shape[1]
    rows = seq // GRID_W
    rope_diff = axial_rope(rows, DIFF_HEAD_DIM)
    rope_mla = axial_rope(rows, MLA_ROPE_DIM)
    for l in range(DEPTH):
        need_ctx = l < DEPTH - 1
        lam_init = 0.8 - 0.6 * math.exp(-0.3 * l)
        sh_m, sc_m, gt_m, sh_f, sc_f, gt_f = [m[:, None, :] for m in modulation(c, w_ada[l], b_ada[l])]
        csh_m, csc_m, cgt_m, csh_f, csc_f, cgt_f = modulation(c_ctx, w_ada[l], b_ada[l])
        p = {
            'w_in': w_in[l],
            'lam_q1': diff_lam_q1[l], 'lam_k1': diff_lam_k1[l],
            'lam_q2': diff_lam_q2[l], 'lam_k2': diff_lam_k2[l],
            'diff_subln': diff_subln[l],
            'gdn_conv': gdn_conv[l], 'gdn_a_log': gdn_a_log[l], 'gdn_dt_bias': gdn_dt_bias[l],
            'gdn_out_norm': gdn_out_norm[l],
            'mla_q_norm': mla_q_norm[l], 'mla_q_up': mla_q_up[l],
            'mla_kv_norm': mla_kv_norm[l], 'mla_kv_up': mla_kv_up[l],
            'w_branch': w_branch[l], 'w_out': w_out[l],
        }
        h_lat = modulate(x, g_pre_mix[l], sh_m, sc_m)
        h_ctx = modulate(ctx, g_pre_mix[l], csh_m, csc_m)
        y_lat, y_ctx = token_mix(h_lat, h_ctx, p, lam_init, rope_diff, rope_mla, need_ctx)
        x = x + gt_m * rms_norm(y_lat, g_post_mix[l])
        f_lat = swiglu(modulate(x, g_pre_ffn[l], sh_f, sc_f), w_ffn_gate[l], w_ffn_up[l], w_ffn_down[l])
        x = x + gt_f * rms_norm(f_lat, g_post_ffn[l])
        if need_ctx:
            ctx = ctx + cgt_m * rms_norm(y_ctx, g_post_mix[l])
            f_ctx = swiglu(modulate(ctx, g_pre_ffn[l], csh_f, csc_f), w_ffn_gate[l], w_ffn_up[l], w_ffn_down[l])
            ctx = ctx + cgt_f * rms_norm(f_ctx, g_post_ffn[l])
    return x
```

```python
import math
from contextlib import ExitStack

import numpy as np
import ml_dtypes

import concourse.bass as bass
import concourse.mybir as mybir
from concourse.bass_utils import run_bass_kernel_spmd

F32 = mybir.dt.float32
BF16 = mybir.dt.bfloat16
AF = mybir.ActivationFunctionType
ALU = mybir.AluOpType

D = 2048
KC = 16
T = 4352
NCTX = 256
FF = 5632
FKC = 44
DEPTH = 4
NT = 34
TB = [(0, 256)] + [(256 + 512 * i, 512) for i in range(8)]
SCS = [TB[0:5], TB[5:9]]
WIN_EXT = 16544
EPOCH = 30000


class Res:
    __slots__ = ("name", "lw", "rd")

    def __init__(self, name=""):
        self.name = name
        self.lw = None
        self.rd = {}


class Prog:
    ENGS = ("pe", "act", "dve", "pool", "sp")

    def __init__(self, nc, n_dma_sems=16, n_cast_sems=4):
        self.nc = nc
        self.h = {"pe": nc.tensor, "act": nc.scalar, "dve": nc.vector, "pool": nc.gpsimd, "sp": nc.sync}
        self.sems = {}
        self.cur = {}
        self.prev_final = {}
        self.epoch = {e: 0 for e in self.ENGS}
        self.waited = {e: {} for e in self.ENGS}
        for e in ("pe", "act", "dve", "pool"):
            self._new_epoch(e)
        self.dma_sems = []
        for i in range(n_dma_sems):
            k = ("dma", i)
            self.sems[k] = nc.alloc_semaphore(name=f"dma{i}")
            self.dma_sems.append([k, 0])
        self.cast_sems = []
        for i in range(n_cast_sems):
            k = ("cast", i)
            self.sems[k] = nc.alloc_semaphore(name=f"cast{i}")
            self.cast_sems.append([k, 0])
        self.dma_rr = 0
        self.cast_rr = 0
        self.n_instr = 0
        self.n_wait = 0
        self.pending = {e: ([], []) for e in self.ENGS}

    def _new_epoch(self, e):
        if e in self.cur:
            if not hasattr(self, "prev_final"):
                self.prev_final = {}
            self.prev_final[e] = (self.cur[e][0], self.cur[e][1])
        k = (e, self.epoch[e])
        self.epoch[e] += 1
        self.sems[k] = self.nc.alloc_semaphore(name=f"s_{e}_{k[1]}")
        self.cur[e] = [k, 0]

    def _need(self, eng, tok, waits):
        if tok is None:
            return
        k, v = tok
        if eng == "pe" and k[0] == "pe":
            return
        if self.waited[eng].get(k, 0) >= v:
            return
        if waits.get(k, 0) < v:
            waits[k] = v

    def _deps(self, eng, reads, writes):
        waits = {}
        for r in reads:
            self._need(eng, r.lw, waits)
        for w in writes:
            self._need(eng, w.lw, waits)
            for k, v in w.rd.items():
                self._need(eng, (k, v), waits)
        return waits

    def _emit_waits(self, eng, waits):
        for k, v in waits.items():
            if getattr(self, "trace", False):
                print("   WAIT", eng, k, v)
            self.h[eng].wait_ge(self.sems[k], v)
            self.waited[eng][k] = v
            self.n_wait += 1

    def _mark(self, tok, reads, writes):
        k, v = tok
        for r in reads:
            if r.rd.get(k, 0) < v:
                r.rd[k] = v
        for w in writes:
            w.lw = tok
            w.rd = {}

    def op(self, eng, fn, reads=(), writes=(), inc=True):
        waits = self._deps(eng, reads, writes)
        self._emit_waits(eng, waits)
        if not inc:
            fn(self.h[eng])
            self.pending[eng][0].extend(reads)
            self.pending[eng][1].extend(writes)
            self.n_instr += 1
            return None
        if self.pending[eng][0] or self.pending[eng][1]:
            reads = list(reads) + self.pending[eng][0]
            writes = list(writes) + self.pending[eng][1]
            self.pending[eng] = ([], [])
        cur = self.cur[eng]
        if cur[1] >= EPOCH:
            self._new_epoch(eng)
            cur = self.cur[eng]
        cur[1] += 1
        tok = (cur[0], cur[1])
        if getattr(self, "trace", False):
            print("OP", eng, tok)
        fn(self.h[eng]).then_inc(self.sems[cur[0]], 1)
        self._mark(tok, reads, writes)
        self.n_instr += 1
        return tok

    def dma(self, q, out, in_, reads=(), writes=(), cast=False, **kw):
        waits = self._deps(q, reads, writes)
        if cast:
            slot = self.cast_sems[self.cast_rr]
            self.cast_rr = (self.cast_rr + 1) % len(self.cast_sems)
        else:
            slot = self.dma_sems[self.dma_rr]
            self.dma_rr = (self.dma_rr + 1) % len(self.dma_sems)
        k = slot[0]
        if slot[1] > 0:
            self._need(q, (k, slot[1]), waits)
        self._emit_waits(q, waits)
        slot[1] += 16
        tok = (k, slot[1])
        self.h[q].dma_start(out=out, in_=in_, **kw).then_inc(self.sems[k], 16)
        self._mark(tok, reads, writes)
        self.n_instr += 1
        return tok

    def barrier(self):
        toks = [(c[0], c[1]) for c in self.cur.values() if c[1] > 0] + [(s[0], s[1]) for s in self.dma_sems if s[1] > 0]
        toks += list(self.prev_final.values())
        for e in self.ENGS:
            waits = {}
            for t in toks:
                self._need(e, t, waits)
            self._emit_waits(e, waits)


P64 = np.concatenate([np.arange(32, 64), np.arange(0, 32)])
P128 = np.concatenate([P64, 64 + P64])


def _win_cols():
    dq, dk, dv, gq, gz, ga, mq, mkv, kr, gates = 0, 1024, 2048, 3072, 6144, 7168, 7200, 7712, 8224, 8288
    cols = []
    for base in (dq, dk):
        for h in range(8):
            cols.append(base + h * 128 + np.arange(128))
            cols.append(base + h * 128 + P128)
    cols.append(gq + np.arange(3072))
    cols.append(gz + np.arange(1024))
    cols.append(mq + np.arange(512))
    cols.append(mkv + np.arange(512))
    cols.append(gates + np.arange(6144))
    cols.append(kr + np.arange(64))
    cols.append(kr + P64)
    cols.append(dv + np.arange(1024))
    cols.append(ga + np.arange(32))
    c = np.concatenate(cols)
    assert c.shape[0] == WIN_EXT
    return c


def _mqup_cols():
    cols = []
    for h in range(8):
        cols.append(h * 192 + np.arange(128))
    for j in range(4):
        a = np.concatenate([(2 * j) * 192 + 128 + np.arange(64), (2 * j + 1) * 192 + 128 + np.arange(64)])
        b = np.concatenate([(2 * j) * 192 + 128 + P64, (2 * j + 1) * 192 + 128 + P64])
        cols.append(a)
        cols.append(b)
    return np.concatenate(cols)


def _mkvup_cols():
    cols = []
    for h in range(8):
        cols.append(h * 256 + np.arange(128))
    for h in range(8):
        cols.append(h * 256 + 128 + np.arange(128))
    return np.concatenate(cols)


def _fm(v, kc):
    return np.ascontiguousarray(v.reshape(kc, 128).T)


NV = 96 + 64 + 256 + 1 + 1 + 4 + 4 + 72 + 544 + 544 + 2
V_LAMI = 1586
V_BADA, V_G, V_LAM, V_SUBLN, V_ONORM, V_MQN, V_MKVN, V_CONV, V_ALOG, V_DTB = 0, 96, 160, 416, 417, 418, 422, 426, 498, 1042


def _pack_vec(inp, l):
    v = np.zeros((128, NV), np.float32)
    v[:, V_BADA:V_BADA + 96] = _fm(inp["b_ada"][l], 96)
    for i, nm in enumerate(("g_pre_mix", "g_post_mix", "g_pre_ffn", "g_post_ffn")):
        v[:, V_G + 16 * i:V_G + 16 * (i + 1)] = _fm(inp[nm][l], 16)
    lam = np.concatenate([inp["diff_lam_q1"][l], inp["diff_lam_k1"][l], inp["diff_lam_q2"][l], inp["diff_lam_k2"][l]])
    v[:, V_LAM:V_LAM + 256] = np.broadcast_to(lam[None, :], (128, 256))
    v[:, V_SUBLN] = inp["diff_subln"][l]
    v[:, V_ONORM] = inp["gdn_out_norm"][l]
    v[:, V_MQN:V_MQN + 4] = _fm(inp["mla_q_norm"][l], 4)
    v[:, V_MKVN:V_MKVN + 4] = _fm(inp["mla_kv_norm"][l], 4)
    cw = inp["gdn_conv"][l]
    v[:, V_CONV:V_CONV + 72] = np.ascontiguousarray(cw.reshape(3, 24, 128).transpose(2, 1, 0)).reshape(128, 72)
    al = inp["gdn_a_log"][l].reshape(16)
    db = inp["gdn_dt_bias"][l].reshape(16)
    v[:, V_ALOG:V_ALOG + 544] = np.broadcast_to(np.tile(al, NT)[None, :], (128, 544))
    v[:, V_DTB:V_DTB + 544] = np.broadcast_to(np.tile(db, NT)[None, :], (128, 544))
    lam_init = 0.8 - 0.6 * math.exp(-0.3 * l)
    v[:, V_LAMI] = lam_init
    v[:, V_LAMI + 1] = 1.0 - lam_init
    return v


def _consts():
    i = np.arange(128)
    ident = np.eye(128, dtype=np.float32)
    tri_f = (i[:, None] <= i[None, :]).astype(np.float32)
    tri_b = (i[:, None] >= i[None, :]).astype(np.float32)
    msf = -(i[None, :] > i[:, None]).astype(np.float32)
    mif = (i[None, :] >= i[:, None]).astype(np.float32)
    msb = -(i[None, :] < i[:, None]).astype(np.float32)
    mib = (i[None, :] <= i[:, None]).astype(np.float32)
    def bd(sz):
        return ((i[:, None] // sz) == (i[None, :] // sz)).astype(np.float32)
    bd16, o32, o64, o128 = bd(16), bd(32) - bd(16), bd(64) - bd(32), 1.0 - bd(64)
    c128 = np.concatenate([ident, tri_f, tri_b, msf, mif, msb, mib, np.ones((128, 128), np.float32), bd16, o32, o64, o128], axis=1)
    sel = np.zeros((128, 32, 128), np.float32)
    for r in range(32):
        sel[r, r, :] = 1.0
    rows = 4096 // 64
    row_id = np.repeat(np.arange(rows), 64).astype(np.float32)
    col_id = np.tile(np.arange(64), rows).astype(np.float32)
    inv_freq = (np.float32(10000.0) ** (-np.arange(16, dtype=np.float32) / np.float32(16))).astype(np.float32)
    ang = np.concatenate([row_id[:, None] * inv_freq, col_id[:, None] * inv_freq], axis=-1).astype(np.float32)
    cos = np.cos(ang).astype(np.float32)
    sin = np.sin(ang).astype(np.float32)
    ctab = np.ones((128, T), np.float32)
    stab = np.zeros((128, T), np.float32)
    for r in range(128):
        ctab[r, NCTX:] = cos[:, r % 32]
        sgn = -1.0 if (r % 64) < 32 else 1.0
        stab[r, NCTX:] = sgn * sin[:, r % 32]
    return c128, sel.reshape(128, 32 * 128), ctab, stab


C_ID, C_TRIF, C_TRIB, C_MSF, C_MIF, C_MSB, C_MIB, C_ONE, C_BD16, C_O32, C_O64, C_O128 = [128 * i for i in range(12)]


class KB:
    def __init__(self, layers, stop_after=None, dbg=(), skip=()):
        self.skip = set(skip)
        self.layers = list(layers)
        self.stop_after = stop_after
        self.dbg = set(dbg)
        nc = self.nc = bass.Bass("TRN2", target_bir_lowering=False)
        self.P = Prog(nc)
        self._uid = 0
        self.xin = self.dram("xin", [D, T], F32, "ExternalInput")
        self.xout = self.dram("xout", [D, T], F32, "ExternalOutput")
        self.cT_d = self.dram("cT", [128, 32], F32, "ExternalInput")
        self.c128_d = self.dram("c128", [128, 1536], F32, "ExternalInput")
        self.ctab_d = self.dram("ctab", [128, T], F32, "ExternalInput")
        self.stab_d = self.dram("stab", [128, T], F32, "ExternalInput")
        self.W = {}
        wshapes = {"w_ada": [D, 6 * D], "w_in": [D, WIN_EXT], "mqup": [512, 2048], "mkvup": [512, 2048],
                   "wbr": [3072, D], "wout": [D, D], "wg": [D, FF], "wu": [D, FF], "wd": [FF, D]}
        for l in self.layers:
            for nm, shp in wshapes.items():
                self.W[(nm, l)] = self.dram(f"{nm}{l}", shp, F32, "ExternalInput")
            self.W[("vec", l)] = self.dram(f"vec{l}", [128, NV], F32, "ExternalInput")
        self.WB = {}
        self.rWB = {}
        for l in self.layers:
            for nm, shp in wshapes.items():
                if nm == "w_ada":
                    continue
                self.WB[(nm, l)] = self.dram(f"{nm}b{l}", shp, BF16)
                self.rWB[(nm, l)] = [Res() for _ in range(shp[0] // 128)]
        sd = self.sdram
        self.S_QD = sd("S_QD", [1024, T], BF16)
        self.S_KD = sd("S_KD", [1024, T], BF16)
        self.S_VD = sd("S_VD", [T, 1024], BF16)
        self.S_G = sd("S_G", [3072, T], F32)
        self.S_Z = sd("S_Z", [1024, T], BF16)
        self.S_AB = sd("S_AB", [T, 32], F32)
        self.S_MQ = sd("S_MQ", [512, T], BF16)
        self.S_MKV = sd("S_MKV", [512, T], BF16)
        self.S_KR = sd("S_KR", [64, T], BF16)
        self.S_GT = sd("S_GT", [6144, T], BF16)
        self.S_MQN = sd("S_MQN", [1024, T], BF16)
        self.S_MQR = sd("S_MQR", [512, T], BF16)
        self.S_MKN = sd("S_MKN", [1024, T], BF16)
        self.S_VM = sd("S_VM", [T, 1024], BF16)
        self.S_YA = sd("S_YA", [1024, T], BF16)
        self.S_YB = sd("S_YB", [1024, T], BF16)
        self.S_YC = sd("S_YC", [1024, T], BF16)
        self.S_OT = [sd("S_OT0", [1024, T], F32), sd("S_OT1", [1024, T], F32)]
        self.S_YS = sd("S_YS", [D, T], BF16)
        self.S_H2 = sd("S_H2", [D, T], BF16)
        self.S_ACT = sd("S_ACT", [FF, T], BF16)
        self.S_F = sd("S_F", [D, T], F32)
        self.ps = []
        self.rps = []
        for i in range(8):
            t = nc.alloc_psum_tensor(f"psb{i}", [128, 512], F32)
            self.ps.append(t)
            self.rps.append(Res(f"ps{i}"))
        self.bank_rr = 0
        self.c128 = nc.alloc_sbuf_tensor("c128s", [128, 1536], F32)
        self.ones_bf = nc.alloc_sbuf_tensor("ones_bf", [128, 128], BF16)
        self.cols = nc.alloc_sbuf_tensor("ccols", [128, 8], F32)
        self.r_const = Res("const")
        self.vec = nc.alloc_sbuf_tensor("vecs", [128, NV], F32)
        self.r_vec = Res("vec")
        self.mod = nc.alloc_sbuf_tensor("mods", [128, 96, 2], F32)
        self.r_mod = Res("mod")
        self.mv = nc.alloc_sbuf_tensor("mvs", [128, 6, 16, 2], F32)
        self.lamc = nc.alloc_sbuf_tensor("lamc", [128, 4], F32)
        P = self.P
        P.dma("sp", self.c128[:], self.c128_d, writes=[self.r_const])
        P.op("dve", lambda h: h.memset(self.ones_bf[:], 1.0), writes=[self.r_const])
        for i, val in enumerate((1.0, 1e-6, 1e-5, 0.0)):
            P.op("dve", lambda h: h.memset(self.cols[:, i:i + 1], val), writes=[self.r_const])

    def dram(self, name, shape, dt, kind="Internal"):
        return self.nc.dram_tensor(name, shape, dt, kind=kind).ap()

    def sdram(self, name, shape, dt):
        kind = "ExternalOutput" if name in self.dbg else "Internal"
        return self.nc.dram_tensor(name, shape, dt, kind=kind).ap()

    def sb(self, st, shape, dt, name=None):
        self._uid += 1
        t = st.enter_context(self.nc.sbuf_tensor(f"{name or 't'}_{self._uid}", shape, dt))
        return t, Res(name or "t")

    def pool(self, st, n, shape, dt, name=None):
        items = [self.sb(st, shape, dt, name) for _ in range(n)]
        state = {"i": 0}

        def nxt():
            it = items[state["i"] % n]
            state["i"] += 1
            return it
        return nxt

    def bank(self):
        i = self.bank_rr
        self.bank_rr = (self.bank_rr + 1) % 8
        return self.ps[i], self.rps[i]

    def ident(self):
        return self.c128[:, C_ID:C_ID + 128]

    def onesf(self):
        return self.c128[:, C_ONE:C_ONE + 128]

    def issue_casts(self, l):
        for nm in ("w_in", "mqup", "mkvup", "wbr", "wout", "wg", "wu", "wd"):
            src = self.W[(nm, l)]
            dst = self.WB[(nm, l)]
            rows = src.shape[0]
            for rb in range(rows // 128):
                self.P.dma("pool", dst[rb * 128:(rb + 1) * 128, :], src[rb * 128:(rb + 1) * 128, :],
                           writes=[self.rWB[(nm, l)][rb]], cast=True)

    def wload(self, q, dst_ap, nm, l, kcs, c0, c1, res):
        src = self.WB[(nm, l)]
        k0, k1 = kcs[0], kcs[-1] + 1
        self.P.dma(q, dst_ap, src[k0 * 128:k1 * 128, c0:c1].rearrange("(k p) n -> p k n", p=128),
                   reads=self.rWB[(nm, l)][k0:k1], writes=[res])

    def fm_rstd(self, chunks, n, scale, eps_idx, sqpool, out_ap, r_out, bank=None):
        P = self.P
        ps, rps = self.bank() if bank is None else (self.ps[bank], self.rps[bank])
        C = len(chunks)
        for c, (ap, r) in enumerate(chunks):
            sq, rsq = sqpool()
            P.op("act", lambda h: h.activation(sq[:, :n], ap, AF.Square), reads=[r], writes=[rsq])
            P.op("pe", lambda h: h.matmul(ps[:, :n], self.ones_bf[:], sq[:, :n], start=(c == 0), stop=(c == C - 1)),
                 reads=[rsq, self.r_const], writes=[rps], inc=(c == C - 1))
        P.op("act", lambda h: h.activation(out_ap, ps[:, :n], AF.Sqrt, bias=self.cols[:, eps_idx:eps_idx + 1], scale=scale),
             reads=[rps, self.r_const], writes=[r_out])
        P.op("dve", lambda h: h.reciprocal(out_ap, out_ap), reads=[r_out], writes=[r_out])

    def phase_mod(self, l):
        P = self.P
        with ExitStack() as st:
            sc, r_sc = self.sb(st, [128, 16, 2], F32, "sc")
            wp = self.pool(st, 2, [128, 16, 512], F32, "wada")
            P.dma("sp", self.vec[:], self.W[("vec", l)], writes=[self.r_vec])
            P.dma("sp", sc[:], self.cT_d.rearrange("p (k v) -> p k v", v=2), writes=[r_sc])
            P.op("act", lambda h: h.activation(sc[:], sc[:], AF.Silu), reads=[r_sc], writes=[r_sc])
            wsrc = self.W[("w_ada", l)]
            for jg in range(24):
                wt, rwt = wp()
                P.dma("sp", wt[:], wsrc[:, jg * 512:(jg + 1) * 512].rearrange("(k p) n -> p k n", p=128), writes=[rwt])
                for jb in range(4):
                    j = jg * 4 + jb
                    ps, rps = self.bank()
                    for k in range(16):
                        P.op("pe", lambda h: h.matmul(ps[:, 0:2], wt[:, k, jb * 128:(jb + 1) * 128], sc[:, k, :],
                                                      start=(k == 0), stop=(k == 15)),
                             reads=[rwt, r_sc], writes=[rps], inc=(k == 15))
                    P.op("dve", lambda h: h.tensor_single_scalar(self.mod[:, j, :], ps[:, 0:2], self.vec[:, V_BADA + j:V_BADA + j + 1], ALU.add),
                         reads=[rps, self.r_vec], writes=[self.r_mod])
            mv, mod, vec = self.mv, self.mod, self.vec
            for v in range(2):
                for (i_out, sc_off, g_off) in ((0, 16, 0), (3, 64, 2)):
                    P.op("dve", lambda h: h.tensor_single_scalar(mv[:, i_out, :, v], mod[:, sc_off:sc_off + 16, v], 1.0, ALU.add),
                         reads=[self.r_mod], writes=[self.r_mod])
                    P.op("dve", lambda h: h.tensor_tensor(mv[:, i_out, :, v], mv[:, i_out, :, v], vec[:, V_G + 16 * g_off:V_G + 16 * g_off + 16], ALU.mult),
                         reads=[self.r_mod, self.r_vec], writes=[self.r_mod])
                for (i_out, sh_off) in ((1, 0), (4, 48)):
                    P.op("dve", lambda h: h.tensor_copy(mv[:, i_out, :, v], mod[:, sh_off:sh_off + 16, v]),
                         reads=[self.r_mod], writes=[self.r_mod])
                for (i_out, gt_off, g_off) in ((2, 32, 1), (5, 80, 3)):
                    P.op("dve", lambda h: h.tensor_tensor(mv[:, i_out, :, v], mod[:, gt_off:gt_off + 16, v], vec[:, V_G + 16 * g_off:V_G + 16 * g_off + 16], ALU.mult),
                         reads=[self.r_mod, self.r_vec], writes=[self.r_mod])
            tmp, r_tmp = self.sb(st, [128, 128], F32, "lamtmp")
            s2, r_s2 = self.sb(st, [128, 2], F32, "lams")
            lamv = vec[:, V_LAM:V_LAM + 256]
            P.op("dve", lambda h: h.tensor_tensor(tmp[:, 0:64], vec[:, V_LAM:V_LAM + 64], vec[:, V_LAM + 64:V_LAM + 128], ALU.mult),
                 reads=[self.r_vec], writes=[r_tmp])
            P.op("dve", lambda h: h.tensor_tensor(tmp[:, 64:128], vec[:, V_LAM + 128:V_LAM + 192], vec[:, V_LAM + 192:V_LAM + 256], ALU.mult),
                 reads=[self.r_vec], writes=[r_tmp])
            P.op("dve", lambda h: h.reduce_sum(s2[:, 0:1], tmp[:, 0:64], mybir.AxisListType.X), reads=[r_tmp], writes=[r_s2])
            P.op("dve", lambda h: h.reduce_sum(s2[:, 1:2], tmp[:, 64:128], mybir.AxisListType.X), reads=[r_tmp], writes=[r_s2])
            P.op("act", lambda h: h.activation(s2[:], s2[:], AF.Exp), reads=[r_s2], writes=[r_s2])
            P.op("dve", lambda h: h.scalar_tensor_tensor(self.lamc[:, 0:1], s2[:, 1:2], vec[:, V_LAMI:V_LAMI + 1], s2[:, 0:1], ALU.subtract, ALU.subtract),
                 reads=[r_s2, self.r_vec], writes=[self.r_mod])
            P.op("dve", lambda h: h.tensor_tensor(self.lamc[:, 1:2], vec[:, V_SUBLN:V_SUBLN + 1], vec[:, V_LAMI + 1:V_LAMI + 2], ALU.mult),
                 reads=[self.r_vec], writes=[self.r_mod])
            P.barrier()

    def rope_epi(self, st_pools, psa, rpa, psb, rpb, m, n, ctab, stab, r_tab, toff, dst_ap):
        P = self.P
        f32p, bfp = st_pools
        t1, r1 = f32p()
        t2, r2 = f32p()
        o, ro = bfp()
        P.op("dve", lambda h: h.tensor_tensor(t1[0:m, :n], psa[0:m, :n], ctab[0:m, toff:toff + n], ALU.mult), reads=[rpa, r_tab], writes=[r1])
        P.op("dve", lambda h: h.tensor_tensor(t2[0:m, :n], psb[0:m, :n], stab[0:m, toff:toff + n], ALU.mult), reads=[rpb, r_tab], writes=[r2])
        P.op("pool", lambda h: h.tensor_tensor(o[0:m, :n], t1[0:m, :n], t2[0:m, :n], ALU.add), reads=[r1, r2], writes=[ro])
        P.dma("sp", dst_ap, o[0:m, :n], reads=[ro])

    def phase_p1(self, l, xsrc):
        P = self.P
        mv = self.mv
        for sci, blocks in enumerate(SCS):
            sc0 = blocks[0][0]
            ntok = sum(b[1] for b in blocks)
            with ExitStack() as st:
                H, r_H = self.sb(st, [128, 16, 2304], BF16, "H")
                ctab, r_tab = self.sb(st, [128, 2304], F32, "ctab")
                stab, _ = self.sb(st, [128, 2304], F32, "stab")
                P.dma("sp", ctab[:, :ntok], self.ctab_d[:, sc0:sc0 + ntok], writes=[r_tab])
                P.dma("sp", stab[:, :ntok], self.stab_d[:, sc0:sc0 + ntok], writes=[r_tab])
                with ExitStack() as st2:
                    xp = self.pool(st2, 1, [128, 16, 512], F32, "xt")
                    sqp = self.pool(st2, 3, [128, 512], BF16, "sq")
                    rsp = self.pool(st2, 2, [128, 512], F32, "rstd")
                    for (t0, n) in blocks:
                        v = 1 if t0 < NCTX else 0
                        off = t0 - sc0
                        xt, r_xt = xp()
                        P.dma("sp", xt[:, :, :n], xsrc[:, t0:t0 + n].rearrange("(k p) n -> p k n", p=128), writes=[r_xt])
                        rstd, r_rs = rsp()
                        self.fm_rstd([(xt[:, k, :n], r_xt) for k in range(16)], n, 1.0 / D, 1, sqp, rstd[:, :n], r_rs)
                        for k in range(16):
                            P.op("dve", lambda h: h.tensor_tensor(xt[:, k, :n], xt[:, k, :n], rstd[:, :n], ALU.mult), reads=[r_xt, r_rs], writes=[r_xt])
                            P.op("act", lambda h: h.activation(H[:, k, off:off + n], xt[:, k, :n], AF.Identity,
                                                               bias=mv[:, 1, k, v:v + 1], scale=mv[:, 0, k, v:v + 1]),
                                 reads=[r_xt, self.r_mod], writes=[r_H])
                    P.barrier()
                wp = self.pool(st, 2, [128, 16, 512], BF16, "win")
                f32p = self.pool(st, 4, [128, 512], F32, "stf")
                bfp = self.pool(st, 4, [128, 512], BF16, "stb")
                mqs, r_mqs = self.sb(st, [128, 4, 512], F32, "mqs")
                sqp = self.pool(st, 3, [128, 512], BF16, "sq")
                rsp = self.pool(st, 2, [128, 512], F32, "rstd")
                evi = [0]

                def mm_fm(ps, rps, wt, rwt, c0, m, off, n):
                    for k in range(16):
                        P.op("pe", lambda h: h.matmul(ps[0:m, :n], wt[:, k, c0:c0 + m], H[:, k, off:off + n], start=(k == 0), stop=(k == 15)),
                             reads=[rwt, r_H], writes=[rps], inc=(k == 15))

                for g in range(31):
                    wt, rwt = wp()
                    ncol = 512 if g < 30 else 128
                    self.wload("sp", wt[:, :, :ncol], "w_in", l, list(range(16)), g * 512, g * 512 + ncol, rwt)
                    for (t0, n) in blocks:
                        off = t0 - sc0
                        if g < 8:
                            for pr in range(2):
                                b = g * 4 + pr * 2
                                psa, rpa = self.bank()
                                mm_fm(psa, rpa, wt, rwt, pr * 256, 128, off, n)
                                psb, rpb = self.bank()
                                mm_fm(psb, rpb, wt, rwt, pr * 256 + 128, 128, off, n)
                                dst = self.S_QD if b < 16 else self.S_KD
                                hh = (b % 16) // 2
                                self.rope_epi((f32p, bfp), psa, rpa, psb, rpb, 128, n, ctab, stab, r_tab, off,
                                              dst[hh * 128:(hh + 1) * 128, t0:t0 + n])
                        elif g == 30:
                            psa, rpa = self.bank()
                            mm_fm(psa, rpa, wt, rwt, 0, 64, off, n)
                            psb, rpb = self.bank()
                            mm_fm(psb, rpb, wt, rwt, 64, 64, off, n)
                            self.rope_epi((f32p, bfp), psa, rpa, psb, rpb, 64, n, ctab, stab, r_tab, off, self.S_KR[:, t0:t0 + n])
                        elif g in (16, 17):
                            pss = []
                            for jb in range(4):
                                ps, rps = self.bank()
                                mm_fm(ps, rps, wt, rwt, jb * 128, 128, off, n)
                                P.op("act" if jb % 2 == 0 else "dve",
                                     (lambda h: h.activation(mqs[:, jb, :n], ps[:, :n], AF.Copy)) if jb % 2 == 0 else
                                     (lambda h: h.tensor_copy(mqs[:, jb, :n], ps[:, :n])),
                                     reads=[rps], writes=[r_mqs])
                            rstd, r_rs = rsp()
                            self.fm_rstd([(mqs[:, jb, :n], r_mqs) for jb in range(4)], n, 1.0 / 512, 1, sqp, rstd[:, :n], r_rs)
                            gofs = V_MQN if g == 16 else V_MKVN
                            dst = self.S_MQ if g == 16 else self.S_MKV
                            for jb in range(4):
                                o, ro = bfp()
                                P.op("dve", lambda h: h.tensor_tensor(mqs[:, jb, :n], mqs[:, jb, :n], rstd[:, :n], ALU.mult), reads=[r_mqs, r_rs], writes=[r_mqs])
                                P.op("act", lambda h: h.activation(o[:, :n], mqs[:, jb, :n], AF.Identity, scale=self.vec[:, gofs + jb:gofs + jb + 1]),
                                     reads=[r_mqs, self.r_vec], writes=[ro])
                                P.dma("sp", dst[jb * 128:(jb + 1) * 128, t0:t0 + n], o[:, :n], reads=[ro])
                        else:
                            for jb in range(4):
                                b = g * 4 + jb
                                ps, rps = self.bank()
                                mm_fm(ps, rps, wt, rwt, jb * 128, 128, off, n)
                                if b < 56:
                                    o, ro = f32p()
                                    evi[0] += 1
                                    if evi[0] % 2 == 0:
                                        P.op("act", lambda h: h.activation(o[:, :n], ps[:, :n], AF.Copy), reads=[rps], writes=[ro])
                                    else:
                                        P.op("dve", lambda h: h.tensor_copy(o[:, :n], ps[:, :n]), reads=[rps], writes=[ro])
                                    P.dma("sp", self.S_G[(b - 32) * 128:(b - 31) * 128, t0:t0 + n], o[:, :n], reads=[ro])
                                elif b < 64:
                                    o, ro = bfp()
                                    P.op("act", lambda h: h.activation(o[:, :n], ps[:, :n], AF.Silu), reads=[rps], writes=[ro])
                                    P.dma("sp", self.S_Z[(b - 56) * 128:(b - 55) * 128, t0:t0 + n], o[:, :n], reads=[ro])
                                else:
                                    o, ro = bfp()
                                    P.op("act", lambda h: h.activation(o[:, :n], ps[:, :n], AF.Sigmoid), reads=[rps], writes=[ro])
                                    P.dma("sp", self.S_GT[(b - 72) * 128:(b - 71) * 128, t0:t0 + n], o[:, :n], reads=[ro])
                for gi in range(3):
                    wt, rwt = wp()
                    c0 = 15488 + gi * 512
                    ncol = 512 if gi < 2 else 32
                    self.wload("sp", wt[:, :, :ncol], "w_in", l, list(range(16)), c0, c0 + ncol, rwt)
                    for tt in range(ntok // 128):
                        ps, rps = self.bank()
                        for k in range(16):
                            P.op("pe", lambda h: h.matmul(ps[:, :ncol], H[:, k, tt * 128:(tt + 1) * 128], wt[:, k, :ncol], start=(k == 0), stop=(k == 15)),
                                 reads=[rwt, r_H], writes=[rps], inc=(k == 15))
                        trow = sc0 + tt * 128
                        if gi < 2:
                            o, ro = bfp()
                            if tt % 2 == 0:
                                P.op("act", lambda h: h.activation(o[:, :], ps[:, :], AF.Copy), reads=[rps], writes=[ro])
                            else:
                                P.op("dve", lambda h: h.tensor_copy(o[:, :], ps[:, :]), reads=[rps], writes=[ro])
                            P.dma("sp", self.S_VD[trow:trow + 128, gi * 512:(gi + 1) * 512], o[:, :], reads=[ro])
                        else:
                            o, ro = f32p()
                            P.op("dve", lambda h: h.tensor_copy(o[:, :32], ps[:, :32]), reads=[rps], writes=[ro])
                            P.dma("sp", self.S_AB[trow:trow + 128, :], o[:, :32], reads=[ro])
                P.barrier()

    def phase_mla_up(self, l):
        P = self.P
        with ExitStack() as st:
            wq, r_wq = self.sb(st, [128, 4, 2048], BF16, "wq")
            wkv, r_wkv = self.sb(st, [128, 4, 2048], BF16, "wkv")
            self.wload("sp", wq[:], "mqup", l, [0, 1, 2, 3], 0, 2048, r_wq)
            self.wload("sp", wkv[:], "mkvup", l, [0, 1, 2, 3], 0, 2048, r_wkv)
            ap = self.pool(st, 2, [128, 8, 512], BF16, "mqkv")
            tabp = self.pool(st, 2, [128, 2, 512], F32, "tab")
            f32p = self.pool(st, 4, [128, 512], F32, "stf")
            bfp = self.pool(st, 6, [128, 512], BF16, "stb")
            ev = [0]

            def copy_out(ps, rps, n, dst):
                o, ro = bfp()
                ev[0] += 1
                if ev[0] % 2 == 0:
                    P.op("act", lambda h: h.activation(o[:, :n], ps[:, :n], AF.Copy), reads=[rps], writes=[ro])
                else:
                    P.op("dve", lambda h: h.tensor_copy(o[:, :n], ps[:, :n]), reads=[rps], writes=[ro])
                P.dma("sp", dst, o[:, :n], reads=[ro])

            for (t0, n) in TB:
                a, r_a = ap()
                P.dma("sp", a[:, 0:4, :n], self.S_MQ[:, t0:t0 + n].rearrange("(k p) n -> p k n", p=128), writes=[r_a])
                P.dma("sp", a[:, 4:8, :n], self.S_MKV[:, t0:t0 + n].rearrange("(k p) n -> p k n", p=128), writes=[r_a])
                tab, r_tab = tabp()
                P.dma("sp", tab[:, 0, :n], self.ctab_d[:, t0:t0 + n], writes=[r_tab])
                P.dma("sp", tab[:, 1, :n], self.stab_d[:, t0:t0 + n], writes=[r_tab])

                def mm(ps, rps, w, rw, c0, kofs):
                    for k in range(4):
                        P.op("pe", lambda h: h.matmul(ps[:, :n], w[:, k, c0:c0 + 128], a[:, kofs + k, :n], start=(k == 0), stop=(k == 3)),
                             reads=[rw, r_a], writes=[rps], inc=(k == 3))
                for hh in range(8):
                    ps, rps = self.bank()
                    mm(ps, rps, wq, r_wq, hh * 128, 0)
                    copy_out(ps, rps, n, self.S_MQN[hh * 128:(hh + 1) * 128, t0:t0 + n])
                for j in range(4):
                    psa, rpa = self.bank()
                    mm(psa, rpa, wq, r_wq, 1024 + j * 256, 0)
                    psb, rpb = self.bank()
                    mm(psb, rpb, wq, r_wq, 1024 + j * 256 + 128, 0)
                    self.rope_epi((f32p, bfp), psa, rpa, psb, rpb, 128, n, tab[:, 0, :], tab[:, 1, :], r_tab, 0,
                                  self.S_MQR[j * 128:(j + 1) * 128, t0:t0 + n])
                for hh in range(8):
                    ps, rps = self.bank()
                    mm(ps, rps, wkv, r_wkv, hh * 128, 4)
                    copy_out(ps, rps, n, self.S_MKN[hh * 128:(hh + 1) * 128, t0:t0 + n])
                for tt in range(n // 128):
                    for half in range(2):
                        ps, rps = self.bank()
                        for k in range(4):
                            P.op("pe", lambda h: h.matmul(ps[:, :], a[:, 4 + k, tt * 128:(tt + 1) * 128], wkv[:, k, 1024 + half * 512:1024 + (half + 1) * 512],
                                                          start=(k == 0), stop=(k == 3)),
                                 reads=[r_wkv, r_a], writes=[rps], inc=(k == 3))
                        copy_out(ps, rps, 512, self.S_VM[t0 + tt * 128:t0 + (tt + 1) * 128, half * 512:(half + 1) * 512])
            P.barrier()

    def attn_core(self, kts, n, s_ops, V, r_V, scale, ptp, sbanks, ol):
        P = self.P
        (O, rO), (L, rL) = ol
        nk = len(kts)
        look = 2
        pts = {}

        def emit_s(idx):
            kt = kts[idx]
            S, rS = sbanks()
            for j, (lf, rhs, rd) in enumerate(s_ops):
                P.op("pe", lambda h: h.matmul(S[:, :n], lf(kt), rhs, start=(j == 0), stop=(j == len(s_ops) - 1)),
                     reads=rd, writes=[rS], inc=(j == len(s_ops) - 1))
            Pt, rPt = ptp()
            P.op("act", lambda h: h.activation(Pt[:, :n], S[:, :n], AF.Exp, scale=scale), reads=[rS], writes=[rPt])
            pts[idx] = (Pt, rPt)

        def emit_pv(idx):
            kt = kts[idx]
            Pt, rPt = pts.pop(idx)
            P.op("pe", lambda h: h.matmul(O[:, :n], V[:, kt, :], Pt[:, :n], start=(idx == 0), stop=(idx == nk - 1)),
                 reads=[r_V, rPt], writes=[rO], inc=False)
            P.op("pe", lambda h: h.matmul(L[:, :n], self.ones_bf[:], Pt[:, :n], start=(idx == 0), stop=(idx == nk - 1)),
                 reads=[rPt, self.r_const], writes=[rL], inc=True)

        for idx in range(min(look, nk)):
            emit_s(idx)
        for idx in range(nk):
            if idx + look < nk:
                emit_s(idx + look)
            emit_pv(idx)

    def phase_attn(self, l):
        P = self.P
        with ExitStack() as st:
            ktp = self.pool(st, 2, [128, T], BF16, "kt")
            vp = self.pool(st, 2, [128, NT, 128], BF16, "v")
            kr, r_kr = self.sb(st, [64, T], BF16, "kr")
            qp = self.pool(st, 2, [128, 512], BF16, "q")
            qrp = self.pool(st, 2, [64, 512], BF16, "qr")
            ptp = self.pool(st, 5, [128, 512], BF16, "pt")
            omp = self.pool(st, 4, [128, 512], F32, "om")
            rlp = self.pool(st, 2, [128, 512], F32, "rl")
            bfp = self.pool(st, 3, [128, 512], BF16, "stb")
            sqp = self.pool(st, 2, [128, 512], BF16, "sq")
            rsp = self.pool(st, 2, [128, 512], F32, "rstd")
            srr = [0]

            def sbanks():
                i = srr[0] % 3
                srr[0] += 1
                return self.ps[i], self.rps[i]
            olr = [0]

            def olpair():
                i = olr[0] % 2
                olr[0] += 1
                return (self.ps[3 + i], self.rps[3 + i]), (self.ps[5 + i], self.rps[5 + i])

            P.dma("sp", kr[:], self.S_KR, writes=[r_kr])
            items = [(kind, hh, tb) for kind in ("diff", "mla") for hh in range(8) for tb in TB]
            loaded = {}

            def load(item):
                kind, hh, (t0, n) = item
                key = (kind, hh)
                if key not in loaded:
                    kt, r_kt = ktp()
                    v, r_v = vp()
                    ksrc = self.S_KD if kind == "diff" else self.S_MKN
                    vsrc = self.S_VD if kind == "diff" else self.S_VM
                    P.dma("sp", kt[:], ksrc[hh * 128:(hh + 1) * 128, :], writes=[r_kt])
                    P.dma("sp", v[:], vsrc[:, hh * 128:(hh + 1) * 128].rearrange("(t p) c -> p t c", p=128), writes=[r_v])
                    loaded.clear()
                    loaded[key] = (kt, r_kt, v, r_v)
                q, r_q = qp()
                qsrc = self.S_QD if kind == "diff" else self.S_MQN
                P.dma("sp", q[:, :n], qsrc[hh * 128:(hh + 1) * 128, t0:t0 + n], writes=[r_q])
                qr = r_qr = None
                if kind == "mla":
                    qr, r_qr = qrp()
                    P.dma("sp", qr[:, :n], self.S_MQR[hh * 64:(hh + 1) * 64, t0:t0 + n], writes=[r_qr])
                return loaded[key] + (q, r_q, qr, r_qr)

            nxt = load(items[0])
            for ii, item in enumerate(items):
                kind, hh, (t0, n) = item
                kt, r_kt, v, r_v, q, r_q, qr, r_qr = nxt
                if ii + 1 < len(items):
                    nxt = load(items[ii + 1])
                kts = [0, 1] if t0 < NCTX else list(range(NT))
                if kind == "diff":
                    oms = []
                    for m in range(2):
                        ol = olpair()
                        s_ops = [(lambda k_, m=m: kt[m * 64:(m + 1) * 64, k_ * 128:(k_ + 1) * 128], q[m * 64:(m + 1) * 64, :n], [r_kt, r_q])]
                        self.attn_core(kts, n, s_ops, v, r_v, 0.125, ptp, sbanks, ol)
                        (O, rO), (L, rL) = ol
                        rl, r_rl = rlp()
                        om, r_om = omp()
                        P.op("dve", lambda h: h.reciprocal(rl[:, :n], L[:, :n]), reads=[rL], writes=[r_rl])
                        P.op("dve", lambda h: h.tensor_tensor(om[:, :n], O[:, :n], rl[:, :n], ALU.mult), reads=[rO, r_rl], writes=[r_om])
                        oms.append((om, r_om))
                    (o0, r0), (o1, r1) = oms
                    P.op("dve", lambda h: h.scalar_tensor_tensor(o0[:, :n], o1[:, :n], self.lamc[:, 0:1], o0[:, :n], ALU.mult, ALU.add),
                         reads=[r0, r1, self.r_mod], writes=[r0])
                    rstd, r_rs = rsp()
                    self.fm_rstd([(o0[:, :n], r0)], n, 1.0 / 128, 2, sqp, rstd[:, :n], r_rs, bank=7)
                    P.op("dve", lambda h: h.tensor_tensor(o0[:, :n], o0[:, :n], rstd[:, :n], ALU.mult), reads=[r0, r_rs], writes=[r0])
                    o, ro = bfp()
                    P.op("act", lambda h: h.activation(o[:, :n], o0[:, :n], AF.Identity, scale=self.lamc[:, 1:2]), reads=[r0, self.r_mod], writes=[ro])
                    P.dma("sp", self.S_YA[hh * 128:(hh + 1) * 128, t0:t0 + n], o[:, :n], reads=[ro])
                else:
                    ol = olpair()
                    s_ops = [(lambda k_: kt[:, k_ * 128:(k_ + 1) * 128], q[:, :n], [r_kt, r_q]),
                             (lambda k_: kr[0:64, k_ * 128:(k_ + 1) * 128], qr[0:64, :n], [r_kr, r_qr])]
                    self.attn_core(kts, n, s_ops, v, r_v, 192.0 ** -0.5, ptp, sbanks, ol)
                    (O, rO), (L, rL) = ol
                    rl, r_rl = rlp()
                    P.op("dve", lambda h: h.reciprocal(rl[:, :n], L[:, :n]), reads=[rL], writes=[r_rl])
                    o, ro = bfp()
                    P.op("dve", lambda h: h.tensor_tensor(o[:, :n], O[:, :n], rl[:, :n], ALU.mult), reads=[rO, r_rl], writes=[ro])
                    P.dma("sp", self.S_YC[hh * 128:(hh + 1) * 128, t0:t0 + n], o[:, :n], reads=[ro])
            P.barrier()

    def phase_gdn(self, l):
        P = self.P
        c128 = self.c128
        ident = self.ident()
        with ExitStack() as st:
            be2, r_be2 = self.sb(st, [128, 2, NT * 8], F32, "be2")
            gcum, r_gc = self.sb(st, [128, 2, NT * 8], F32, "gcum")
            egc, r_egc = self.sb(st, [128, 2, NT * 8], F32, "egc")
            etl, r_etl = self.sb(st, [128, 2, NT * 8], F32, "etl")
            egl, r_egl = self.sb(st, [128, 2, NT * 8], F32, "egl")
            bege, r_bege = self.sb(st, [128, 2, NT * 8], F32, "bege")
            st0 = ExitStack()
            ab, r_ab = self.sb(st0, [128, NT, 32], F32, "ab")
            t1, r_t1 = self.sb(st0, [128, NT, 16], F32, "t1")
            t2, r_t2 = self.sb(st0, [128, NT, 16], F32, "t2")
            g2, r_g2 = self.sb(st0, [128, 2, NT * 8], F32, "g2")
            gtot, r_gt = self.sb(st0, [128, 2, NT * 8], F32, "gtot")
            vec = self.vec
            P.dma("sp", ab[:], self.S_AB.rearrange("(t p) c -> p t c", p=128), writes=[r_ab])
            dtb = vec[:, V_DTB:V_DTB + 544].rearrange("p (t c) -> p t c", c=16)
            alog = vec[:, V_ALOG:V_ALOG + 544].rearrange("p (t c) -> p t c", c=16)
            P.op("dve", lambda h: h.tensor_tensor(t1[:], ab[:, :, 0:16], dtb, ALU.add), reads=[r_ab, self.r_vec], writes=[r_t1])
            P.op("act", lambda h: h.activation(t1[:], t1[:], AF.Exp), reads=[r_t1], writes=[r_t1])
            P.op("act", lambda h: h.activation(t1[:], t1[:], AF.Ln, bias=self.cols[:, 0:1]), reads=[r_t1, self.r_const], writes=[r_t1])
            P.op("act", lambda h: h.activation(t2[:], alog, AF.Exp), reads=[self.r_vec], writes=[r_t2])
            P.op("dve", lambda h: h.scalar_tensor_tensor(t1[:], t1[:], -1.0, t2[:], ALU.mult, ALU.mult), reads=[r_t1, r_t2], writes=[r_t1])
            P.op("act", lambda h: h.activation(t2[:], ab[:, :, 16:32], AF.Sigmoid), reads=[r_ab, r_t1], writes=[r_t2])
            for d in range(2):
                P.op("dve", lambda h: h.tensor_copy(g2[:, d, :].rearrange("p (t c) -> p t c", c=8), t1[:, :, d * 8:(d + 1) * 8]), reads=[r_t1], writes=[r_g2])
                P.op("dve", lambda h: h.tensor_copy(be2[:, d, :].rearrange("p (t c) -> p t c", c=8), t2[:, :, d * 8:(d + 1) * 8]), reads=[r_t2], writes=[r_be2])
            for d in range(2):
                tri = c128[:, (C_TRIF if d == 0 else C_TRIB):(C_TRIF if d == 0 else C_TRIB) + 128]
                ps, rps = self.bank()
                P.op("pe", lambda h: h.matmul(ps[:, :272], tri, g2[:, d, :], start=True, stop=True), reads=[r_g2, self.r_const], writes=[rps])
                P.op("dve", lambda h: h.tensor_copy(gcum[:, d, :], ps[:, :272]), reads=[rps], writes=[r_gc])
                ps, rps = self.bank()
                P.op("pe", lambda h: h.matmul(ps[:, :272], self.onesf(), g2[:, d, :], start=True, stop=True), reads=[r_g2, self.r_const], writes=[rps])
                P.op("dve", lambda h: h.tensor_copy(gtot[:, d, :], ps[:, :272]), reads=[rps], writes=[r_gt])
            P.op("act", lambda h: h.activation(egc[:], gcum[:], AF.Exp), reads=[r_gc], writes=[r_egc])
            P.op("act", lambda h: h.activation(egl[:], gtot[:], AF.Exp), reads=[r_gt], writes=[r_egl])
            P.op("dve", lambda h: h.tensor_tensor(etl[:], gtot[:], gcum[:], ALU.subtract), reads=[r_gt, r_gc], writes=[r_etl])
            P.op("act", lambda h: h.activation(etl[:], etl[:], AF.Exp), reads=[r_etl], writes=[r_etl])
            P.op("dve", lambda h: h.tensor_tensor(bege[:], be2[:], egc[:], ALU.mult), reads=[r_be2, r_egc], writes=[r_bege])
            P.barrier()
            st0.close()
            stats = dict(gcum=gcum, egc=egc, etl=etl, egl=egl, beta=be2, bege=bege)

            B, r_B = self.sb(st, [128, 4360], F32, "cbuf")
            qT, r_qT = self.sb(st, [128, T], F32, "qT")
            kT, r_kT = self.sb(st, [128, T], F32, "kT")
            ktm, r_ktm = self.sb(st, [128, NT, 128], F32, "ktm")
            vtm, r_vtm = self.sb(st, [128, NT, 128], F32, "vtm")
            sqp = self.pool(st, 2, [128, 512], BF16, "sq")
            rsp = self.pool(st, 2, [128, 512], F32, "rstd")
            Sst = [self.sb(st, [128, 128], F32, "S0"), self.sb(st, [128, 128], F32, "S1")]
            ostg = [self.pool(st, 2, [128, 4, 128], F32, "ostg0"), self.pool(st, 2, [128, 4, 128], F32, "ostg1")]
            tmps = [{}, {}]

            def tmp(c, name, n=2):
                if name not in tmps[c]:
                    tmps[c][name] = self.pool(st, n, [128, 128], F32, f"{name}{c}")
                    if _os.environ.get('GDN_TRACE'):
                        print("TMP", name, c, "sbuf_base", self.nc.sbuf_base)
                return tmps[c][name]()
            qres = [[Res(f"q{c}_{i}") for i in range(4)] for c in range(2)]
            qrr = [0, 0]

            def qps(c):
                i = qrr[c] % 4
                qrr[c] += 1
                return self.ps[4 * c + i][:, 0:128], qres[c][i]

            P.op("dve", lambda h: h.memset(B[:], 0.0), writes=[r_B])
            import os as _os
            for c_ in range(2):
                for nm_, n_ in (("Dg", 1), ("Db", 1), ("dpre", 1), ("DT", 1), ("Er", 1), ("DTs", 1), ("DTi", 1), ("KKs", 1), ("Yt", 1), ("Y0", 2), ("AiT", 2), ("qd", 2), ("X", 2), ("Yd", 1), ("Xd", 1), ("R", 3), ("Xn", 2), ("Yn", 2), ("Inv", 3), ("Xb", 1), ("Qs", 1), ("Yb", 1), ("Q2s", 1), ("vb", 2), ("kbg", 2), ("ktl", 2), ("u", 2), ("wT", 2), ("vn", 2)):
                    tmp(c_, nm_, n_)
            if _os.environ.get('GDN_STOP') == 'g0':
                return
            for hh in range(int(_os.environ.get('GDN_HEADS', '8'))):
                for typ, (Y, r_Y) in ((2, (kT, r_kT)), (1, (kT, r_kT)), (0, (qT, r_qT))):
                    blk = typ * 8 + hh
                    src = self.S_G[blk * 128:(blk + 1) * 128, :]
                    P.dma("sp", B[:, 1:257], src[:, 0:256], writes=[r_B])
                    P.dma("sp", B[:, 259:4355], src[:, 256:T], writes=[r_B])
                    w = [vec[:, V_CONV + blk * 3 + tp:V_CONV + blk * 3 + tp + 1] for tp in range(3)]
                    for (s0, ln, y0) in ((1, 256, 0), (259, 4096, 256)):
                        e1 = "dve"
                        P.op(e1, lambda h: h.tensor_single_scalar(Y[:, y0:y0 + ln], B[:, s0:s0 + ln], w[1], ALU.mult), reads=[r_B, self.r_vec], writes=[r_Y])
                        P.op(e1, lambda h: h.scalar_tensor_tensor(Y[:, y0:y0 + ln], B[:, s0 - 1:s0 - 1 + ln], w[0], Y[:, y0:y0 + ln], ALU.mult, ALU.add), reads=[r_B, self.r_vec, r_Y], writes=[r_Y])
                        P.op(e1, lambda h: h.scalar_tensor_tensor(Y[:, y0:y0 + ln], B[:, s0 + 1:s0 + 1 + ln], w[2], Y[:, y0:y0 + ln], ALU.mult, ALU.add), reads=[r_B, self.r_vec, r_Y], writes=[r_Y])
                    P.op("act", lambda h: h.activation(Y[:], Y[:], AF.Silu), reads=[r_Y], writes=[r_Y])
                    if typ < 2:
                        for (t0, n) in TB:
                            rstd, r_rs = rsp()
                            self.fm_rstd([(Y[:, t0:t0 + n], r_Y)], n, 1.0, 1, sqp, rstd[:, :n], r_rs)
                            sc_ = (128.0 ** -0.5) if typ == 0 else 1.0
                            P.op("dve", lambda h: h.scalar_tensor_tensor(Y[:, t0:t0 + n], Y[:, t0:t0 + n], sc_, rstd[:, :n], ALU.mult, ALU.mult), reads=[r_Y, r_rs], writes=[r_Y])
                    if typ in (1, 2):
                        dst, r_d = (ktm, r_ktm) if typ == 1 else (vtm, r_vtm)
                        for t4 in range(0, NT, 4):
                            ps, rps = self.bank()
                            nn = min(4, NT - t4)
                            for j in range(nn):
                                P.op("pe", lambda h: h.transpose(ps[:, j * 128:(j + 1) * 128], Y[:, (t4 + j) * 128:(t4 + j + 1) * 128], ident), reads=[r_Y, self.r_const], writes=[rps], inc=(j == nn - 1))
                            if (t4 // 4) % 2 == 0:
                                P.op("act", lambda h: h.activation(dst[:, t4:t4 + nn, :], ps[:, :nn * 128].rearrange("p (t c) -> p t c", c=128), AF.Copy), reads=[rps], writes=[r_d])
                            else:
                                P.op("dve", lambda h: h.tensor_copy(dst[:, t4:t4 + nn, :], ps[:, :nn * 128].rearrange("p (t c) -> p t c", c=128)), reads=[rps], writes=[r_d])
                P.barrier()
                if _os.environ.get('GDN_STOP') == 'g1':
                    continue
                gens = [self.gdn_chain(c, hh, stats, qT, r_qT, kT, r_kT, ktm, r_ktm, vtm, r_vtm, Sst[c], ostg[c], tmp, qps) for c in range(2)]
                P.trace = bool(_os.environ.get('GDN_TRACE'))
                alive = [True, True]
                budget = int(_os.environ.get('GDN_OPS', '100000000'))
                while any(alive) and budget > 0:
                    for c in range(2):
                        if alive[c]:
                            try:
                                next(gens[c])
                                budget -= 1
                            except StopIteration:
                                alive[c] = False
                P.barrier()

    def gdn_chain(self, c, hh, stats, qT, r_qT, kT, r_kT, ktm, r_ktm, vtm, r_vtm, Sres, ostg, tmp, qps):
        P = self.P
        c128 = self.c128
        S, r_S = Sres
        order = list(range(NT)) if c == 0 else [1, 0] + list(range(NT - 1, 1, -1))
        mS = c128[:, (C_MSF if c == 0 else C_MSB):(C_MSF if c == 0 else C_MSB) + 128]
        mI = c128[:, (C_MIF if c == 0 else C_MIB):(C_MIF if c == 0 else C_MIB) + 128]
        ident = self.ident()
        rc = self.r_const
        P.op("pool", lambda h: h.memset(S[:], 0.0), writes=[r_S])
        yield
        ost = None
        import os as _os
        order = order[:int(_os.environ.get('GDN_STEPS', '34'))]
        for si, t in enumerate(order):
            ts = slice(t * 128, (t + 1) * 128)
            ci = t * 8 + hh

            def col(nm):
                return stats[nm][:, c, ci:ci + 1]
            r_st = [Res()]
            Dr, rDr = qps(c)
            Br, rBr = qps(c)
            Dg, r_Dg = tmp(c, "Dg", 1)
            Db, r_Db = tmp(c, "Db", 1)
            P.op("pool", lambda h: h.tensor_single_scalar(Dg[:], ident, col("gcum"), ALU.mult), reads=[rc], writes=[r_Dg]); yield
            P.op("pool", lambda h: h.tensor_single_scalar(Db[:], ident, col("beta"), ALU.mult), reads=[rc], writes=[r_Db]); yield
            P.op("pe", lambda h: h.matmul(Dr, self.onesf(), Dg[:], start=True, stop=True), reads=[r_Dg, rc], writes=[rDr]); yield
            P.op("pe", lambda h: h.matmul(Br, self.onesf(), Db[:], start=True, stop=True), reads=[r_Db, rc], writes=[rBr]); yield
            KK, rKK = qps(c)
            QK, rQK = qps(c)
            P.op("pe", lambda h: h.matmul(KK, kT[:, ts], kT[:, ts], start=True, stop=True), reads=[r_kT], writes=[rKK]); yield
            P.op("pe", lambda h: h.matmul(QK, kT[:, ts], qT[:, ts], start=True, stop=True), reads=[r_kT, r_qT], writes=[rQK]); yield
            dpre, r_dpre = tmp(c, "dpre", 1)
            P.op("dve", lambda h: h.tensor_scalar(dpre[:], Dr, col("gcum"), 0.0, ALU.subtract, ALU.min), reads=[], writes=[r_dpre, rDr]); yield
            DT, r_DT = tmp(c, "DT", 1)
            P.op("act", lambda h: h.activation(DT[:], dpre[:], AF.Exp), reads=[r_dpre], writes=[r_DT]); yield
            Er, r_Er = tmp(c, "Er", 1)
            import os as _o3
            if _o3.environ.get("GDN_NOER"):
                P.op("act", lambda h: h.activation(Er[:], dpre[:], AF.Exp), reads=[r_dpre], writes=[r_Er]); yield
            else:
                P.op("act", lambda h: h.activation(Er[:], Dr, AF.Exp), reads=[], writes=[r_Er, rDr]); yield
            DTs, r_DTs = tmp(c, "DTs", 1)
            P.op("pool", lambda h: h.tensor_tensor(DTs[:], DT[:], mS, ALU.mult), reads=[r_DT, rc], writes=[r_DTs]); yield
            DTi, r_DTi = tmp(c, "DTi", 1)
            P.op("pool", lambda h: h.tensor_tensor(DTi[:], DT[:], mI, ALU.mult), reads=[r_DT, rc], writes=[r_DTi]); yield
            KKs, r_KKs = tmp(c, "KKs", 1)
            P.op("act", lambda h: h.activation(KKs[:], KK, AF.Copy), reads=[], writes=[r_KKs, rKK]); yield
            Yt, r_Yt = tmp(c, "Yt", 1)
            P.op("dve", lambda h: h.tensor_tensor(Yt[:], KKs[:], Br, ALU.mult), reads=[r_KKs], writes=[r_Yt, rBr]); yield
            Y0, r_Y0 = tmp(c, "Y0")
            P.op("pool", lambda h: h.tensor_tensor(Y0[:], Yt[:], DTs[:], ALU.mult), reads=[r_Yt, r_DTs], writes=[r_Y0]); yield
            AiT, r_AiT = tmp(c, "AiT")
            P.op("dve", lambda h: h.tensor_tensor(AiT[:], DTi[:], QK, ALU.mult), reads=[r_DTi], writes=[r_AiT, rQK]); yield
            qd, r_qd = tmp(c, "qd")
            P.op("pool", lambda h: h.tensor_tensor(qd[:], qT[:, ts], Er[:], ALU.mult), reads=[r_qT, r_Er], writes=[r_qd]); yield
            Xp_ps, rXp_ps = qps(c)
            P.op("pe", lambda h: h.transpose(Xp_ps, Y0[:], ident), reads=[r_Y0, rc], writes=[rXp_ps]); yield
            Xp, r_Xp = tmp(c, "X", 2)
            P.op("act", lambda h: h.activation(Xp[:], Xp_ps, AF.Copy), reads=[], writes=[r_Xp, rXp_ps]); yield
            X0, r_X0 = Xp, r_Xp
            bd16 = c128[:, C_BD16:C_BD16 + 128]
            Yd, r_Yd = tmp(c, "Yd", 1)
            P.op("pool", lambda h: h.tensor_tensor(Yd[:], Y0[:], bd16, ALU.mult), reads=[r_Y0, rc], writes=[r_Yd]); yield
            Xd, r_Xd = tmp(c, "Xd", 1)
            P.op("pool", lambda h: h.tensor_tensor(Xd[:], X0[:], bd16, ALU.mult), reads=[r_X0, rc], writes=[r_Xd]); yield
            R, r_R = tmp(c, "R", 3)
            P.op("pool", lambda h: h.tensor_tensor(R[:], Yd[:], ident, ALU.add), reads=[r_Yd, rc], writes=[r_R]); yield
            Yp, r_Yp, Xp, r_Xp = Yd, r_Yd, Xd, r_Xd
            for k in range(1, 4):
                X2, rX2 = qps(c)
                P.op("pe", lambda h: h.matmul(X2, Yp[:], Xp[:], start=True, stop=True), reads=[r_Yp, r_Xp], writes=[rX2]); yield
                Xn, r_Xn = tmp(c, "Xn", 2)
                P.op("act", lambda h: h.activation(Xn[:], X2, AF.Copy), reads=[], writes=[r_Xn, rX2]); yield
                if k <= 2:
                    Y2, rY2 = qps(c)
                    P.op("pe", lambda h: h.matmul(Y2, Xp[:], Yp[:], start=True, stop=True), reads=[r_Yp, r_Xp], writes=[rY2]); yield
                    Yn, r_Yn = tmp(c, "Yn", 2)
                    P.op("dve", lambda h: h.tensor_copy(Yn[:], Y2), reads=[], writes=[r_Yn, rY2]); yield
                pr, rpr = qps(c)
                P.op("pe", lambda h: h.matmul(pr, Xn[:], R[:], start=True, stop=True), reads=[r_Xn, r_R], writes=[rpr]); yield
                P.op("dve", lambda h: h.tensor_tensor(R[:], R[:], pr, ALU.add), reads=[], writes=[r_R, rpr]); yield
                Xp, r_Xp = Xn, r_Xn
                if k <= 2:
                    Yp, r_Yp = Yn, r_Yn
            it_ps, rit_ps = qps(c)
            P.op("pe", lambda h: h.transpose(it_ps, R[:], ident), reads=[r_R, rc], writes=[rit_ps]); yield
            Inv, r_Inv = tmp(c, "Inv", 3)
            P.op("act", lambda h: h.activation(Inv[:], it_ps, AF.Copy), reads=[], writes=[r_Inv, rit_ps]); yield
            for lvl, mko in enumerate((C_O32, C_O64, C_O128)):
                mk = c128[:, mko:mko + 128]
                Xb, r_Xb = tmp(c, "Xb", 1)
                P.op("pool", lambda h: h.tensor_tensor(Xb[:], X0[:], mk, ALU.mult), reads=[r_X0, rc], writes=[r_Xb]); yield
                Q, rQ = qps(c)
                P.op("pe", lambda h: h.matmul(Q, Xb[:], R[:], start=True, stop=True), reads=[r_Xb, r_R], writes=[rQ]); yield
                Qs, r_Qs = tmp(c, "Qs", 1)
                P.op("dve", lambda h: h.tensor_copy(Qs[:], Q), reads=[], writes=[r_Qs, rQ]); yield
                if lvl < 2:
                    Yb, r_Yb = tmp(c, "Yb", 1)
                    P.op("pool", lambda h: h.tensor_tensor(Yb[:], Y0[:], mk, ALU.mult), reads=[r_Y0, rc], writes=[r_Yb]); yield
                    Q2, rQ2 = qps(c)
                    P.op("pe", lambda h: h.matmul(Q2, Yb[:], Inv[:], start=True, stop=True), reads=[r_Yb, r_Inv], writes=[rQ2]); yield
                    Q2s, r_Q2s = tmp(c, "Q2s", 1)
                    P.op("act", lambda h: h.activation(Q2s[:], Q2, AF.Copy), reads=[], writes=[r_Q2s, rQ2]); yield
                P1, rP1 = qps(c)
                P.op("pe", lambda h: h.matmul(P1, Inv[:], Qs[:], start=True, stop=True), reads=[r_Inv, r_Qs], writes=[rP1]); yield
                Rn, r_Rn = tmp(c, "R", 3)
                P.op("dve", lambda h: h.tensor_tensor(Rn[:], R[:], P1, ALU.add), reads=[r_R], writes=[r_Rn, rP1]); yield
                if lvl < 2:
                    P2, rP2 = qps(c)
                    P.op("pe", lambda h: h.matmul(P2, R[:], Q2s[:], start=True, stop=True), reads=[r_R, r_Q2s], writes=[rP2]); yield
                    Invn, r_Invn = tmp(c, "Inv", 3)
                    P.op("dve", lambda h: h.tensor_tensor(Invn[:], Inv[:], P2, ALU.add), reads=[r_Inv], writes=[r_Invn, rP2]); yield
                    Inv, r_Inv = Invn, r_Invn
                R, r_R = Rn, r_Rn
            vb, r_vb = tmp(c, "vb")
            P.op("pool", lambda h: h.tensor_single_scalar(vb[:], vtm[:, t, :], col("beta"), ALU.mult), reads=[r_vtm], writes=[r_vb]); yield
            kbg, r_kbg = tmp(c, "kbg")
            P.op("pool", lambda h: h.tensor_single_scalar(kbg[:], ktm[:, t, :], col("bege"), ALU.mult), reads=[r_ktm], writes=[r_kbg]); yield
            ktl, r_ktl = tmp(c, "ktl")
            P.op("pool", lambda h: h.tensor_single_scalar(ktl[:], ktm[:, t, :], col("etl"), ALU.mult), reads=[r_ktm], writes=[r_ktl]); yield
            u_ps, ru_ps = qps(c)
            P.op("pe", lambda h: h.matmul(u_ps, R[:], vb[:], start=True, stop=True), reads=[r_R, r_vb], writes=[ru_ps]); yield
            w_ps, rw_ps = qps(c)
            P.op("pe", lambda h: h.matmul(w_ps, kbg[:], R[:], start=True, stop=True), reads=[r_R, r_kbg], writes=[rw_ps]); yield
            u, r_u = tmp(c, "u")
            P.op("act", lambda h: h.activation(u[:], u_ps, AF.Copy), reads=[], writes=[r_u, ru_ps]); yield
            wT, r_wT = tmp(c, "wT")
            P.op("dve", lambda h: h.tensor_copy(wT[:], w_ps), reads=[], writes=[r_wT, rw_ps]); yield
            p1, rp1 = qps(c)
            P.op("pe", lambda h: h.matmul(p1, wT[:], S[:], start=True, stop=True), reads=[r_wT, r_S], writes=[rp1]); yield
            vn, r_vn = tmp(c, "vn")
            P.op("dve", lambda h: h.tensor_tensor(vn[:], u[:], p1, ALU.subtract), reads=[r_u], writes=[r_vn, rp1]); yield
            o_ps, ro_ps = qps(c)
            P.op("pe", lambda h: h.matmul(o_ps, S[:], qd[:], start=True, stop=False), reads=[r_S, r_qd], writes=[ro_ps], inc=False); yield
            P.op("pe", lambda h: h.matmul(o_ps, vn[:], AiT[:], start=False, stop=True), reads=[r_vn, r_AiT], writes=[ro_ps]); yield
            p3, rp3 = qps(c)
            P.op("pe", lambda h: h.matmul(p3, ktl[:], vn[:], start=True, stop=True), reads=[r_ktl, r_vn], writes=[rp3]); yield
            if si % 4 == 0:
                ost = ostg()
                ost_t0 = t
            P.op("act", lambda h: h.activation(ost[0][:, si % 4, :], o_ps, AF.Copy), reads=[], writes=[ost[1], ro_ps]); yield
            P.op("dve", lambda h: h.scalar_tensor_tensor(S[:], S[:], col("egl"), p3, ALU.mult, ALU.add), reads=[r_S], writes=[r_S, rp3]); yield
            if si % 4 == 3 or si == len(order) - 1:
                nst = si % 4 + 1
                for j in range(nst):
                    tt = order[si - nst + 1 + j]
                    P.dma("sp", self.S_OT[c][hh * 128:(hh + 1) * 128, tt * 128:(tt + 1) * 128], ost[0][:, j, :], reads=[ost[1]])
                yield

    def phase_gdn_out(self, l):
        P = self.P
        with ExitStack() as st:
            op0 = self.pool(st, 2, [128, 8, 512], F32, "o0")
            op1 = self.pool(st, 2, [128, 8, 512], F32, "o1")
            zp = self.pool(st, 2, [128, 8, 512], BF16, "z")
            yp = self.pool(st, 2, [128, 8, 512], BF16, "yb")
            sqp = self.pool(st, 2, [128, 512], BF16, "sq")
            rsp = self.pool(st, 2, [128, 512], F32, "rstd")
            for (t0, n) in TB:
                o0, r0 = op0()
                o1, r1 = op1()
                z, rz = zp()
                y, ry = yp()
                P.dma("sp", o0[:, :, :n], self.S_OT[0][:, t0:t0 + n].rearrange("(h p) n -> p h n", p=128), writes=[r0])
                P.dma("sp", o1[:, :, :n], self.S_OT[1][:, t0:t0 + n].rearrange("(h p) n -> p h n", p=128), writes=[r1])
                P.dma("sp", z[:, :, :n], self.S_Z[:, t0:t0 + n].rearrange("(h p) n -> p h n", p=128), writes=[rz])
                P.op("pool", lambda h: h.tensor_tensor(o0[:, :, :n], o0[:, :, :n], o1[:, :, :n], ALU.add), reads=[r0, r1], writes=[r0])
                for hh in range(8):
                    rstd, r_rs = rsp()
                    self.fm_rstd([(o0[:, hh, :n], r0)], n, 1.0 / 128, 1, sqp, rstd[:, :n], r_rs)
                    P.op("dve", lambda h: h.tensor_tensor(o0[:, hh, :n], o0[:, hh, :n], rstd[:, :n], ALU.mult), reads=[r0, r_rs], writes=[r0])
                    P.op("dve", lambda h: h.tensor_tensor(o0[:, hh, :n], o0[:, hh, :n], z[:, hh, :n], ALU.mult), reads=[r0, rz], writes=[r0])
                    P.op("act", lambda h: h.activation(y[:, hh, :n], o0[:, hh, :n], AF.Identity, scale=self.vec[:, V_ONORM:V_ONORM + 1]),
                         reads=[r0, self.r_vec], writes=[ry])
                P.dma("sp", self.S_YB[:, t0:t0 + n].rearrange("(h p) n -> p h n", p=128), y[:, :, :n], reads=[ry])
            P.barrier()

    def phase_merge1(self, l):
        P = self.P
        with ExitStack() as st:
            yps = [self.pool(st, 2, [128, 8, 512], BF16, f"y{i}") for i in range(3)]
            wp = self.pool(st, 2, [128, 24, 512], BF16, "wbr")
            gp = self.pool(st, 2, [128, 3, 4, 512], BF16, "gt")
            accp = self.pool(st, 3, [128, 512], F32, "acc")
            tp = self.pool(st, 3, [128, 512], F32, "tt")
            bfp = self.pool(st, 3, [128, 512], BF16, "stb")
            srcs = (self.S_YA, self.S_YB, self.S_YC)
            for (t0, n) in TB:
                ys = []
                for i in range(3):
                    y, ry = yps[i]()
                    P.dma("sp", y[:, :, :n], srcs[i][:, t0:t0 + n].rearrange("(k p) n -> p k n", p=128), writes=[ry])
                    ys.append((y, ry))
                for cg in range(4):
                    w, rw = wp()
                    self.wload("sp", w[:], "wbr", l, list(range(24)), cg * 512, (cg + 1) * 512, rw)
                    gt, rg = gp()
                    for i in range(3):
                        P.dma("sp", gt[:, i, :, :n], self.S_GT[i * 2048 + cg * 512:i * 2048 + (cg + 1) * 512, t0:t0 + n].rearrange("(j p) n -> p j n", p=128), writes=[rg])
                    for jb in range(4):
                        cb = cg * 4 + jb
                        pss = []
                        for i in range(3):
                            ps, rps = self.bank()
                            y, ry = ys[i]
                            for k in range(8):
                                P.op("pe", lambda h: h.matmul(ps[:, :n], w[:, i * 8 + k, jb * 128:(jb + 1) * 128], y[:, k, :n], start=(k == 0), stop=(k == 7)),
                                     reads=[rw, ry], writes=[rps], inc=(k == 7))
                            pss.append((ps, rps))
                        acc, racc = accp()
                        P.op("dve", lambda h: h.tensor_tensor(acc[:, :n], pss[0][0][:, :n], gt[:, 0, jb, :n], ALU.mult), reads=[pss[0][1], rg], writes=[racc])
                        t1, rt1 = tp()
                        P.op("dve", lambda h: h.tensor_tensor(t1[:, :n], pss[1][0][:, :n], gt[:, 1, jb, :n], ALU.mult), reads=[pss[1][1], rg], writes=[rt1])
                        P.op("pool", lambda h: h.tensor_tensor(acc[:, :n], acc[:, :n], t1[:, :n], ALU.add), reads=[rt1], writes=[racc])
                        t2, rt2 = tp()
                        P.op("dve", lambda h: h.tensor_tensor(t2[:, :n], pss[2][0][:, :n], gt[:, 2, jb, :n], ALU.mult), reads=[pss[2][1], rg], writes=[rt2])
                        o, ro = bfp()
                        P.op("pool", lambda h: h.tensor_tensor(o[:, :n], acc[:, :n], t2[:, :n], ALU.add), reads=[racc, rt2], writes=[ro])
                        P.dma("sp", self.S_YS[cb * 128:(cb + 1) * 128, t0:t0 + n], o[:, :n], reads=[ro])
            P.barrier()

    def phase_merge2(self, l, xsrc):
        P = self.P
        mv = self.mv
        with ExitStack() as st:
            ysp = self.pool(st, 1, [128, 16, 512], BF16, "ys")
            xp = self.pool(st, 1, [128, 16, 512], F32, "x")
            y2p = self.pool(st, 1, [128, 16, 512], F32, "y2")
            h2p = self.pool(st, 1, [128, 16, 512], BF16, "h2")
            wp = self.pool(st, 2, [128, 16, 512], BF16, "wout")
            sqp = self.pool(st, 2, [128, 512], BF16, "sq")
            rsp = self.pool(st, 2, [128, 512], F32, "rstd")
            for (t0, n) in TB:
                v = 1 if t0 < NCTX else 0
                ysb, rys = ysp()
                x, rx = xp()
                y2, ry2 = y2p()
                P.dma("sp", ysb[:, :, :n], self.S_YS[:, t0:t0 + n].rearrange("(k p) n -> p k n", p=128), writes=[rys])
                P.dma("sp", x[:, :, :n], xsrc[:, t0:t0 + n].rearrange("(k p) n -> p k n", p=128), writes=[rx])
                for cg in range(4):
                    w, rw = wp()
                    self.wload("sp", w[:], "wout", l, list(range(16)), cg * 512, (cg + 1) * 512, rw)
                    for jb in range(4):
                        cb = cg * 4 + jb
                        ps, rps = self.bank()
                        for k in range(16):
                            P.op("pe", lambda h: h.matmul(ps[:, :n], w[:, k, jb * 128:(jb + 1) * 128], ysb[:, k, :n], start=(k == 0), stop=(k == 15)),
                                 reads=[rw, rys], writes=[rps], inc=(k == 15))
                        if cb % 2 == 0:
                            P.op("act", lambda h: h.activation(y2[:, cb, :n], ps[:, :n], AF.Copy), reads=[rps], writes=[ry2])
                        else:
                            P.op("dve", lambda h: h.tensor_copy(y2[:, cb, :n], ps[:, :n]), reads=[rps], writes=[ry2])
                rstd, r_rs = rsp()
                self.fm_rstd([(y2[:, cb, :n], ry2) for cb in range(16)], n, 1.0 / D, 1, sqp, rstd[:, :n], r_rs)
                for cb in range(16):
                    P.op("dve", lambda h: h.tensor_tensor(y2[:, cb, :n], y2[:, cb, :n], rstd[:, :n], ALU.mult), reads=[ry2, r_rs], writes=[ry2])
                    P.op("dve", lambda h: h.scalar_tensor_tensor(x[:, cb, :n], y2[:, cb, :n], mv[:, 2, cb, v:v + 1], x[:, cb, :n], ALU.mult, ALU.add),
                         reads=[ry2, self.r_mod], writes=[rx])
                P.dma("sp", self.xout[:, t0:t0 + n].rearrange("(k p) n -> p k n", p=128), x[:, :, :n], reads=[rx])
                rstd2, r_rs2 = rsp()
                self.fm_rstd([(x[:, cb, :n], rx) for cb in range(16)], n, 1.0 / D, 1, sqp, rstd2[:, :n], r_rs2)
                h2, rh2 = h2p()
                for cb in range(16):
                    P.op("dve", lambda h: h.tensor_tensor(y2[:, cb, :n], x[:, cb, :n], rstd2[:, :n], ALU.mult), reads=[rx, r_rs2], writes=[ry2])
                    P.op("act", lambda h: h.activation(h2[:, cb, :n], y2[:, cb, :n], AF.Identity, bias=mv[:, 4, cb, v:v + 1], scale=mv[:, 3, cb, v:v + 1]),
                         reads=[ry2, self.r_mod], writes=[rh2])
                P.dma("sp", self.S_H2[:, t0:t0 + n].rearrange("(k p) n -> p k n", p=128), h2[:, :, :n], reads=[rh2])
            P.barrier()

    def phase_ffn1(self, l):
        P = self.P
        for blocks in SCS:
            sc0 = blocks[0][0]
            ntok = sum(b[1] for b in blocks)
            with ExitStack() as st:
                H, r_H = self.sb(st, [128, 16, 2304], BF16, "H2")
                for (t0, n) in blocks:
                    P.dma("sp", H[:, :, t0 - sc0:t0 - sc0 + n], self.S_H2[:, t0:t0 + n].rearrange("(k p) n -> p k n", p=128), writes=[r_H])
                wgp = self.pool(st, 2, [128, 16, 512], BF16, "wg")
                wup = self.pool(st, 2, [128, 16, 512], BF16, "wu")
                sgp = self.pool(st, 3, [128, 512], F32, "sg")
                bfp = self.pool(st, 3, [128, 512], BF16, "stb")
                for hg in range(11):
                    wg, rwg = wgp()
                    wu, rwu = wup()
                    self.wload("sp", wg[:], "wg", l, list(range(16)), hg * 512, (hg + 1) * 512, rwg)
                    self.wload("sp", wu[:], "wu", l, list(range(16)), hg * 512, (hg + 1) * 512, rwu)
                    for jb in range(4):
                        j = hg * 4 + jb
                        for (t0, n) in blocks:
                            off = t0 - sc0
                            pg, rpg = self.bank()
                            for k in range(16):
                                P.op("pe", lambda h: h.matmul(pg[:, :n], wg[:, k, jb * 128:(jb + 1) * 128], H[:, k, off:off + n], start=(k == 0), stop=(k == 15)),
                                     reads=[rwg, r_H], writes=[rpg], inc=(k == 15))
                            pu, rpu = self.bank()
                            for k in range(16):
                                P.op("pe", lambda h: h.matmul(pu[:, :n], wu[:, k, jb * 128:(jb + 1) * 128], H[:, k, off:off + n], start=(k == 0), stop=(k == 15)),
                                     reads=[rwu, r_H], writes=[rpu], inc=(k == 15))
                            sg, rsg = sgp()
                            P.op("act", lambda h: h.activation(sg[:, :n], pg[:, :n], AF.Silu), reads=[rpg], writes=[rsg])
                            o, ro = bfp()
                            P.op("dve", lambda h: h.tensor_tensor(o[:, :n], sg[:, :n], pu[:, :n], ALU.mult), reads=[rsg, rpu], writes=[ro])
                            P.dma("sp", self.S_ACT[j * 128:(j + 1) * 128, t0:t0 + n], o[:, :n], reads=[ro])
                P.barrier()

    def phase_ffn2(self, l):
        P = self.P
        with ExitStack() as st:
            ap_ = self.pool(st, 1, [128, FKC, 512], BF16, "act")
            wp = self.pool(st, 2, [128, FKC, 512], BF16, "wd")
            fp_ = self.pool(st, 4, [128, 512], F32, "stf")
            ev = 0
            for (t0, n) in TB:
                a, ra = ap_()
                P.dma("sp", a[:, 0:22, :n], self.S_ACT[0:22 * 128, t0:t0 + n].rearrange("(k p) n -> p k n", p=128), writes=[ra])
                P.dma("sp", a[:, 22:44, :n], self.S_ACT[22 * 128:44 * 128, t0:t0 + n].rearrange("(k p) n -> p k n", p=128), writes=[ra])
                for cg in range(4):
                    w, rw = wp()
                    self.wload("sp", w[:, 0:22, :], "wd", l, list(range(22)), cg * 512, (cg + 1) * 512, rw)
                    self.wload("sp", w[:, 22:44, :], "wd", l, list(range(22, 44)), cg * 512, (cg + 1) * 512, rw)
                    for jb in range(4):
                        cb = cg * 4 + jb
                        ps, rps = self.bank()
                        for k in range(FKC):
                            P.op("pe", lambda h: h.matmul(ps[:, :n], w[:, k, jb * 128:(jb + 1) * 128], a[:, k, :n], start=(k == 0), stop=(k == FKC - 1)),
                                 reads=[rw, ra], writes=[rps], inc=(k == FKC - 1))
                        o, ro = fp_()
                        ev += 1
                        if ev % 2 == 0:
                            P.op("act", lambda h: h.activation(o[:, :n], ps[:, :n], AF.Copy), reads=[rps], writes=[ro])
                        else:
                            P.op("dve", lambda h: h.tensor_copy(o[:, :n], ps[:, :n]), reads=[rps], writes=[ro])
                        P.dma("sp", self.S_F[cb * 128:(cb + 1) * 128, t0:t0 + n], o[:, :n], reads=[ro])
            P.barrier()

    def phase_ffn3(self, l):
        P = self.P
        mv = self.mv
        with ExitStack() as st:
            fp_ = self.pool(st, 2, [128, 16, 512], F32, "f")
            xp = self.pool(st, 2, [128, 16, 512], F32, "x")
            sqp = self.pool(st, 2, [128, 512], BF16, "sq")
            rsp = self.pool(st, 2, [128, 512], F32, "rstd")
            for (t0, n) in TB:
                v = 1 if t0 < NCTX else 0
                f, rf = fp_()
                x, rx = xp()
                P.dma("sp", f[:, :, :n], self.S_F[:, t0:t0 + n].rearrange("(k p) n -> p k n", p=128), writes=[rf])
                P.dma("sp", x[:, :, :n], self.xout[:, t0:t0 + n].rearrange("(k p) n -> p k n", p=128), writes=[rx])
                rstd, r_rs = rsp()
                self.fm_rstd([(f[:, cb, :n], rf) for cb in range(16)], n, 1.0 / D, 1, sqp, rstd[:, :n], r_rs)
                for cb in range(16):
                    P.op("dve", lambda h: h.tensor_tensor(f[:, cb, :n], f[:, cb, :n], rstd[:, :n], ALU.mult), reads=[rf, r_rs], writes=[rf])
                    P.op("dve", lambda h: h.scalar_tensor_tensor(x[:, cb, :n], f[:, cb, :n], mv[:, 5, cb, v:v + 1], x[:, cb, :n], ALU.mult, ALU.add),
                         reads=[rf, self.r_mod], writes=[rx])
                P.dma("sp", self.xout[:, t0:t0 + n].rearrange("(k p) n -> p k n", p=128), x[:, :, :n], reads=[rx])
            P.barrier()

    def build(self):
        P = self.P
        first = True
        for li, l in enumerate(self.layers):
            if li == 0:
                self.issue_casts(l)
            if li + 1 < len(self.layers):
                self.issue_casts(self.layers[li + 1])
            xsrc = self.xin if first else self.xout
            first = False
            self.phase_mod(l)
            if self.stop_after == "mod":
                break
            self.phase_p1(l, xsrc)
            self.phase_mla_up(l)
            if self.stop_after == "p1":
                break
            if "attn" not in self.skip:
                self.phase_attn(l)
            if self.stop_after == "attn":
                break
            if "gdn" not in self.skip:
                self.phase_gdn(l)
                import os as _os2
                if not _os2.environ.get('GDN_NOOUT'):
                    self.phase_gdn_out(l)
            if self.stop_after == "gdn":
                break
            self.phase_merge1(l)
            self.phase_merge2(l, xsrc)
            if self.stop_after == "merge":
                break
            self.phase_ffn1(l)
            self.phase_ffn2(l)
            self.phase_ffn3(l)
        P.barrier()
        return self.nc


_CONSTS = None


def _layer_inputs(inp, l, tag):
    d = {}
    d[f"w_ada{tag}"] = np.ascontiguousarray(inp["w_ada"][l])
    d[f"w_in{tag}"] = np.ascontiguousarray(inp["w_in"][l][:, _win_cols()])
    d[f"mqup{tag}"] = np.ascontiguousarray(inp["mla_q_up"][l][:, _mqup_cols()])
    d[f"mkvup{tag}"] = np.ascontiguousarray(inp["mla_kv_up"][l][:, _mkvup_cols()])
    d[f"wbr{tag}"] = np.ascontiguousarray(inp["w_branch"][l].reshape(3072, D))
    d[f"wout{tag}"] = np.ascontiguousarray(inp["w_out"][l])
    d[f"wg{tag}"] = np.ascontiguousarray(inp["w_ffn_gate"][l])
    d[f"wu{tag}"] = np.ascontiguousarray(inp["w_ffn_up"][l])
    d[f"wd{tag}"] = np.ascontiguousarray(inp["w_ffn_down"][l])
    d[f"vec{tag}"] = _pack_vec(inp, l)
    return d


def _core_inputs(inp, b):
    global _CONSTS
    if _CONSTS is None:
        _CONSTS = _consts()
    c128, sel, ctab, stab = _CONSTS
    cT = np.stack([inp["c"][b].reshape(16, 128).T, inp["c_ctx"].reshape(16, 128).T], axis=-1).reshape(128, 32)
    return {"cT": np.ascontiguousarray(cT, dtype=np.float32), "c128": c128, "ctab": ctab, "stab": stab}


FUSED = False
_PROG_CACHE = {}


def _get_prog(layers):
    key = tuple(layers)
    if key not in _PROG_CACHE:
        kb = KB(list(layers))
        _PROG_CACHE[key] = kb.build()
    return _PROG_CACHE[key]


def kernel(**inputs):
    inp = {k: np.asarray(v) for k, v in inputs.items()}
    B = inp["x"].shape[0]
    cores = list(range(B))
    xT = [np.ascontiguousarray(np.concatenate([inp["ctx"][b], inp["x"][b]], axis=0).T.astype(np.float32)) for b in range(B)]
    base = [_core_inputs(inp, b) for b in range(B)]
    if FUSED:
        nc = _get_prog(range(DEPTH))
        wl = {}
        for l in range(DEPTH):
            wl.update(_layer_inputs(inp, l, str(l)))
        in_maps = []
        for b in range(B):
            m = dict(base[b])
            m.update(wl)
            m["xin"] = xT[b]
            in_maps.append(m)
        res = run_bass_kernel_spmd(nc, in_maps, core_ids=cores)
        xT = [np.asarray(res.results[b]["xout"]) for b in range(B)]
    else:
        nc = _get_prog([0])
        for l in range(DEPTH):
            wl = _layer_inputs(inp, l, "0")
            in_maps = []
            for b in range(B):
                m = dict(base[b])
                m.update(wl)
                m["xin"] = xT[b]
                in_maps.append(m)
            res = run_bass_kernel_spmd(nc, in_maps, core_ids=cores)
            xT = [np.ascontiguousarray(np.asarray(res.results[b]["xout"])) for b in range(B)]
    out = np.stack([xT[b][:, NCTX:].T for b in range(B)], axis=0)
    return np.ascontiguousarray(out.astype(np.float32))
```

```python
import math
from contextlib import ExitStack

import numpy as np
import ml_dtypes

import concourse.bass as bass
import concourse.mybir as mybir
from concourse.bass_utils import run_bass_kernel_spmd

F32 = mybir.dt.float32
BF16 = mybir.dt.bfloat16
AF = mybir.ActivationFunctionType
ALU = mybir.AluOpType

D = 2048
KC = 16
T = 4352
NCTX = 256
FF = 5632
FKC = 44
DEPTH = 4
NT = 34
TB = [(0, 256)] + [(256 + 512 * i, 512) for i in range(8)]
SCS = [TB[0:5], TB[5:9]]
WIN_EXT = 16544
EPOCH = 30000


class Res:
    __slots__ = ("name", "lw", "rd")

    def __init__(self, name=""):
        self.name = name
        self.lw = None
        self.rd = {}


class Prog:
    ENGS = ("pe", "act", "dve", "pool", "sp")

    def __init__(self, nc, n_dma_sems=16, n_cast_sems=4):
        self.nc = nc
        self.h = {"pe": nc.tensor, "act": nc.scalar, "dve": nc.vector, "pool": nc.gpsimd, "sp": nc.sync}
        self.sems = {}
        self.cur = {}
        self.prev_final = {}
        self.epoch = {e: 0 for e in self.ENGS}
        self.waited = {e: {} for e in self.ENGS}
        for e in ("pe", "act", "dve", "pool"):
            self._new_epoch(e)
        self.dma_sems = []
        for i in range(n_dma_sems):
            k = ("dma", i)
            self.sems[k] = nc.alloc_semaphore(name=f"dma{i}")
            self.dma_sems.append([k, 0])
        self.cast_sems = []
        for i in range(n_cast_sems):
            k = ("cast", i)
            self.sems[k] = nc.alloc_semaphore(name=f"cast{i}")
            self.cast_sems.append([k, 0])
        self.dma_rr = 0
        self.cast_rr = 0
        self.n_instr = 0
        self.n_wait = 0
        self.pending = {e: ([], []) for e in self.ENGS}

    def _new_epoch(self, e):
        if e in self.cur:
            if not hasattr(self, "prev_final"):
                self.prev_final = {}
            self.prev_final[e] = (self.cur[e][0], self.cur[e][1])
        k = (e, self.epoch[e])
        self.epoch[e] += 1
        self.sems[k] = self.nc.alloc_semaphore(name=f"s_{e}_{k[1]}")
        self.cur[e] = [k, 0]

    def _need(self, eng, tok, waits):
        if tok is None:
            return
        k, v = tok
        if eng == "pe" and k[0] == "pe":
            return
        if self.waited[eng].get(k, 0) >= v:
            return
        if waits.get(k, 0) < v:
            waits[k] = v

    def _deps(self, eng, reads, writes):
        waits = {}
        for r in reads:
            self._need(eng, r.lw, waits)
        for w in writes:
            self._need(eng, w.lw, waits)
            for k, v in w.rd.items():
                self._need(eng, (k, v), waits)
        return waits

    def _emit_waits(self, eng, waits):
        for k, v in waits.items():
            if getattr(self, "trace", False):
                print("   WAIT", eng, k, v)
            self.h[eng].wait_ge(self.sems[k], v)
            self.waited[eng][k] = v
            self.n_wait += 1

    def _mark(self, tok, reads, writes):
        k, v = tok
        for r in reads:
            if r.rd.get(k, 0) < v:
                r.rd[k] = v
        for w in writes:
            w.lw = tok
            w.rd = {}

    def op(self, eng, fn, reads=(), writes=(), inc=True):
        waits = self._deps(eng, reads, writes)
        self._emit_waits(eng, waits)
        if not inc:
            fn(self.h[eng])
            self.pending[eng][0].extend(reads)
            self.pending[eng][1].extend(writes)
            self.n_instr += 1
            return None
        if self.pending[eng][0] or self.pending[eng][1]:
            reads = list(reads) + self.pending[eng][0]
            writes = list(writes) + self.pending[eng][1]
            self.pending[eng] = ([], [])
        cur = self.cur[eng]
        if cur[1] >= EPOCH:
            self._new_epoch(eng)
            cur = self.cur[eng]
        cur[1] += 1
        tok = (cur[0], cur[1])
        if getattr(self, "trace", False):
            print("OP", eng, tok)
        fn(self.h[eng]).then_inc(self.sems[cur[0]], 1)
        self._mark(tok, reads, writes)
        self.n_instr += 1
        return tok

    def dma(self, q, out, in_, reads=(), writes=(), cast=False, **kw):
        waits = self._deps(q, reads, writes)
        if cast:
            slot = self.cast_sems[self.cast_rr]
            self.cast_rr = (self.cast_rr + 1) % len(self.cast_sems)
        else:
            slot = self.dma_sems[self.dma_rr]
            self.dma_rr = (self.dma_rr + 1) % len(self.dma_sems)
        k = slot[0]
        if slot[1] > 0:
            self._need(q, (k, slot[1]), waits)
        self._emit_waits(q, waits)
        slot[1] += 16
        tok = (k, slot[1])
        self.h[q].dma_start(out=out, in_=in_, **kw).then_inc(self.sems[k], 16)
        self._mark(tok, reads, writes)
        self.n_instr += 1
        return tok

    def barrier(self):
        toks = [(c[0], c[1]) for c in self.cur.values() if c[1] > 0] + [(s[0], s[1]) for s in self.dma_sems if s[1] > 0]
        toks += list(self.prev_final.values())
        for e in self.ENGS:
            waits = {}
            for t in toks:
                self._need(e, t, waits)
            self._emit_waits(e, waits)


P64 = np.concatenate([np.arange(32, 64), np.arange(0, 32)])
P128 = np.concatenate([P64, 64 + P64])


def _win_cols():
    dq, dk, dv, gq, gz, ga, mq, mkv, kr, gates = 0, 1024, 2048, 3072, 6144, 7168, 7200, 7712, 8224, 8288
    cols = []
    for base in (dq, dk):
        for h in range(8):
            cols.append(base + h * 128 + np.arange(128))
            cols.append(base + h * 128 + P128)
    cols.append(gq + np.arange(3072))
    cols.append(gz + np.arange(1024))
    cols.append(mq + np.arange(512))
    cols.append(mkv + np.arange(512))
    cols.append(gates + np.arange(6144))
    cols.append(kr + np.arange(64))
    cols.append(kr + P64)
    cols.append(dv + np.arange(1024))
    cols.append(ga + np.arange(32))
    c = np.concatenate(cols)
    assert c.shape[0] == WIN_EXT
    return c


def _mqup_cols():
    cols = []
    for h in range(8):
        cols.append(h * 192 + np.arange(128))
    for j in range(4):
        a = np.concatenate([(2 * j) * 192 + 128 + np.arange(64), (2 * j + 1) * 192 + 128 + np.arange(64)])
        b = np.concatenate([(2 * j) * 192 + 128 + P64, (2 * j + 1) * 192 + 128 + P64])
        cols.append(a)
        cols.append(b)
    return np.concatenate(cols)


def _mkvup_cols():
    cols = []
    for h in range(8):
        cols.append(h * 256 + np.arange(128))
    for h in range(8):
        cols.append(h * 256 + 128 + np.arange(128))
    return np.concatenate(cols)


def _fm(v, kc):
    return np.ascontiguousarray(v.reshape(kc, 128).T)


NV = 96 + 64 + 256 + 1 + 1 + 4 + 4 + 72 + 544 + 544 + 2
V_LAMI = 1586
V_BADA, V_G, V_LAM, V_SUBLN, V_ONORM, V_MQN, V_MKVN, V_CONV, V_ALOG, V_DTB = 0, 96, 160, 416, 417, 418, 422, 426, 498, 1042


def _pack_vec(inp, l):
    v = np.zeros((128, NV), np.float32)
    v[:, V_BADA:V_BADA + 96] = _fm(inp["b_ada"][l], 96)
    for i, nm in enumerate(("g_pre_mix", "g_post_mix", "g_pre_ffn", "g_post_ffn")):
        v[:, V_G + 16 * i:V_G + 16 * (i + 1)] = _fm(inp[nm][l], 16)
    lam = np.concatenate([inp["diff_lam_q1"][l], inp["diff_lam_k1"][l], inp["diff_lam_q2"][l], inp["diff_lam_k2"][l]])
    v[:, V_LAM:V_LAM + 256] = np.broadcast_to(lam[None, :], (128, 256))
    v[:, V_SUBLN] = inp["diff_subln"][l]
    v[:, V_ONORM] = inp["gdn_out_norm"][l]
    v[:, V_MQN:V_MQN + 4] = _fm(inp["mla_q_norm"][l], 4)
    v[:, V_MKVN:V_MKVN + 4] = _fm(inp["mla_kv_norm"][l], 4)
    cw = inp["gdn_conv"][l]
    v[:, V_CONV:V_CONV + 72] = np.ascontiguousarray(cw.reshape(3, 24, 128).transpose(2, 1, 0)).reshape(128, 72)
    al = inp["gdn_a_log"][l].reshape(16)
    db = inp["gdn_dt_bias"][l].reshape(16)
    v[:, V_ALOG:V_ALOG + 544] = np.broadcast_to(np.tile(al, NT)[None, :], (128, 544))
    v[:, V_DTB:V_DTB + 544] = np.broadcast_to(np.tile(db, NT)[None, :], (128, 544))
    lam_init = 0.8 - 0.6 * math.exp(-0.3 * l)
    v[:, V_LAMI] = lam_init
    v[:, V_LAMI + 1] = 1.0 - lam_init
    return v


def _consts():
    i = np.arange(128)
    ident = np.eye(128, dtype=np.float32)
    tri_f = (i[:, None] <= i[None, :]).astype(np.float32)
    tri_b = (i[:, None] >= i[None, :]).astype(np.float32)
    msf = -(i[None, :] > i[:, None]).astype(np.float32)
    mif = (i[None, :] >= i[:, None]).astype(np.float32)
    msb = -(i[None, :] < i[:, None]).astype(np.float32)
    mib = (i[None, :] <= i[:, None]).astype(np.float32)
    def bd(sz):
        return ((i[:, None] // sz) == (i[None, :] // sz)).astype(np.float32)
    bd16, o32, o64, o128 = bd(16), bd(32) - bd(16), bd(64) - bd(32), 1.0 - bd(64)
    c128 = np.concatenate([ident, tri_f, tri_b, msf, mif, msb, mib, np.ones((128, 128), np.float32), bd16, o32, o64, o128], axis=1)
    sel = np.zeros((128, 32, 128), np.float32)
    for r in range(32):
        sel[r, r, :] = 1.0
    rows = 4096 // 64
    row_id = np.repeat(np.arange(rows), 64).astype(np.float32)
    col_id = np.tile(np.arange(64), rows).astype(np.float32)
    inv_freq = (np.float32(10000.0) ** (-np.arange(16, dtype=np.float32) / np.float32(16))).astype(np.float32)
    ang = np.concatenate([row_id[:, None] * inv_freq, col_id[:, None] * inv_freq], axis=-1).astype(np.float32)
    cos = np.cos(ang).astype(np.float32)
    sin = np.sin(ang).astype(np.float32)
    ctab = np.ones((128, T), np.float32)
    stab = np.zeros((128, T), np.float32)
    for r in range(128):
        ctab[r, NCTX:] = cos[:, r % 32]
        sgn = -1.0 if (r % 64) < 32 else 1.0
        stab[r, NCTX:] = sgn * sin[:, r % 32]
    return c128, sel.reshape(128, 32 * 128), ctab, stab


C_ID, C_TRIF, C_TRIB, C_MSF, C_MIF, C_MSB, C_MIB, C_ONE, C_BD16, C_O32, C_O64, C_O128 = [128 * i for i in range(12)]


class KB:
    def __init__(self, layers, stop_after=None, dbg=(), skip=()):
        self.skip = set(skip)
        self.layers = list(layers)
        self.stop_after = stop_after
        self.dbg = set(dbg)
        nc = self.nc = bass.Bass("TRN2", target_bir_lowering=False)
        self.P = Prog(nc)
        self._uid = 0
        self.xin = self.dram("xin", [D, T], F32, "ExternalInput")
        self.xout = self.dram("xout", [D, T], F32, "ExternalOutput")
        self.cT_d = self.dram("cT", [128, 32], F32, "ExternalInput")
        self.c128_d = self.dram("c128", [128, 1536], F32, "ExternalInput")
        self.ctab_d = self.dram("ctab", [128, T], F32, "ExternalInput")
        self.stab_d = self.dram("stab", [128, T], F32, "ExternalInput")
        self.W = {}
        wshapes = {"w_ada": [D, 6 * D], "w_in": [D, WIN_EXT], "mqup": [512, 2048], "mkvup": [512, 2048],
                   "wbr": [3072, D], "wout": [D, D], "wg": [D, FF], "wu": [D, FF], "wd": [FF, D]}
        for l in self.layers:
            for nm, shp in wshapes.items():
                self.W[(nm, l)] = self.dram(f"{nm}{l}", shp, F32, "ExternalInput")
            self.W[("vec", l)] = self.dram(f"vec{l}", [128, NV], F32, "ExternalInput")
        self.WB = {}
        self.rWB = {}
        for l in self.layers:
            for nm, shp in wshapes.items():
                if nm == "w_ada":
                    continue
                self.WB[(nm, l)] = self.dram(f"{nm}b{l}", shp, BF16)
                self.rWB[(nm, l)] = [Res() for _ in range(shp[0] // 128)]
        sd = self.sdram
        self.S_QD = sd("S_QD", [1024, T], BF16)
        self.S_KD = sd("S_KD", [1024, T], BF16)
        self.S_VD = sd("S_VD", [T, 1024], BF16)
        self.S_G = sd("S_G", [3072, T], F32)
        self.S_Z = sd("S_Z", [1024, T], BF16)
        self.S_AB = sd("S_AB", [T, 32], F32)
        self.S_MQ = sd("S_MQ", [512, T], BF16)
        self.S_MKV = sd("S_MKV", [512, T], BF16)
        self.S_KR = sd("S_KR", [64, T], BF16)
        self.S_GT = sd("S_GT", [6144, T], BF16)
        self.S_MQN = sd("S_MQN", [1024, T], BF16)
        self.S_MQR = sd("S_MQR", [512, T], BF16)
        self.S_MKN = sd("S_MKN", [1024, T], BF16)
        self.S_VM = sd("S_VM", [T, 1024], BF16)
        self.S_YA = sd("S_YA", [1024, T], BF16)
        self.S_YB = sd("S_YB", [1024, T], BF16)
        self.S_YC = sd("S_YC", [1024, T], BF16)
        self.S_OT = [sd("S_OT0", [1024, T], F32), sd("S_OT1", [1024, T], F32)]
        self.S_YS = sd("S_YS", [D, T], BF16)
        self.S_H2 = sd("S_H2", [D, T], BF16)
        self.S_ACT = sd("S_ACT", [FF, T], BF16)
        self.S_F = sd("S_F", [D, T], F32)
        self.ps = []
        self.rps = []
        for i in range(8):
            t = nc.alloc_psum_tensor(f"psb{i}", [128, 512], F32)
            self.ps.append(t)
            self.rps.append(Res(f"ps{i}"))
        self.bank_rr = 0
        self.c128 = nc.alloc_sbuf_tensor("c128s", [128, 1536], F32)
        self.ones_bf = nc.alloc_sbuf_tensor("ones_bf", [128, 128], BF16)
        self.cols = nc.alloc_sbuf_tensor("ccols", [128, 8], F32)
        self.r_const = Res("const")
        self.vec = nc.alloc_sbuf_tensor("vecs", [128, NV], F32)
        self.r_vec = Res("vec")
        self.mod = nc.alloc_sbuf_tensor("mods", [128, 96, 2], F32)
        self.r_mod = Res("mod")
        self.mv = nc.alloc_sbuf_tensor("mvs", [128, 6, 16, 2], F32)
        self.lamc = nc.alloc_sbuf_tensor("lamc", [128, 4], F32)
        P = self.P
        P.dma("sp", self.c128[:], self.c128_d, writes=[self.r_const])
        P.op("dve", lambda h: h.memset(self.ones_bf[:], 1.0), writes=[self.r_const])
        for i, val in enumerate((1.0, 1e-6, 1e-5, 0.0)):
            P.op("dve", lambda h: h.memset(self.cols[:, i:i + 1], val), writes=[self.r_const])

    def dram(self, name, shape, dt, kind="Internal"):
        return self.nc.dram_tensor(name, shape, dt, kind=kind).ap()

    def sdram(self, name, shape, dt):
        kind = "ExternalOutput" if name in self.dbg else "Internal"
        return self.nc.dram_tensor(name, shape, dt, kind=kind).ap()

    def sb(self, st, shape, dt, name=None):
        self._uid += 1
        t = st.enter_context(self.nc.sbuf_tensor(f"{name or 't'}_{self._uid}", shape, dt))
        return t, Res(name or "t")

    def pool(self, st, n, shape, dt, name=None):
        items = [self.sb(st, shape, dt, name) for _ in range(n)]
        state = {"i": 0}

        def nxt():
            it = items[state["i"] % n]
            state["i"] += 1
            return it
        return nxt

    def bank(self):
        i = self.bank_rr
        self.bank_rr = (self.bank_rr + 1) % 8
        return self.ps[i], self.rps[i]

    def ident(self):
        return self.c128[:, C_ID:C_ID + 128]

    def onesf(self):
        return self.c128[:, C_ONE:C_ONE + 128]

    def issue_casts(self, l):
        for nm in ("w_in", "mqup", "mkvup", "wbr", "wout", "wg", "wu", "wd"):
            src = self.W[(nm, l)]
            dst = self.WB[(nm, l)]
            rows = src.shape[0]
            for rb in range(rows // 128):
                self.P.dma("pool", dst[rb * 128:(rb + 1) * 128, :], src[rb * 128:(rb + 1) * 128, :],
                           writes=[self.rWB[(nm, l)][rb]], cast=True)

    def wload(self, q, dst_ap, nm, l, kcs, c0, c1, res):
        src = self.WB[(nm, l)]
        k0, k1 = kcs[0], kcs[-1] + 1
        self.P.dma(q, dst_ap, src[k0 * 128:k1 * 128, c0:c1].rearrange("(k p) n -> p k n", p=128),
                   reads=self.rWB[(nm, l)][k0:k1], writes=[res])

    def fm_rstd(self, chunks, n, scale, eps_idx, sqpool, out_ap, r_out, bank=None):
        P = self.P
        ps, rps = self.bank() if bank is None else (self.ps[bank], self.rps[bank])
        C = len(chunks)
        for c, (ap, r) in enumerate(chunks):
            sq, rsq = sqpool()
            P.op("act", lambda h: h.activation(sq[:, :n], ap, AF.Square), reads=[r], writes=[rsq])
            P.op("pe", lambda h: h.matmul(ps[:, :n], self.ones_bf[:], sq[:, :n], start=(c == 0), stop=(c == C - 1)),
                 reads=[rsq, self.r_const], writes=[rps], inc=(c == C - 1))
        P.op("act", lambda h: h.activation(out_ap, ps[:, :n], AF.Sqrt, bias=self.cols[:, eps_idx:eps_idx + 1], scale=scale),
             reads=[rps, self.r_const], writes=[r_out])
        P.op("dve", lambda h: h.reciprocal(out_ap, out_ap), reads=[r_out], writes=[r_out])

    def phase_mod(self, l):
        P = self.P
        with ExitStack() as st:
            sc, r_sc = self.sb(st, [128, 16, 2], F32, "sc")
            wp = self.pool(st, 2, [128, 16, 512], F32, "wada")
            P.dma("sp", self.vec[:], self.W[("vec", l)], writes=[self.r_vec])
            P.dma("sp", sc[:], self.cT_d.rearrange("p (k v) -> p k v", v=2), writes=[r_sc])
            P.op("act", lambda h: h.activation(sc[:], sc[:], AF.Silu), reads=[r_sc], writes=[r_sc])
            wsrc = self.W[("w_ada", l)]
            for jg in range(24):
                wt, rwt = wp()
                P.dma("sp", wt[:], wsrc[:, jg * 512:(jg + 1) * 512].rearrange("(k p) n -> p k n", p=128), writes=[rwt])
                for jb in range(4):
                    j = jg * 4 + jb
                    ps, rps = self.bank()
                    for k in range(16):
                        P.op("pe", lambda h: h.matmul(ps[:, 0:2], wt[:, k, jb * 128:(jb + 1) * 128], sc[:, k, :],
                                                      start=(k == 0), stop=(k == 15)),
                             reads=[rwt, r_sc], writes=[rps], inc=(k == 15))
                    P.op("dve", lambda h: h.tensor_single_scalar(self.mod[:, j, :], ps[:, 0:2], self.vec[:, V_BADA + j:V_BADA + j + 1], ALU.add),
                         reads=[rps, self.r_vec], writes=[self.r_mod])
            mv, mod, vec = self.mv, self.mod, self.vec
            for v in range(2):
                for (i_out, sc_off, g_off) in ((0, 16, 0), (3, 64, 2)):
                    P.op("dve", lambda h: h.tensor_single_scalar(mv[:, i_out, :, v], mod[:, sc_off:sc_off + 16, v], 1.0, ALU.add),
                         reads=[self.r_mod], writes=[self.r_mod])
                    P.op("dve", lambda h: h.tensor_tensor(mv[:, i_out, :, v], mv[:, i_out, :, v], vec[:, V_G + 16 * g_off:V_G + 16 * g_off + 16], ALU.mult),
                         reads=[self.r_mod, self.r_vec], writes=[self.r_mod])
                for (i_out, sh_off) in ((1, 0), (4, 48)):
                    P.op("dve", lambda h: h.tensor_copy(mv[:, i_out, :, v], mod[:, sh_off:sh_off + 16, v]),
                         reads=[self.r_mod], writes=[self.r_mod])
                for (i_out, gt_off, g_off) in ((2, 32, 1), (5, 80, 3)):
                    P.op("dve", lambda h: h.tensor_tensor(mv[:, i_out, :, v], mod[:, gt_off:gt_off + 16, v], vec[:, V_G + 16 * g_off:V_G + 16 * g_off + 16], ALU.mult),
                         reads=[self.r_mod, self.r_vec], writes=[self.r_mod])
            tmp, r_tmp = self.sb(st, [128, 128], F32, "lamtmp")
            s2, r_s2 = self.sb(st, [128, 2], F32, "lams")
            lamv = vec[:, V_LAM:V_LAM + 256]
            P.op("dve", lambda h: h.tensor_tensor(tmp[:, 0:64], vec[:, V_LAM:V_LAM + 64], vec[:, V_LAM + 64:V_LAM + 128], ALU.mult),
                 reads=[self.r_vec], writes=[r_tmp])
            P.op("dve", lambda h: h.tensor_tensor(tmp[:, 64:128], vec[:, V_LAM + 128:V_LAM + 192], vec[:, V_LAM + 192:V_LAM + 256], ALU.mult),
                 reads=[self.r_vec], writes=[r_tmp])
            P.op("dve", lambda h: h.reduce_sum(s2[:, 0:1], tmp[:, 0:64], mybir.AxisListType.X), reads=[r_tmp], writes=[r_s2])
            P.op("dve", lambda h: h.reduce_sum(s2[:, 1:2], tmp[:, 64:128], mybir.AxisListType.X), reads=[r_tmp], writes=[r_s2])
            P.op("act", lambda h: h.activation(s2[:], s2[:], AF.Exp), reads=[r_s2], writes=[r_s2])
            P.op("dve", lambda h: h.scalar_tensor_tensor(self.lamc[:, 0:1], s2[:, 1:2], vec[:, V_LAMI:V_LAMI + 1], s2[:, 0:1], ALU.subtract, ALU.subtract),
                 reads=[r_s2, self.r_vec], writes=[self.r_mod])
            P.op("dve", lambda h: h.tensor_tensor(self.lamc[:, 1:2], vec[:, V_SUBLN:V_SUBLN + 1], vec[:, V_LAMI + 1:V_LAMI + 2], ALU.mult),
                 reads=[self.r_vec], writes=[self.r_mod])
            P.barrier()

    def rope_epi(self, st_pools, psa, rpa, psb, rpb, m, n, ctab, stab, r_tab, toff, dst_ap):
        P = self.P
        f32p, bfp = st_pools
        t1, r1 = f32p()
        t2, r2 = f32p()
        o, ro = bfp()
        P.op("dve", lambda h: h.tensor_tensor(t1[0:m, :n], psa[0:m, :n], ctab[0:m, toff:toff + n], ALU.mult), reads=[rpa, r_tab], writes=[r1])
        P.op("dve", lambda h: h.tensor_tensor(t2[0:m, :n], psb[0:m, :n], stab[0:m, toff:toff + n], ALU.mult), reads=[rpb, r_tab], writes=[r2])
        P.op("pool", lambda h: h.tensor_tensor(o[0:m, :n], t1[0:m, :n], t2[0:m, :n], ALU.add), reads=[r1, r2], writes=[ro])
        P.dma("sp", dst_ap, o[0:m, :n], reads=[ro])

    def phase_p1(self, l, xsrc):
        P = self.P
        mv = self.mv
        for sci, blocks in enumerate(SCS):
            sc0 = blocks[0][0]
            ntok = sum(b[1] for b in blocks)
            with ExitStack() as st:
                H, r_H = self.sb(st, [128, 16, 2304], BF16, "H")
                ctab, r_tab = self.sb(st, [128, 2304], F32, "ctab")
                stab, _ = self.sb(st, [128, 2304], F32, "stab")
                P.dma("sp", ctab[:, :ntok], self.ctab_d[:, sc0:sc0 + ntok], writes=[r_tab])
                P.dma("sp", stab[:, :ntok], self.stab_d[:, sc0:sc0 + ntok], writes=[r_tab])
                with ExitStack() as st2:
                    xp = self.pool(st2, 1, [128, 16, 512], F32, "xt")
                    sqp = self.pool(st2, 3, [128, 512], BF16, "sq")
                    rsp = self.pool(st2, 2, [128, 512], F32, "rstd")
                    for (t0, n) in blocks:
                        v = 1 if t0 < NCTX else 0
                        off = t0 - sc0
                        xt, r_xt = xp()
                        P.dma("sp", xt[:, :, :n], xsrc[:, t0:t0 + n].rearrange("(k p) n -> p k n", p=128), writes=[r_xt])
                        rstd, r_rs = rsp()
                        self.fm_rstd([(xt[:, k, :n], r_xt) for k in range(16)], n, 1.0 / D, 1, sqp, rstd[:, :n], r_rs)
                        for k in range(16):
                            P.op("dve", lambda h: h.tensor_tensor(xt[:, k, :n], xt[:, k, :n], rstd[:, :n], ALU.mult), reads=[r_xt, r_rs], writes=[r_xt])
                            P.op("act", lambda h: h.activation(H[:, k, off:off + n], xt[:, k, :n], AF.Identity,
                                                               bias=mv[:, 1, k, v:v + 1], scale=mv[:, 0, k, v:v + 1]),
                                 reads=[r_xt, self.r_mod], writes=[r_H])
                    P.barrier()
                wp = self.pool(st, 2, [128, 16, 512], BF16, "win")
                f32p = self.pool(st, 4, [128, 512], F32, "stf")
                bfp = self.pool(st, 4, [128, 512], BF16, "stb")
                mqs, r_mqs = self.sb(st, [128, 4, 512], F32, "mqs")
                sqp = self.pool(st, 3, [128, 512], BF16, "sq")
                rsp = self.pool(st, 2, [128, 512], F32, "rstd")
                evi = [0]

                def mm_fm(ps, rps, wt, rwt, c0, m, off, n):
                    for k in range(16):
                        P.op("pe", lambda h: h.matmul(ps[0:m, :n], wt[:, k, c0:c0 + m], H[:, k, off:off + n], start=(k == 0), stop=(k == 15)),
                             reads=[rwt, r_H], writes=[rps], inc=(k == 15))

                for g in range(31):
                    wt, rwt = wp()
                    ncol = 512 if g < 30 else 128
                    self.wload("sp", wt[:, :, :ncol], "w_in", l, list(range(16)), g * 512, g * 512 + ncol, rwt)
                    for (t0, n) in blocks:
                        off = t0 - sc0
                        if g < 8:
                            for pr in range(2):
                                b = g * 4 + pr * 2
                                psa, rpa = self.bank()
                                mm_fm(psa, rpa, wt, rwt, pr * 256, 128, off, n)
                                psb, rpb = self.bank()
                                mm_fm(psb, rpb, wt, rwt, pr * 256 + 128, 128, off, n)
                                dst = self.S_QD if b < 16 else self.S_KD
                                hh = (b % 16) // 2
                                self.rope_epi((f32p, bfp), psa, rpa, psb, rpb, 128, n, ctab, stab, r_tab, off,
                                              dst[hh * 128:(hh + 1) * 128, t0:t0 + n])
                        elif g == 30:
                            psa, rpa = self.bank()
                            mm_fm(psa, rpa, wt, rwt, 0, 64, off, n)
                            psb, rpb = self.bank()
                            mm_fm(psb, rpb, wt, rwt, 64, 64, off, n)
                            self.rope_epi((f32p, bfp), psa, rpa, psb, rpb, 64, n, ctab, stab, r_tab, off, self.S_KR[:, t0:t0 + n])
                        elif g in (16, 17):
                            pss = []
                            for jb in range(4):
                                ps, rps = self.bank()
                                mm_fm(ps, rps, wt, rwt, jb * 128, 128, off, n)
                                P.op("act" if jb % 2 == 0 else "dve",
                                     (lambda h: h.activation(mqs[:, jb, :n], ps[:, :n], AF.Copy)) if jb % 2 == 0 else
                                     (lambda h: h.tensor_copy(mqs[:, jb, :n], ps[:, :n])),
                                     reads=[rps], writes=[r_mqs])
                            rstd, r_rs = rsp()
                            self.fm_rstd([(mqs[:, jb, :n], r_mqs) for jb in range(4)], n, 1.0 / 512, 1, sqp, rstd[:, :n], r_rs)
                            gofs = V_MQN if g == 16 else V_MKVN
                            dst = self.S_MQ if g == 16 else self.S_MKV
                            for jb in range(4):
                                o, ro = bfp()
                                P.op("dve", lambda h: h.tensor_tensor(mqs[:, jb, :n], mqs[:, jb, :n], rstd[:, :n], ALU.mult), reads=[r_mqs, r_rs], writes=[r_mqs])
                                P.op("act", lambda h: h.activation(o[:, :n], mqs[:, jb, :n], AF.Identity, scale=self.vec[:, gofs + jb:gofs + jb + 1]),
                                     reads=[r_mqs, self.r_vec], writes=[ro])
                                P.dma("sp", dst[jb * 128:(jb + 1) * 128, t0:t0 + n], o[:, :n], reads=[ro])
                        else:
                            for jb in range(4):
                                b = g * 4 + jb
                                ps, rps = self.bank()
                                mm_fm(ps, rps, wt, rwt, jb * 128, 128, off, n)
                                if b < 56:
                                    o, ro = f32p()
                                    evi[0] += 1
                                    if evi[0] % 2 == 0:
                                        P.op("act", lambda h: h.activation(o[:, :n], ps[:, :n], AF.Copy), reads=[rps], writes=[ro])
                                    else:
                                        P.op("dve", lambda h: h.tensor_copy(o[:, :n], ps[:, :n]), reads=[rps], writes=[ro])
                                    P.dma("sp", self.S_G[(b - 32) * 128:(b - 31) * 128, t0:t0 + n], o[:, :n], reads=[ro])
                                elif b < 64:
                                    o, ro = bfp()
                                    P.op("act", lambda h: h.activation(o[:, :n], ps[:, :n], AF.Silu), reads=[rps], writes=[ro])
                                    P.dma("sp", self.S_Z[(b - 56) * 128:(b - 55) * 128, t0:t0 + n], o[:, :n], reads=[ro])
                                else:
                                    o, ro = bfp()
                                    P.op("act", lambda h: h.activation(o[:, :n], ps[:, :n], AF.Sigmoid), reads=[rps], writes=[ro])
                                    P.dma("sp", self.S_GT[(b - 72) * 128:(b - 71) * 128, t0:t0 + n], o[:, :n], reads=[ro])
                for gi in range(3):
                    wt, rwt = wp()
                    c0 = 15488 + gi * 512
                    ncol = 512 if gi < 2 else 32
                    self.wload("sp", wt[:, :, :ncol], "w_in", l, list(range(16)), c0, c0 + ncol, rwt)
                    for tt in range(ntok // 128):
                        ps, rps = self.bank()
                        for k in range(16):
                            P.op("pe", lambda h: h.matmul(ps[:, :ncol], H[:, k, tt * 128:(tt + 1) * 128], wt[:, k, :ncol], start=(k == 0), stop=(k == 15)),
                                 reads=[rwt, r_H], writes=[rps], inc=(k == 15))
                        trow = sc0 + tt * 128
                        if gi < 2:
                            o, ro = bfp()
                            if tt % 2 == 0:
                                P.op("act", lambda h: h.activation(o[:, :], ps[:, :], AF.Copy), reads=[rps], writes=[ro])
                            else:
                                P.op("dve", lambda h: h.tensor_copy(o[:, :], ps[:, :]), reads=[rps], writes=[ro])
                            P.dma("sp", self.S_VD[trow:trow + 128, gi * 512:(gi + 1) * 512], o[:, :], reads=[ro])
                        else:
                            o, ro = f32p()
                            P.op("dve", lambda h: h.tensor_copy(o[:, :32], ps[:, :32]), reads=[rps], writes=[ro])
                            P.dma("sp", self.S_AB[trow:trow + 128, :], o[:, :32], reads=[ro])
                P.barrier()

    def phase_mla_up(self, l):
        P = self.P
        with ExitStack() as st:
            wq, r_wq = self.sb(st, [128, 4, 2048], BF16, "wq")
            wkv, r_wkv = self.sb(st, [128, 4, 2048], BF16, "wkv")
            self.wload("sp", wq[:], "mqup", l, [0, 1, 2, 3], 0, 2048, r_wq)
            self.wload("sp", wkv[:], "mkvup", l, [0, 1, 2, 3], 0, 2048, r_wkv)
            ap = self.pool(st, 2, [128, 8, 512], BF16, "mqkv")
            tabp = self.pool(st, 2, [128, 2, 512], F32, "tab")
            f32p = self.pool(st, 4, [128, 512], F32, "stf")
            bfp = self.pool(st, 6, [128, 512], BF16, "stb")
            ev = [0]

            def copy_out(ps, rps, n, dst):
                o, ro = bfp()
                ev[0] += 1
                if ev[0] % 2 == 0:
                    P.op("act", lambda h: h.activation(o[:, :n], ps[:, :n], AF.Copy), reads=[rps], writes=[ro])
                else:
                    P.op("dve", lambda h: h.tensor_copy(o[:, :n], ps[:, :n]), reads=[rps], writes=[ro])
                P.dma("sp", dst, o[:, :n], reads=[ro])

            for (t0, n) in TB:
                a, r_a = ap()
                P.dma("sp", a[:, 0:4, :n], self.S_MQ[:, t0:t0 + n].rearrange("(k p) n -> p k n", p=128), writes=[r_a])
                P.dma("sp", a[:, 4:8, :n], self.S_MKV[:, t0:t0 + n].rearrange("(k p) n -> p k n", p=128), writes=[r_a])
                tab, r_tab = tabp()
                P.dma("sp", tab[:, 0, :n], self.ctab_d[:, t0:t0 + n], writes=[r_tab])
                P.dma("sp", tab[:, 1, :n], self.stab_d[:, t0:t0 + n], writes=[r_tab])

                def mm(ps, rps, w, rw, c0, kofs):
                    for k in range(4):
                        P.op("pe", lambda h: h.matmul(ps[:, :n], w[:, k, c0:c0 + 128], a[:, kofs + k, :n], start=(k == 0), stop=(k == 3)),
                             reads=[rw, r_a], writes=[rps], inc=(k == 3))
                for hh in range(8):
                    ps, rps = self.bank()
                    mm(ps, rps, wq, r_wq, hh * 128, 0)
                    copy_out(ps, rps, n, self.S_MQN[hh * 128:(hh + 1) * 128, t0:t0 + n])
                for j in range(4):
                    psa, rpa = self.bank()
                    mm(psa, rpa, wq, r_wq, 1024 + j * 256, 0)
                    psb, rpb = self.bank()
                    mm(psb, rpb, wq, r_wq, 1024 + j * 256 + 128, 0)
                    self.rope_epi((f32p, bfp), psa, rpa, psb, rpb, 128, n, tab[:, 0, :], tab[:, 1, :], r_tab, 0,
                                  self.S_MQR[j * 128:(j + 1) * 128, t0:t0 + n])
                for hh in range(8):
                    ps, rps = self.bank()
                    mm(ps, rps, wkv, r_wkv, hh * 128, 4)
                    copy_out(ps, rps, n, self.S_MKN[hh * 128:(hh + 1) * 128, t0:t0 + n])
                for tt in range(n // 128):
                    for half in range(2):
                        ps, rps = self.bank()
                        for k in range(4):
                            P.op("pe", lambda h: h.matmul(ps[:, :], a[:, 4 + k, tt * 128:(tt + 1) * 128], wkv[:, k, 1024 + half * 512:1024 + (half + 1) * 512],
                                                          start=(k == 0), stop=(k == 3)),
                                 reads=[r_wkv, r_a], writes=[rps], inc=(k == 3))
                        copy_out(ps, rps, 512, self.S_VM[t0 + tt * 128:t0 + (tt + 1) * 128, half * 512:(half + 1) * 512])
            P.barrier()

    def attn_core(self, kts, n, s_ops, V, r_V, scale, ptp, sbanks, ol):
        P = self.P
        (O, rO), (L, rL) = ol
        nk = len(kts)
        look = 2
        pts = {}

        def emit_s(idx):
            kt = kts[idx]
            S, rS = sbanks()
            for j, (lf, rhs, rd) in enumerate(s_ops):
                P.op("pe", lambda h: h.matmul(S[:, :n], lf(kt), rhs, start=(j == 0), stop=(j == len(s_ops) - 1)),
                     reads=rd, writes=[rS], inc=(j == len(s_ops) - 1))
            Pt, rPt = ptp()
            P.op("act", lambda h: h.activation(Pt[:, :n], S[:, :n], AF.Exp, scale=scale), reads=[rS], writes=[rPt])
            pts[idx] = (Pt, rPt)

        def emit_pv(idx):
            kt = kts[idx]
            Pt, rPt = pts.pop(idx)
            P.op("pe", lambda h: h.matmul(O[:, :n], V[:, kt, :], Pt[:, :n], start=(idx == 0), stop=(idx == nk - 1)),
                 reads=[r_V, rPt], writes=[rO], inc=False)
            P.op("pe", lambda h: h.matmul(L[:, :n], self.ones_bf[:], Pt[:, :n], start=(idx == 0), stop=(idx == nk - 1)),
                 reads=[rPt, self.r_const], writes=[rL], inc=True)

        for idx in range(min(look, nk)):
            emit_s(idx)
        for idx in range(nk):
            if idx + look < nk:
                emit_s(idx + look)
            emit_pv(idx)

    def phase_attn(self, l):
        P = self.P
        with ExitStack() as st:
            ktp = self.pool(st, 2, [128, T], BF16, "kt")
            vp = self.pool(st, 2, [128, NT, 128], BF16, "v")
            kr, r_kr = self.sb(st, [64, T], BF16, "kr")
            qp = self.pool(st, 2, [128, 512], BF16, "q")
            qrp = self.pool(st, 2, [64, 512], BF16, "qr")
            ptp = self.pool(st, 5, [128, 512], BF16, "pt")
            omp = self.pool(st, 4, [128, 512], F32, "om")
            rlp = self.pool(st, 2, [128, 512], F32, "rl")
            bfp = self.pool(st, 3, [128, 512], BF16, "stb")
            sqp = self.pool(st, 2, [128, 512], BF16, "sq")
            rsp = self.pool(st, 2, [128, 512], F32, "rstd")
            srr = [0]

            def sbanks():
                i = srr[0] % 3
                srr[0] += 1
                return self.ps[i], self.rps[i]
            olr = [0]

            def olpair():
                i = olr[0] % 2
                olr[0] += 1
                return (self.ps[3 + i], self.rps[3 + i]), (self.ps[5 + i], self.rps[5 + i])

            P.dma("sp", kr[:], self.S_KR, writes=[r_kr])
            items = [(kind, hh, tb) for kind in ("diff", "mla") for hh in range(8) for tb in TB]
            loaded = {}

            def load(item):
                kind, hh, (t0, n) = item
                key = (kind, hh)
                if key not in loaded:
                    kt, r_kt = ktp()
                    v, r_v = vp()
                    ksrc = self.S_KD if kind == "diff" else self.S_MKN
                    vsrc = self.S_VD if kind == "diff" else self.S_VM
                    P.dma("sp", kt[:], ksrc[hh * 128:(hh + 1) * 128, :], writes=[r_kt])
                    P.dma("sp", v[:], vsrc[:, hh * 128:(hh + 1) * 128].rearrange("(t p) c -> p t c", p=128), writes=[r_v])
                    loaded.clear()
                    loaded[key] = (kt, r_kt, v, r_v)
                q, r_q = qp()
                qsrc = self.S_QD if kind == "diff" else self.S_MQN
                P.dma("sp", q[:, :n], qsrc[hh * 128:(hh + 1) * 128, t0:t0 + n], writes=[r_q])
                qr = r_qr = None
                if kind == "mla":
                    qr, r_qr = qrp()
                    P.dma("sp", qr[:, :n], self.S_MQR[hh * 64:(hh + 1) * 64, t0:t0 + n], writes=[r_qr])
                return loaded[key] + (q, r_q, qr, r_qr)

            nxt = load(items[0])
            for ii, item in enumerate(items):
                kind, hh, (t0, n) = item
                kt, r_kt, v, r_v, q, r_q, qr, r_qr = nxt
                if ii + 1 < len(items):
                    nxt = load(items[ii + 1])
                kts = [0, 1] if t0 < NCTX else list(range(NT))
                if kind == "diff":
                    oms = []
                    for m in range(2):
                        ol = olpair()
                        s_ops = [(lambda k_, m=m: kt[m * 64:(m + 1) * 64, k_ * 128:(k_ + 1) * 128], q[m * 64:(m + 1) * 64, :n], [r_kt, r_q])]
                        self.attn_core(kts, n, s_ops, v, r_v, 0.125, ptp, sbanks, ol)
                        (O, rO), (L, rL) = ol
                        rl, r_rl = rlp()
                        om, r_om = omp()
                        P.op("dve", lambda h: h.reciprocal(rl[:, :n], L[:, :n]), reads=[rL], writes=[r_rl])
                        P.op("dve", lambda h: h.tensor_tensor(om[:, :n], O[:, :n], rl[:, :n], ALU.mult), reads=[rO, r_rl], writes=[r_om])
                        oms.append((om, r_om))
                    (o0, r0), (o1, r1) = oms
                    P.op("dve", lambda h: h.scalar_tensor_tensor(o0[:, :n], o1[:, :n], self.lamc[:, 0:1], o0[:, :n], ALU.mult, ALU.add),
                         reads=[r0, r1, self.r_mod], writes=[r0])
                    rstd, r_rs = rsp()
                    self.fm_rstd([(o0[:, :n], r0)], n, 1.0 / 128, 2, sqp, rstd[:, :n], r_rs, bank=7)
                    P.op("dve", lambda h: h.tensor_tensor(o0[:, :n], o0[:, :n], rstd[:, :n], ALU.mult), reads=[r0, r_rs], writes=[r0])
                    o, ro = bfp()
                    P.op("act", lambda h: h.activation(o[:, :n], o0[:, :n], AF.Identity, scale=self.lamc[:, 1:2]), reads=[r0, self.r_mod], writes=[ro])
                    P.dma("sp", self.S_YA[hh * 128:(hh + 1) * 128, t0:t0 + n], o[:, :n], reads=[ro])
                else:
                    ol = olpair()
                    s_ops = [(lambda k_: kt[:, k_ * 128:(k_ + 1) * 128], q[:, :n], [r_kt, r_q]),
                             (lambda k_: kr[0:64, k_ * 128:(k_ + 1) * 128], qr[0:64, :n], [r_kr, r_qr])]
                    self.attn_core(kts, n, s_ops, v, r_v, 192.0 ** -0.5, ptp, sbanks, ol)
                    (O, rO), (L, rL) = ol
                    rl, r_rl = rlp()
                    P.op("dve", lambda h: h.reciprocal(rl[:, :n], L[:, :n]), reads=[rL], writes=[r_rl])
                    o, ro = bfp()
                    P.op("dve", lambda h: h.tensor_tensor(o[:, :n], O[:, :n], rl[:, :n], ALU.mult), reads=[rO, r_rl], writes=[ro])
                    P.dma("sp", self.S_YC[hh * 128:(hh + 1) * 128, t0:t0 + n], o[:, :n], reads=[ro])
            P.barrier()

    def phase_gdn(self, l):
        P = self.P
        c128 = self.c128
        ident = self.ident()
        with ExitStack() as st:
            be2, r_be2 = self.sb(st, [128, 2, NT * 8], F32, "be2")
            gcum, r_gc = self.sb(st, [128, 2, NT * 8], F32, "gcum")
            egc, r_egc = self.sb(st, [128, 2, NT * 8], F32, "egc")
            etl, r_etl = self.sb(st, [128, 2, NT * 8], F32, "etl")
            egl, r_egl = self.sb(st, [128, 2, NT * 8], F32, "egl")
            bege, r_bege = self.sb(st, [128, 2, NT * 8], F32, "bege")
            st0 = ExitStack()
            ab, r_ab = self.sb(st0, [128, NT, 32], F32, "ab")
            t1, r_t1 = self.sb(st0, [128, NT, 16], F32, "t1")
            t2, r_t2 = self.sb(st0, [128, NT, 16], F32, "t2")
            g2, r_g2 = self.sb(st0, [128, 2, NT * 8], F32, "g2")
            gtot, r_gt = self.sb(st0, [128, 2, NT * 8], F32, "gtot")
            vec = self.vec
            P.dma("sp", ab[:], self.S_AB.rearrange("(t p) c -> p t c", p=128), writes=[r_ab])
            dtb = vec[:, V_DTB:V_DTB + 544].rearrange("p (t c) -> p t c", c=16)
            alog = vec[:, V_ALOG:V_ALOG + 544].rearrange("p (t c) -> p t c", c=16)
            P.op("dve", lambda h: h.tensor_tensor(t1[:], ab[:, :, 0:16], dtb, ALU.add), reads=[r_ab, self.r_vec], writes=[r_t1])
            P.op("act", lambda h: h.activation(t1[:], t1[:], AF.Exp), reads=[r_t1], writes=[r_t1])
            P.op("act", lambda h: h.activation(t1[:], t1[:], AF.Ln, bias=self.cols[:, 0:1]), reads=[r_t1, self.r_const], writes=[r_t1])
            P.op("act", lambda h: h.activation(t2[:], alog, AF.Exp), reads=[self.r_vec], writes=[r_t2])
            P.op("dve", lambda h: h.scalar_tensor_tensor(t1[:], t1[:], -1.0, t2[:], ALU.mult, ALU.mult), reads=[r_t1, r_t2], writes=[r_t1])
            P.op("act", lambda h: h.activation(t2[:], ab[:, :, 16:32], AF.Sigmoid), reads=[r_ab, r_t1], writes=[r_t2])
            for d in range(2):
                P.op("dve", lambda h: h.tensor_copy(g2[:, d, :].rearrange("p (t c) -> p t c", c=8), t1[:, :, d * 8:(d + 1) * 8]), reads=[r_t1], writes=[r_g2])
                P.op("dve", lambda h: h.tensor_copy(be2[:, d, :].rearrange("p (t c) -> p t c", c=8), t2[:, :, d * 8:(d + 1) * 8]), reads=[r_t2], writes=[r_be2])
            for d in range(2):
                tri = c128[:, (C_TRIF if d == 0 else C_TRIB):(C_TRIF if d == 0 else C_TRIB) + 128]
                ps, rps = self.bank()
                P.op("pe", lambda h: h.matmul(ps[:, :272], tri, g2[:, d, :], start=True, stop=True), reads=[r_g2, self.r_const], writes=[rps])
                P.op("dve", lambda h: h.tensor_copy(gcum[:, d, :], ps[:, :272]), reads=[rps], writes=[r_gc])
                ps, rps = self.bank()
                P.op("pe", lambda h: h.matmul(ps[:, :272], self.onesf(), g2[:, d, :], start=True, stop=True), reads=[r_g2, self.r_const], writes=[rps])
                P.op("dve", lambda h: h.tensor_copy(gtot[:, d, :], ps[:, :272]), reads=[rps], writes=[r_gt])
            P.op("act", lambda h: h.activation(egc[:], gcum[:], AF.Exp), reads=[r_gc], writes=[r_egc])
            P.op("act", lambda h: h.activation(egl[:], gtot[:], AF.Exp), reads=[r_gt], writes=[r_egl])
            P.op("dve", lambda h: h.tensor_tensor(etl[:], gtot[:], gcum[:], ALU.subtract), reads=[r_gt, r_gc], writes=[r_etl])
            P.op("act", lambda h: h.activation(etl[:], etl[:], AF.Exp), reads=[r_etl], writes=[r_etl])
            P.op("dve", lambda h: h.tensor_tensor(bege[:], be2[:], egc[:], ALU.mult), reads=[r_be2, r_egc], writes=[r_bege])
            P.barrier()
            st0.close()
            stats = dict(gcum=gcum, egc=egc, etl=etl, egl=egl, beta=be2, bege=bege)

            B, r_B = self.sb(st, [128, 4360], F32, "cbuf")
            qT, r_qT = self.sb(st, [128, T], F32, "qT")
            kT, r_kT = self.sb(st, [128, T], F32, "kT")
            ktm, r_ktm = self.sb(st, [128, NT, 128], F32, "ktm")
            vtm, r_vtm = self.sb(st, [128, NT, 128], F32, "vtm")
            sqp = self.pool(st, 2, [128, 512], BF16, "sq")
            rsp = self.pool(st, 2, [128, 512], F32, "rstd")
            Sst = [self.sb(st, [128, 128], F32, "S0"), self.sb(st, [128, 128], F32, "S1")]
            ostg = [self.pool(st, 2, [128, 4, 128], F32, "ostg0"), self.pool(st, 2, [128, 4, 128], F32, "ostg1")]
            tmps = [{}, {}]

            def tmp(c, name, n=2):
                if name not in tmps[c]:
                    tmps[c][name] = self.pool(st, n, [128, 128], F32, f"{name}{c}")
                    if _os.environ.get('GDN_TRACE'):
                        print("TMP", name, c, "sbuf_base", self.nc.sbuf_base)
                return tmps[c][name]()
            qres = [[Res(f"q{c}_{i}") for i in range(4)] for c in range(2)]
            qrr = [0, 0]

            def qps(c):
                i = qrr[c] % 4
                qrr[c] += 1
                return self.ps[4 * c + i][:, 0:128], qres[c][i]

            P.op("dve", lambda h: h.memset(B[:], 0.0), writes=[r_B])
            import os as _os
            for c_ in range(2):
                for nm_, n_ in (("Dg", 1), ("Db", 1), ("dpre", 1), ("DT", 1), ("Er", 1), ("DTs", 1), ("DTi", 1), ("KKs", 1), ("Yt", 1), ("Y0", 2), ("AiT", 2), ("qd", 2), ("X", 2), ("Yd", 1), ("Xd", 1), ("R", 3), ("Xn", 2), ("Yn", 2), ("Inv", 3), ("Xb", 1), ("Qs", 1), ("Yb", 1), ("Q2s", 1), ("vb", 2), ("kbg", 2), ("ktl", 2), ("u", 2), ("wT", 2), ("vn", 2)):
                    tmp(c_, nm_, n_)
            if _os.environ.get('GDN_STOP') == 'g0':
                return
            for hh in range(int(_os.environ.get('GDN_HEADS', '8'))):
                for typ, (Y, r_Y) in ((2, (kT, r_kT)), (1, (kT, r_kT)), (0, (qT, r_qT))):
                    blk = typ * 8 + hh
                    src = self.S_G[blk * 128:(blk + 1) * 128, :]
                    P.dma("sp", B[:, 1:257], src[:, 0:256], writes=[r_B])
                    P.dma("sp", B[:, 259:4355], src[:, 256:T], writes=[r_B])
                    w = [vec[:, V_CONV + blk * 3 + tp:V_CONV + blk * 3 + tp + 1] for tp in range(3)]
                    for (s0, ln, y0) in ((1, 256, 0), (259, 4096, 256)):
                        e1 = "dve"
                        P.op(e1, lambda h: h.tensor_single_scalar(Y[:, y0:y0 + ln], B[:, s0:s0 + ln], w[1], ALU.mult), reads=[r_B, self.r_vec], writes=[r_Y])
                        P.op(e1, lambda h: h.scalar_tensor_tensor(Y[:, y0:y0 + ln], B[:, s0 - 1:s0 - 1 + ln], w[0], Y[:, y0:y0 + ln], ALU.mult, ALU.add), reads=[r_B, self.r_vec, r_Y], writes=[r_Y])
                        P.op(e1, lambda h: h.scalar_tensor_tensor(Y[:, y0:y0 + ln], B[:, s0 + 1:s0 + 1 + ln], w[2], Y[:, y0:y0 + ln], ALU.mult, ALU.add), reads=[r_B, self.r_vec, r_Y], writes=[r_Y])
                    P.op("act", lambda h: h.activation(Y[:], Y[:], AF.Silu), reads=[r_Y], writes=[r_Y])
                    if typ < 2:
                        for (t0, n) in TB:
                            rstd, r_rs = rsp()
                            self.fm_rstd([(Y[:, t0:t0 + n], r_Y)], n, 1.0, 1, sqp, rstd[:, :n], r_rs)
                            sc_ = (128.0 ** -0.5) if typ == 0 else 1.0
                            P.op("dve", lambda h: h.scalar_tensor_tensor(Y[:, t0:t0 + n], Y[:, t0:t0 + n], sc_, rstd[:, :n], ALU.mult, ALU.mult), reads=[r_Y, r_rs], writes=[r_Y])
                    if typ in (1, 2):
                        dst, r_d = (ktm, r_ktm) if typ == 1 else (vtm, r_vtm)
                        for t4 in range(0, NT, 4):
                            ps, rps = self.bank()
                            nn = min(4, NT - t4)
                            for j in range(nn):
                                P.op("pe", lambda h: h.transpose(ps[:, j * 128:(j + 1) * 128], Y[:, (t4 + j) * 128:(t4 + j + 1) * 128], ident), reads=[r_Y, self.r_const], writes=[rps], inc=(j == nn - 1))
                            if (t4 // 4) % 2 == 0:
                                P.op("act", lambda h: h.activation(dst[:, t4:t4 + nn, :], ps[:, :nn * 128].rearrange("p (t c) -> p t c", c=128), AF.Copy), reads=[rps], writes=[r_d])
                            else:
                                P.op("dve", lambda h: h.tensor_copy(dst[:, t4:t4 + nn, :], ps[:, :nn * 128].rearrange("p (t c) -> p t c", c=128)), reads=[rps], writes=[r_d])
                P.barrier()
                if _os.environ.get('GDN_STOP') == 'g1':
                    continue
                gens = [self.gdn_chain(c, hh, stats, qT, r_qT, kT, r_kT, ktm, r_ktm, vtm, r_vtm, Sst[c], ostg[c], tmp, qps) for c in range(2)]
                P.trace = bool(_os.environ.get('GDN_TRACE'))
                alive = [True, True]
                budget = int(_os.environ.get('GDN_OPS', '100000000'))
                while any(alive) and budget > 0:
                    for c in range(2):
                        if alive[c]:
                            try:
                                next(gens[c])
                                budget -= 1
                            except StopIteration:
                                alive[c] = False
                P.barrier()

    def gdn_chain(self, c, hh, stats, qT, r_qT, kT, r_kT, ktm, r_ktm, vtm, r_vtm, Sres, ostg, tmp, qps):
        P = self.P
        c128 = self.c128
        S, r_S = Sres
        order = list(range(NT)) if c == 0 else [1, 0] + list(range(NT - 1, 1, -1))
        mS = c128[:, (C_MSF if c == 0 else C_MSB):(C_MSF if c == 0 else C_MSB) + 128]
        mI = c128[:, (C_MIF if c == 0 else C_MIB):(C_MIF if c == 0 else C_MIB) + 128]
        ident = self.ident()
        rc = self.r_const
        P.op("pool", lambda h: h.memset(S[:], 0.0), writes=[r_S])
        yield
        ost = None
        import os as _os
        order = order[:int(_os.environ.get('GDN_STEPS', '34'))]
        for si, t in enumerate(order):
            ts = slice(t * 128, (t + 1) * 128)
            ci = t * 8 + hh

            def col(nm):
                return stats[nm][:, c, ci:ci + 1]
            r_st = [Res()]
            Dr, rDr = qps(c)
            Br, rBr = qps(c)
            Dg, r_Dg = tmp(c, "Dg", 1)
            Db, r_Db = tmp(c, "Db", 1)
            P.op("pool", lambda h: h.tensor_single_scalar(Dg[:], ident, col("gcum"), ALU.mult), reads=[rc], writes=[r_Dg]); yield
            P.op("pool", lambda h: h.tensor_single_scalar(Db[:], ident, col("beta"), ALU.mult), reads=[rc], writes=[r_Db]); yield
            P.op("pe", lambda h: h.matmul(Dr, self.onesf(), Dg[:], start=True, stop=True), reads=[r_Dg, rc], writes=[rDr]); yield
            P.op("pe", lambda h: h.matmul(Br, self.onesf(), Db[:], start=True, stop=True), reads=[r_Db, rc], writes=[rBr]); yield
            KK, rKK = qps(c)
            QK, rQK = qps(c)
            P.op("pe", lambda h: h.matmul(KK, kT[:, ts], kT[:, ts], start=True, stop=True), reads=[r_kT], writes=[rKK]); yield
            P.op("pe", lambda h: h.matmul(QK, kT[:, ts], qT[:, ts], start=True, stop=True), reads=[r_kT, r_qT], writes=[rQK]); yield
            dpre, r_dpre = tmp(c, "dpre", 1)
            P.op("dve", lambda h: h.tensor_scalar(dpre[:], Dr, col("gcum"), 0.0, ALU.subtract, ALU.min), reads=[], writes=[r_dpre, rDr]); yield
            DT, r_DT = tmp(c, "DT", 1)
            P.op("act", lambda h: h.activation(DT[:], dpre[:], AF.Exp), reads=[r_dpre], writes=[r_DT]); yield
            Er, r_Er = tmp(c, "Er", 1)
            import os as _o3
            if _o3.environ.get("GDN_NOER"):
                P.op("act", lambda h: h.activation(Er[:], dpre[:], AF.Exp), reads=[r_dpre], writes=[r_Er]); yield
            else:
                P.op("act", lambda h: h.activation(Er[:], Dr, AF.Exp), reads=[], writes=[r_Er, rDr]); yield
            DTs, r_DTs = tmp(c, "DTs", 1)
            P.op("pool", lambda h: h.tensor_tensor(DTs[:], DT[:], mS, ALU.mult), reads=[r_DT, rc], writes=[r_DTs]); yield
            DTi, r_DTi = tmp(c, "DTi", 1)
            P.op("pool", lambda h: h.tensor_tensor(DTi[:], DT[:], mI, ALU.mult), reads=[r_DT, rc], writes=[r_DTi]); yield
            KKs, r_KKs = tmp(c, "KKs", 1)
            P.op("act", lambda h: h.activation(KKs[:], KK, AF.Copy), reads=[], writes=[r_KKs, rKK]); yield
            Yt, r_Yt = tmp(c, "Yt", 1)
            P.op("dve", lambda h: h.tensor_tensor(Yt[:], KKs[:], Br, ALU.mult), reads=[r_KKs], writes=[r_Yt, rBr]); yield
            Y0, r_Y0 = tmp(c, "Y0")
            P.op("pool", lambda h: h.tensor_tensor(Y0[:], Yt[:], DTs[:], ALU.mult), reads=[r_Yt, r_DTs], writes=[r_Y0]); yield
            AiT, r_AiT = tmp(c, "AiT")
            P.op("dve", lambda h: h.tensor_tensor(AiT[:], DTi[:], QK, ALU.mult), reads=[r_DTi], writes=[r_AiT, rQK]); yield
            qd, r_qd = tmp(c, "qd")
            P.op("pool", lambda h: h.tensor_tensor(qd[:], qT[:, ts], Er[:], ALU.mult), reads=[r_qT, r_Er], writes=[r_qd]); yield
            Xp_ps, rXp_ps = qps(c)
            P.op("pe", lambda h: h.transpose(Xp_ps, Y0[:], ident), reads=[r_Y0, rc], writes=[rXp_ps]); yield
            Xp, r_Xp = tmp(c, "X", 2)
            P.op("act", lambda h: h.activation(Xp[:], Xp_ps, AF.Copy), reads=[], writes=[r_Xp, rXp_ps]); yield
            X0, r_X0 = Xp, r_Xp
            bd16 = c128[:, C_BD16:C_BD16 + 128]
            Yd, r_Yd = tmp(c, "Yd", 1)
            P.op("pool", lambda h: h.tensor_tensor(Yd[:], Y0[:], bd16, ALU.mult), reads=[r_Y0, rc], writes=[r_Yd]); yield
            Xd, r_Xd = tmp(c, "Xd", 1)
            P.op("pool", lambda h: h.tensor_tensor(Xd[:], X0[:], bd16, ALU.mult), reads=[r_X0, rc], writes=[r_Xd]); yield
            R, r_R = tmp(c, "R", 3)
            P.op("pool", lambda h: h.tensor_tensor(R[:], Yd[:], ident, ALU.add), reads=[r_Yd, rc], writes=[r_R]); yield
            Yp, r_Yp, Xp, r_Xp = Yd, r_Yd, Xd, r_Xd
            for k in range(1, 4):
                X2, rX2 = qps(c)
                P.op("pe", lambda h: h.matmul(X2, Yp[:], Xp[:], start=True, stop=True), reads=[r_Yp, r_Xp], writes=[rX2]); yield
                Xn, r_Xn = tmp(c, "Xn", 2)
                P.op("act", lambda h: h.activation(Xn[:], X2, AF.Copy), reads=[], writes=[r_Xn, rX2]); yield
                if k <= 2:
                    Y2, rY2 = qps(c)
                    P.op("pe", lambda h: h.matmul(Y2, Xp[:], Yp[:], start=True, stop=True), reads=[r_Yp, r_Xp], writes=[rY2]); yield
                    Yn, r_Yn = tmp(c, "Yn", 2)
                    P.op("dve", lambda h: h.tensor_copy(Yn[:], Y2), reads=[], writes=[r_Yn, rY2]); yield
                pr, rpr = qps(c)
                P.op("pe", lambda h: h.matmul(pr, Xn[:], R[:], start=True, stop=True), reads=[r_Xn, r_R], writes=[rpr]); yield
                P.op("dve", lambda h: h.tensor_tensor(R[:], R[:], pr, ALU.add), reads=[], writes=[r_R, rpr]); yield
                Xp, r_Xp = Xn, r_Xn
                if k <= 2:
                    Yp, r_Yp = Yn, r_Yn
            it_ps, rit_ps = qps(c)
            P.op("pe", lambda h: h.transpose(it_ps, R[:], ident), reads=[r_R, rc], writes=[rit_ps]); yield
            Inv, r_Inv = tmp(c, "Inv", 3)
            P.op("act", lambda h: h.activation(Inv[:], it_ps, AF.Copy), reads=[], writes=[r_Inv, rit_ps]); yield
            for lvl, mko in enumerate((C_O32, C_O64, C_O128)):
                mk = c128[:, mko:mko + 128]
                Xb, r_Xb = tmp(c, "Xb", 1)
                P.op("pool", lambda h: h.tensor_tensor(Xb[:], X0[:], mk, ALU.mult), reads=[r_X0, rc], writes=[r_Xb]); yield
                Q, rQ = qps(c)
                P.op("pe", lambda h: h.matmul(Q, Xb[:], R[:], start=True, stop=True), reads=[r_Xb, r_R], writes=[rQ]); yield
                Qs, r_Qs = tmp(c, "Qs", 1)
                P.op("dve", lambda h: h.tensor_copy(Qs[:], Q), reads=[], writes=[r_Qs, rQ]); yield
                if lvl < 2:
                    Yb, r_Yb = tmp(c, "Yb", 1)
                    P.op("pool", lambda h: h.tensor_tensor(Yb[:], Y0[:], mk, ALU.mult), reads=[r_Y0, rc], writes=[r_Yb]); yield
                    Q2, rQ2 = qps(c)
                    P.op("pe", lambda h: h.matmul(Q2, Yb[:], Inv[:], start=True, stop=True), reads=[r_Yb, r_Inv], writes=[rQ2]); yield
                    Q2s, r_Q2s = tmp(c, "Q2s", 1)
                    P.op("act", lambda h: h.activation(Q2s[:], Q2, AF.Copy), reads=[], writes=[r_Q2s, rQ2]); yield
                P1, rP1 = qps(c)
                P.op("pe", lambda h: h.matmul(P1, Inv[:], Qs[:], start=True, stop=True), reads=[r_Inv, r_Qs], writes=[rP1]); yield
                Rn, r_Rn = tmp(c, "R", 3)
                P.op("dve", lambda h: h.tensor_tensor(Rn[:], R[:], P1, ALU.add), reads=[r_R], writes=[r_Rn, rP1]); yield
                if lvl < 2:
                    P2, rP2 = qps(c)
                    P.op("pe", lambda h: h.matmul(P2, R[:], Q2s[:], start=True, stop=True), reads=[r_R, r_Q2s], writes=[rP2]); yield
                    Invn, r_Invn = tmp(c, "Inv", 3)
                    P.op("dve", lambda h: h.tensor_tensor(Invn[:], Inv[:], P2, ALU.add), reads=[r_Inv], writes=[r_Invn, rP2]); yield
                    Inv, r_Inv = Invn, r_Invn
                R, r_R = Rn, r_Rn
            vb, r_vb = tmp(c, "vb")
            P.op("pool", lambda h: h.tensor_single_scalar(vb[:], vtm[:, t, :], col("beta"), ALU.mult), reads=[r_vtm], writes=[r_vb]); yield
            kbg, r_kbg = tmp(c, "kbg")
            P.op("pool", lambda h: h.tensor_single_scalar(kbg[:], ktm[:, t, :], col("bege"), ALU.mult), reads=[r_ktm], writes=[r_kbg]); yield
            ktl, r_ktl = tmp(c, "ktl")
            P.op("pool", lambda h: h.tensor_single_scalar(ktl[:], ktm[:, t, :], col("etl"), ALU.mult), reads=[r_ktm], writes=[r_ktl]); yield
            u_ps, ru_ps = qps(c)
            P.op("pe", lambda h: h.matmul(u_ps, R[:], vb[:], start=True, stop=True), reads=[r_R, r_vb], writes=[ru_ps]); yield
            w_ps, rw_ps = qps(c)
            P.op("pe", lambda h: h.matmul(w_ps, kbg[:], R[:], start=True, stop=True), reads=[r_R, r_kbg], writes=[rw_ps]); yield
            u, r_u = tmp(c, "u")
            P.op("act", lambda h: h.activation(u[:], u_ps, AF.Copy), reads=[], writes=[r_u, ru_ps]); yield
            wT, r_wT = tmp(c, "wT")
            P.op("dve", lambda h: h.tensor_copy(wT[:], w_ps), reads=[], writes=[r_wT, rw_ps]); yield
            p1, rp1 = qps(c)
            P.op("pe", lambda h: h.matmul(p1, wT[:], S[:], start=True, stop=True), reads=[r_wT, r_S], writes=[rp1]); yield
            vn, r_vn = tmp(c, "vn")
            P.op("dve", lambda h: h.tensor_tensor(vn[:], u[:], p1, ALU.subtract), reads=[r_u], writes=[r_vn, rp1]); yield
            o_ps, ro_ps = qps(c)
            P.op("pe", lambda h: h.matmul(o_ps, S[:], qd[:], start=True, stop=False), reads=[r_S, r_qd], writes=[ro_ps], inc=False); yield
            P.op("pe", lambda h: h.matmul(o_ps, vn[:], AiT[:], start=False, stop=True), reads=[r_vn, r_AiT], writes=[ro_ps]); yield
            p3, rp3 = qps(c)
            P.op("pe", lambda h: h.matmul(p3, ktl[:], vn[:], start=True, stop=True), reads=[r_ktl, r_vn], writes=[rp3]); yield
            if si % 4 == 0:
                ost = ostg()
                ost_t0 = t
            P.op("act", lambda h: h.activation(ost[0][:, si % 4, :], o_ps, AF.Copy), reads=[], writes=[ost[1], ro_ps]); yield
            P.op("dve", lambda h: h.scalar_tensor_tensor(S[:], S[:], col("egl"), p3, ALU.mult, ALU.add), reads=[r_S], writes=[r_S, rp3]); yield
            if si % 4 == 3 or si == len(order) - 1:
                nst = si % 4 + 1
                for j in range(nst):
                    tt = order[si - nst + 1 + j]
                    P.dma("sp", self.S_OT[c][hh * 128:(hh + 1) * 128, tt * 128:(tt + 1) * 128], ost[0][:, j, :], reads=[ost[1]])
                yield

    def phase_gdn_out(self, l):
        P = self.P
        with ExitStack() as st:
            op0 = self.pool(st, 2, [128, 8, 512], F32, "o0")
            op1 = self.pool(st, 2, [128, 8, 512], F32, "o1")
            zp = self.pool(st, 2, [128, 8, 512], BF16, "z")
            yp = self.pool(st, 2, [128, 8, 512], BF16, "yb")
            sqp = self.pool(st, 2, [128, 512], BF16, "sq")
            rsp = self.pool(st, 2, [128, 512], F32, "rstd")
            for (t0, n) in TB:
                o0, r0 = op0()
                o1, r1 = op1()
                z, rz = zp()
                y, ry = yp()
                P.dma("sp", o0[:, :, :n], self.S_OT[0][:, t0:t0 + n].rearrange("(h p) n -> p h n", p=128), writes=[r0])
                P.dma("sp", o1[:, :, :n], self.S_OT[1][:, t0:t0 + n].rearrange("(h p) n -> p h n", p=128), writes=[r1])
                P.dma("sp", z[:, :, :n], self.S_Z[:, t0:t0 + n].rearrange("(h p) n -> p h n", p=128), writes=[rz])
                P.op("pool", lambda h: h.tensor_tensor(o0[:, :, :n], o0[:, :, :n], o1[:, :, :n], ALU.add), reads=[r0, r1], writes=[r0])
                for hh in range(8):
                    rstd, r_rs = rsp()
                    self.fm_rstd([(o0[:, hh, :n], r0)], n, 1.0 / 128, 1, sqp, rstd[:, :n], r_rs)
                    P.op("dve", lambda h: h.tensor_tensor(o0[:, hh, :n], o0[:, hh, :n], rstd[:, :n], ALU.mult), reads=[r0, r_rs], writes=[r0])
                    P.op("dve", lambda h: h.tensor_tensor(o0[:, hh, :n], o0[:, hh, :n], z[:, hh, :n], ALU.mult), reads=[r0, rz], writes=[r0])
                    P.op("act", lambda h: h.activation(y[:, hh, :n], o0[:, hh, :n], AF.Identity, scale=self.vec[:, V_ONORM:V_ONORM + 1]),
                         reads=[r0, self.r_vec], writes=[ry])
                P.dma("sp", self.S_YB[:, t0:t0 + n].rearrange("(h p) n -> p h n", p=128), y[:, :, :n], reads=[ry])
            P.barrier()

    def phase_merge1(self, l):
        P = self.P
        with ExitStack() as st:
            yps = [self.pool(st, 2, [128, 8, 512], BF16, f"y{i}") for i in range(3)]
            wp = self.pool(st, 2, [128, 24, 512], BF16, "wbr")
            gp = self.pool(st, 2, [128, 3, 4, 512], BF16, "gt")
            accp = self.pool(st, 3, [128, 512], F32, "acc")
            tp = self.pool(st, 3, [128, 512], F32, "tt")
            bfp = self.pool(st, 3, [128, 512], BF16, "stb")
            srcs = (self.S_YA, self.S_YB, self.S_YC)
            for (t0, n) in TB:
                ys = []
                for i in range(3):
                    y, ry = yps[i]()
                    P.dma("sp", y[:, :, :n], srcs[i][:, t0:t0 + n].rearrange("(k p) n -> p k n", p=128), writes=[ry])
                    ys.append((y, ry))
                for cg in range(4):
                    w, rw = wp()
                    self.wload("sp", w[:], "wbr", l, list(range(24)), cg * 512, (cg + 1) * 512, rw)
                    gt, rg = gp()
                    for i in range(3):
                        P.dma("sp", gt[:, i, :, :n], self.S_GT[i * 2048 + cg * 512:i * 2048 + (cg + 1) * 512, t0:t0 + n].rearrange("(j p) n -> p j n", p=128), writes=[rg])
                    for jb in range(4):
                        cb = cg * 4 + jb
                        pss = []
                        for i in range(3):
                            ps, rps = self.bank()
                            y, ry = ys[i]
                            for k in range(8):
                                P.op("pe", lambda h: h.matmul(ps[:, :n], w[:, i * 8 + k, jb * 128:(jb + 1) * 128], y[:, k, :n], start=(k == 0), stop=(k == 7)),
                                     reads=[rw, ry], writes=[rps], inc=(k == 7))
                            pss.append((ps, rps))
                        acc, racc = accp()
                        P.op("dve", lambda h: h.tensor_tensor(acc[:, :n], pss[0][0][:, :n], gt[:, 0, jb, :n], ALU.mult), reads=[pss[0][1], rg], writes=[racc])
                        t1, rt1 = tp()
                        P.op("dve", lambda h: h.tensor_tensor(t1[:, :n], pss[1][0][:, :n], gt[:, 1, jb, :n], ALU.mult), reads=[pss[1][1], rg], writes=[rt1])
                        P.op("pool", lambda h: h.tensor_tensor(acc[:, :n], acc[:, :n], t1[:, :n], ALU.add), reads=[rt1], writes=[racc])
                        t2, rt2 = tp()
                        P.op("dve", lambda h: h.tensor_tensor(t2[:, :n], pss[2][0][:, :n], gt[:, 2, jb, :n], ALU.mult), reads=[pss[2][1], rg], writes=[rt2])
                        o, ro = bfp()
                        P.op("pool", lambda h: h.tensor_tensor(o[:, :n], acc[:, :n], t2[:, :n], ALU.add), reads=[racc, rt2], writes=[ro])
                        P.dma("sp", self.S_YS[cb * 128:(cb + 1) * 128, t0:t0 + n], o[:, :n], reads=[ro])
            P.barrier()

    def phase_merge2(self, l, xsrc):
        P = self.P
        mv = self.mv
        with ExitStack() as st:
            ysp = self.pool(st, 1, [128, 16, 512], BF16, "ys")
            xp = self.pool(st, 1, [128, 16, 512], F32, "x")
            y2p = self.pool(st, 1, [128, 16, 512], F32, "y2")
            h2p = self.pool(st, 1, [128, 16, 512], BF16, "h2")
            wp = self.pool(st, 2, [128, 16, 512], BF16, "wout")
            sqp = self.pool(st, 2, [128, 512], BF16, "sq")
            rsp = self.pool(st, 2, [128, 512], F32, "rstd")
            for (t0, n) in TB:
                v = 1 if t0 < NCTX else 0
                ysb, rys = ysp()
                x, rx = xp()
                y2, ry2 = y2p()
                P.dma("sp", ysb[:, :, :n], self.S_YS[:, t0:t0 + n].rearrange("(k p) n -> p k n", p=128), writes=[rys])
                P.dma("sp", x[:, :, :n], xsrc[:, t0:t0 + n].rearrange("(k p) n -> p k n", p=128), writes=[rx])
                for cg in range(4):
                    w, rw = wp()
                    self.wload("sp", w[:], "wout", l, list(range(16)), cg * 512, (cg + 1) * 512, rw)
                    for jb in range(4):
                        cb = cg * 4 + jb
                        ps, rps = self.bank()
                        for k in range(16):
                            P.op("pe", lambda h: h.matmul(ps[:, :n], w[:, k, jb * 128:(jb + 1) * 128], ysb[:, k, :n], start=(k == 0), stop=(k == 15)),
                                 reads=[rw, rys], writes=[rps], inc=(k == 15))
                        if cb % 2 == 0:
                            P.op("act", lambda h: h.activation(y2[:, cb, :n], ps[:, :n], AF.Copy), reads=[rps], writes=[ry2])
                        else:
                            P.op("dve", lambda h: h.tensor_copy(y2[:, cb, :n], ps[:, :n]), reads=[rps], writes=[ry2])
                rstd, r_rs = rsp()
                self.fm_rstd([(y2[:, cb, :n], ry2) for cb in range(16)], n, 1.0 / D, 1, sqp, rstd[:, :n], r_rs)
                for cb in range(16):
                    P.op("dve", lambda h: h.tensor_tensor(y2[:, cb, :n], y2[:, cb, :n], rstd[:, :n], ALU.mult), reads=[ry2, r_rs], writes=[ry2])
                    P.op("dve", lambda h: h.scalar_tensor_tensor(x[:, cb, :n], y2[:, cb, :n], mv[:, 2, cb, v:v + 1], x[:, cb, :n], ALU.mult, ALU.add),
                         reads=[ry2, self.r_mod], writes=[rx])
                P.dma("sp", self.xout[:, t0:t0 + n].rearrange("(k p) n -> p k n", p=128), x[:, :, :n], reads=[rx])
                rstd2, r_rs2 = rsp()
                self.fm_rstd([(x[:, cb, :n], rx) for cb in range(16)], n, 1.0 / D, 1, sqp, rstd2[:, :n], r_rs2)
                h2, rh2 = h2p()
                for cb in range(16):
                    P.op("dve", lambda h: h.tensor_tensor(y2[:, cb, :n], x[:, cb, :n], rstd2[:, :n], ALU.mult), reads=[rx, r_rs2], writes=[ry2])
                    P.op("act", lambda h: h.activation(h2[:, cb, :n], y2[:, cb, :n], AF.Identity, bias=mv[:, 4, cb, v:v + 1], scale=mv[:, 3, cb, v:v + 1]),
                         reads=[ry2, self.r_mod], writes=[rh2])
                P.dma("sp", self.S_H2[:, t0:t0 + n].rearrange("(k p) n -> p k n", p=128), h2[:, :, :n], reads=[rh2])
            P.barrier()

    def phase_ffn1(self, l):
        P = self.P
        for blocks in SCS:
            sc0 = blocks[0][0]
            ntok = sum(b[1] for b in blocks)
            with ExitStack() as st:
                H, r_H = self.sb(st, [128, 16, 2304], BF16, "H2")
                for (t0, n) in blocks:
                    P.dma("sp", H[:, :, t0 - sc0:t0 - sc0 + n], self.S_H2[:, t0:t0 + n].rearrange("(k p) n -> p k n", p=128), writes=[r_H])
                wgp = self.pool(st, 2, [128, 16, 512], BF16, "wg")
                wup = self.pool(st, 2, [128, 16, 512], BF16, "wu")
                sgp = self.pool(st, 3, [128, 512], F32, "sg")
                bfp = self.pool(st, 3, [128, 512], BF16, "stb")
                for hg in range(11):
                    wg, rwg = wgp()
                    wu, rwu = wup()
                    self.wload("sp", wg[:], "wg", l, list(range(16)), hg * 512, (hg + 1) * 512, rwg)
                    self.wload("sp", wu[:], "wu", l, list(range(16)), hg * 512, (hg + 1) * 512, rwu)
                    for jb in range(4):
                        j = hg * 4 + jb
                        for (t0, n) in blocks:
                            off = t0 - sc0
                            pg, rpg = self.bank()
                            for k in range(16):
                                P.op("pe", lambda h: h.matmul(pg[:, :n], wg[:, k, jb * 128:(jb + 1) * 128], H[:, k, off:off + n], start=(k == 0), stop=(k == 15)),
                                     reads=[rwg, r_H], writes=[rpg], inc=(k == 15))
                            pu, rpu = self.bank()
                            for k in range(16):
                                P.op("pe", lambda h: h.matmul(pu[:, :n], wu[:, k, jb * 128:(jb + 1) * 128], H[:, k, off:off + n], start=(k == 0), stop=(k == 15)),
                                     reads=[rwu, r_H], writes=[rpu], inc=(k == 15))
                            sg, rsg = sgp()
                            P.op("act", lambda h: h.activation(sg[:, :n], pg[:, :n], AF.Silu), reads=[rpg], writes=[rsg])
                            o, ro = bfp()
                            P.op("dve", lambda h: h.tensor_tensor(o[:, :n], sg[:, :n], pu[:, :n], ALU.mult), reads=[rsg, rpu], writes=[ro])
                            P.dma("sp", self.S_ACT[j * 128:(j + 1) * 128, t0:t0 + n], o[:, :n], reads=[ro])
                P.barrier()

    def phase_ffn2(self, l):
        P = self.P
        with ExitStack() as st:
            ap_ = self.pool(st, 1, [128, FKC, 512], BF16, "act")
            wp = self.pool(st, 2, [128, FKC, 512], BF16, "wd")
            fp_ = self.pool(st, 4, [128, 512], F32, "stf")
            ev = 0
            for (t0, n) in TB:
                a, ra = ap_()
                P.dma("sp", a[:, 0:22, :n], self.S_ACT[0:22 * 128, t0:t0 + n].rearrange("(k p) n -> p k n", p=128), writes=[ra])
                P.dma("sp", a[:, 22:44, :n], self.S_ACT[22 * 128:44 * 128, t0:t0 + n].rearrange("(k p) n -> p k n", p=128), writes=[ra])
                for cg in range(4):
                    w, rw = wp()
                    self.wload("sp", w[:, 0:22, :], "wd", l, list(range(22)), cg * 512, (cg + 1) * 512, rw)
                    self.wload("sp", w[:, 22:44, :], "wd", l, list(range(22, 44)), cg * 512, (cg + 1) * 512, rw)
                    for jb in range(4):
                        cb = cg * 4 + jb
                        ps, rps = self.bank()
                        for k in range(FKC):
                            P.op("pe", lambda h: h.matmul(ps[:, :n], w[:, k, jb * 128:(jb + 1) * 128], a[:, k, :n], start=(k == 0), stop=(k == FKC - 1)),
                                 reads=[rw, ra], writes=[rps], inc=(k == FKC - 1))
                        o, ro = fp_()
                        ev += 1
                        if ev % 2 == 0:
                            P.op("act", lambda h: h.activation(o[:, :n], ps[:, :n], AF.Copy), reads=[rps], writes=[ro])
                        else:
                            P.op("dve", lambda h: h.tensor_copy(o[:, :n], ps[:, :n]), reads=[rps], writes=[ro])
                        P.dma("sp", self.S_F[cb * 128:(cb + 1) * 128, t0:t0 + n], o[:, :n], reads=[ro])
            P.barrier()

    def phase_ffn3(self, l):
        P = self.P
        mv = self.mv
        with ExitStack() as st:
            fp_ = self.pool(st, 2, [128, 16, 512], F32, "f")
            xp = self.pool(st, 2, [128, 16, 512], F32, "x")
            sqp = self.pool(st, 2, [128, 512], BF16, "sq")
            rsp = self.pool(st, 2, [128, 512], F32, "rstd")
            for (t0, n) in TB:
                v = 1 if t0 < NCTX else 0
                f, rf = fp_()
                x, rx = xp()
                P.dma("sp", f[:, :, :n], self.S_F[:, t0:t0 + n].rearrange("(k p) n -> p k n", p=128), writes=[rf])
                P.dma("sp", x[:, :, :n], self.xout[:, t0:t0 + n].rearrange("(k p) n -> p k n", p=128), writes=[rx])
                rstd, r_rs = rsp()
                self.fm_rstd([(f[:, cb, :n], rf) for cb in range(16)], n, 1.0 / D, 1, sqp, rstd[:, :n], r_rs)
                for cb in range(16):
                    P.op("dve", lambda h: h.tensor_tensor(f[:, cb, :n], f[:, cb, :n], rstd[:, :n], ALU.mult), reads=[rf, r_rs], writes=[rf])
                    P.op("dve", lambda h: h.scalar_tensor_tensor(x[:, cb, :n], f[:, cb, :n], mv[:, 5, cb, v:v + 1], x[:, cb, :n], ALU.mult, ALU.add),
                         reads=[rf, self.r_mod], writes=[rx])
                P.dma("sp", self.xout[:, t0:t0 + n].rearrange("(k p) n -> p k n", p=128), x[:, :, :n], reads=[rx])
            P.barrier()

    def build(self):
        P = self.P
        first = True
        for li, l in enumerate(self.layers):
            if li == 0:
                self.issue_casts(l)
            if li + 1 < len(self.layers):
                self.issue_casts(self.layers[li + 1])
            xsrc = self.xin if first else self.xout
            first = False
            self.phase_mod(l)
            if self.stop_after == "mod":
                break
            self.phase_p1(l, xsrc)
            self.phase_mla_up(l)
            if self.stop_after == "p1":
                break
            if "attn" not in self.skip:
                self.phase_attn(l)
            if self.stop_after == "attn":
                break
            if "gdn" not in self.skip:
                self.phase_gdn(l)
                import os as _os2
                if not _os2.environ.get('GDN_NOOUT'):
                    self.phase_gdn_out(l)
            if self.stop_after == "gdn":
                break
            self.phase_merge1(l)
            self.phase_merge2(l, xsrc)
            if self.stop_after == "merge":
                break
            self.phase_ffn1(l)
            self.phase_ffn2(l)
            self.phase_ffn3(l)
        P.barrier()
        return self.nc


_CONSTS = None


def _layer_inputs(inp, l, tag):
    d = {}
    d[f"w_ada{tag}"] = np.ascontiguousarray(inp["w_ada"][l])
    d[f"w_in{tag}"] = np.ascontiguousarray(inp["w_in"][l][:, _win_cols()])
    d[f"mqup{tag}"] = np.ascontiguousarray(inp["mla_q_up"][l][:, _mqup_cols()])
    d[f"mkvup{tag}"] = np.ascontiguousarray(inp["mla_kv_up"][l][:, _mkvup_cols()])
    d[f"wbr{tag}"] = np.ascontiguousarray(inp["w_branch"][l].reshape(3072, D))
    d[f"wout{tag}"] = np.ascontiguousarray(inp["w_out"][l])
    d[f"wg{tag}"] = np.ascontiguousarray(inp["w_ffn_gate"][l])
    d[f"wu{tag}"] = np.ascontiguousarray(inp["w_ffn_up"][l])
    d[f"wd{tag}"] = np.ascontiguousarray(inp["w_ffn_down"][l])
    d[f"vec{tag}"] = _pack_vec(inp, l)
    return d


def _core_inputs(inp, b):
    global _CONSTS
    if _CONSTS is None:
        _CONSTS = _consts()
    c128, sel, ctab, stab = _CONSTS
    cT = np.stack([inp["c"][b].reshape(16, 128).T, inp["c_ctx"].reshape(16, 128).T], axis=-1).reshape(128, 32)
    return {"cT": np.ascontiguousarray(cT, dtype=np.float32), "c128": c128, "ctab": ctab, "stab": stab}


FUSED = True
_PROG_CACHE = {}


def _get_prog(layers):
    key = tuple(layers)
    if key not in _PROG_CACHE:
        kb = KB(list(layers))
        _PROG_CACHE[key] = kb.build()
    return _PROG_CACHE[key]


def kernel(**inputs):
    inp = {k: np.asarray(v) for k, v in inputs.items()}
    B = inp["x"].shape[0]
    cores = list(range(B))
    xT = [np.ascontiguousarray(np.concatenate([inp["ctx"][b], inp["x"][b]], axis=0).T.astype(np.float32)) for b in range(B)]
    base = [_core_inputs(inp, b) for b in range(B)]
    if FUSED:
        nc = _get_prog(range(DEPTH))
        wl = {}
        for l in range(DEPTH):
            wl.update(_layer_inputs(inp, l, str(l)))
        in_maps = []
        for b in range(B):
            m = dict(base[b])
            m.update(wl)
            m["xin"] = xT[b]
            in_maps.append(m)
        res = run_bass_kernel_spmd(nc, in_maps, core_ids=cores)
        xT = [np.asarray(res.results[b]["xout"]) for b in range(B)]
    else:
        nc = _get_prog([0])
        for l in range(DEPTH):
            wl = _layer_inputs(inp, l, "0")
            in_maps = []
            for b in range(B):
                m = dict(base[b])
                m.update(wl)
                m["xin"] = xT[b]
                in_maps.append(m)
            res = run_bass_kernel_spmd(nc, in_maps, core_ids=cores)
            xT = [np.ascontiguousarray(np.asarray(res.results[b]["xout"])) for b in range(B)]
    out = np.stack([xT[b][:, NCTX:].T for b in range(B)], axis=0)
    return np.ascontiguousarray(out.astype(np.float32))
```

```python
import math
from contextlib import ExitStack

import numpy as np
import ml_dtypes

import concourse.bass as bass
import concourse.mybir as mybir
from concourse.bass_utils import run_bass_kernel_spmd

F32 = mybir.dt.float32
BF16 = mybir.dt.bfloat16
AF = mybir.ActivationFunctionType
ALU = mybir.AluOpType

D = 2048
KC = 16
T = 4352
NCTX = 256
FF = 5632
FKC = 44
DEPTH = 4
NT = 34
TB = [(0, 256)] + [(256 + 512 * i, 512) for i in range(8)]
SCS = [TB[0:5], TB[5:9]]
WIN_EXT = 16544
EPOCH = 30000


class Res:
    __slots__ = ("name", "lw", "rd")

    def __init__(self, name=""):
        self.name = name
        self.lw = None
        self.rd = {}


class Prog:
    ENGS = ("pe", "act", "dve", "pool", "sp")

    def __init__(self, nc, n_dma_sems=16, n_cast_sems=4):
        self.nc = nc
        self.h = {"pe": nc.tensor, "act": nc.scalar, "dve": nc.vector, "pool": nc.gpsimd, "sp": nc.sync}
        self.sems = {}
        self.cur = {}
        self.prev_final = {}
        self.epoch = {e: 0 for e in self.ENGS}
        self.waited = {e: {} for e in self.ENGS}
        for e in ("pe", "act", "dve", "pool"):
            self._new_epoch(e)
        self.dma_sems = []
        for i in range(n_dma_sems):
            k = ("dma", i)
            self.sems[k] = nc.alloc_semaphore(name=f"dma{i}")
            self.dma_sems.append([k, 0])
        self.cast_sems = []
        for i in range(n_cast_sems):
            k = ("cast", i)
            self.sems[k] = nc.alloc_semaphore(name=f"cast{i}")
            self.cast_sems.append([k, 0])
        self.dma_rr = 0
        self.cast_rr = 0
        self.n_instr = 0
        self.n_wait = 0
        self.pending = {e: ([], []) for e in self.ENGS}

    def _new_epoch(self, e):
        if e in self.cur:
            if not hasattr(self, "prev_final"):
                self.prev_final = {}
            self.prev_final[e] = (self.cur[e][0], self.cur[e][1])
        k = (e, self.epoch[e])
        self.epoch[e] += 1
        self.sems[k] = self.nc.alloc_semaphore(name=f"s_{e}_{k[1]}")
        self.cur[e] = [k, 0]

    def _need(self, eng, tok, waits):
        if tok is None:
            return
        k, v = tok
        if eng == "pe" and k[0] == "pe":
            return
        if self.waited[eng].get(k, 0) >= v:
            return
        if waits.get(k, 0) < v:
            waits[k] = v

    def _deps(self, eng, reads, writes):
        waits = {}
        for r in reads:
            self._need(eng, r.lw, waits)
        for w in writes:
            self._need(eng, w.lw, waits)
            for k, v in w.rd.items():
                self._need(eng, (k, v), waits)
        return waits

    def _emit_waits(self, eng, waits):
        for k, v in waits.items():
            if getattr(self, "trace", False):
                print("   WAIT", eng, k, v)
            self.h[eng].wait_ge(self.sems[k], v)
            self.waited[eng][k] = v
            self.n_wait += 1

    def _mark(self, tok, reads, writes):
        k, v = tok
        for r in reads:
            if r.rd.get(k, 0) < v:
                r.rd[k] = v
        for w in writes:
            w.lw = tok
            w.rd = {}

    def op(self, eng, fn, reads=(), writes=(), inc=True):
        waits = self._deps(eng, reads, writes)
        self._emit_waits(eng, waits)
        if not inc:
            fn(self.h[eng])
            self.pending[eng][0].extend(reads)
            self.pending[eng][1].extend(writes)
            self.n_instr += 1
            return None
        if self.pending[eng][0] or self.pending[eng][1]:
            reads = list(reads) + self.pending[eng][0]
            writes = list(writes) + self.pending[eng][1]
            self.pending[eng] = ([], [])
        cur = self.cur[eng]
        if cur[1] >= EPOCH:
            self._new_epoch(eng)
            cur = self.cur[eng]
        cur[1] += 1
        tok = (cur[0], cur[1])
        if getattr(self, "trace", False):
            print("OP", eng, tok)
        fn(self.h[eng]).then_inc(self.sems[cur[0]], 1)
        self._mark(tok, reads, writes)
        self.n_instr += 1
        return tok

    def dma(self, q, out, in_, reads=(), writes=(), cast=False, **kw):
        waits = self._deps(q, reads, writes)
        if cast:
            slot = self.cast_sems[self.cast_rr]
            self.cast_rr = (self.cast_rr + 1) % len(self.cast_sems)
        else:
            slot = self.dma_sems[self.dma_rr]
            self.dma_rr = (self.dma_rr + 1) % len(self.dma_sems)
        k = slot[0]
        if slot[1] > 0:
            self._need(q, (k, slot[1]), waits)
        self._emit_waits(q, waits)
        slot[1] += 16
        tok = (k, slot[1])
        self.h[q].dma_start(out=out, in_=in_, **kw).then_inc(self.sems[k], 16)
        self._mark(tok, reads, writes)
        self.n_instr += 1
        return tok

    def barrier(self):
        toks = [(c[0], c[1]) for c in self.cur.values() if c[1] > 0] + [(s[0], s[1]) for s in self.dma_sems if s[1] > 0]
        toks += list(self.prev_final.values())
        for e in self.ENGS:
            waits = {}
            for t in toks:
                self._need(e, t, waits)
            self._emit_waits(e, waits)


P64 = np.concatenate([np.arange(32, 64), np.arange(0, 32)])
P128 = np.concatenate([P64, 64 + P64])


def _win_cols():
    dq, dk, dv, gq, gz, ga, mq, mkv, kr, gates = 0, 1024, 2048, 3072, 6144, 7168, 7200, 7712, 8224, 8288
    cols = []
    for base in (dq, dk):
        for h in range(8):
            cols.append(base + h * 128 + np.arange(128))
            cols.append(base + h * 128 + P128)
    cols.append(gq + np.arange(3072))
    cols.append(gz + np.arange(1024))
    cols.append(mq + np.arange(512))
    cols.append(mkv + np.arange(512))
    cols.append(gates + np.arange(6144))
    cols.append(kr + np.arange(64))
    cols.append(kr + P64)
    cols.append(dv + np.arange(1024))
    cols.append(ga + np.arange(32))
    c = np.concatenate(cols)
    assert c.shape[0] == WIN_EXT
    return c


def _mqup_cols():
    cols = []
    for h in range(8):
        cols.append(h * 192 + np.arange(128))
    for j in range(4):
        a = np.concatenate([(2 * j) * 192 + 128 + np.arange(64), (2 * j + 1) * 192 + 128 + np.arange(64)])
        b = np.concatenate([(2 * j) * 192 + 128 + P64, (2 * j + 1) * 192 + 128 + P64])
        cols.append(a)
        cols.append(b)
    return np.concatenate(cols)


def _mkvup_cols():
    cols = []
    for h in range(8):
        cols.append(h * 256 + np.arange(128))
    for h in range(8):
        cols.append(h * 256 + 128 + np.arange(128))
    return np.concatenate(cols)


def _fm(v, kc):
    return np.ascontiguousarray(v.reshape(kc, 128).T)


NV = 96 + 64 + 256 + 1 + 1 + 4 + 4 + 72 + 544 + 544 + 2
V_LAMI = 1586
V_BADA, V_G, V_LAM, V_SUBLN, V_ONORM, V_MQN, V_MKVN, V_CONV, V_ALOG, V_DTB = 0, 96, 160, 416, 417, 418, 422, 426, 498, 1042


def _pack_vec(inp, l):
    v = np.zeros((128, NV), np.float32)
    v[:, V_BADA:V_BADA + 96] = _fm(inp["b_ada"][l], 96)
    for i, nm in enumerate(("g_pre_mix", "g_post_mix", "g_pre_ffn", "g_post_ffn")):
        v[:, V_G + 16 * i:V_G + 16 * (i + 1)] = _fm(inp[nm][l], 16)
    lam = np.concatenate([inp["diff_lam_q1"][l], inp["diff_lam_k1"][l], inp["diff_lam_q2"][l], inp["diff_lam_k2"][l]])
    v[:, V_LAM:V_LAM + 256] = np.broadcast_to(lam[None, :], (128, 256))
    v[:, V_SUBLN] = inp["diff_subln"][l]
    v[:, V_ONORM] = inp["gdn_out_norm"][l]
    v[:, V_MQN:V_MQN + 4] = _fm(inp["mla_q_norm"][l], 4)
    v[:, V_MKVN:V_MKVN + 4] = _fm(inp["mla_kv_norm"][l], 4)
    cw = inp["gdn_conv"][l]
    v[:, V_CONV:V_CONV + 72] = np.ascontiguousarray(cw.reshape(3, 24, 128).transpose(2, 1, 0)).reshape(128, 72)
    al = inp["gdn_a_log"][l].reshape(16)
    db = inp["gdn_dt_bias"][l].reshape(16)
    v[:, V_ALOG:V_ALOG + 544] = np.broadcast_to(np.tile(al, NT)[None, :], (128, 544))
    v[:, V_DTB:V_DTB + 544] = np.broadcast_to(np.tile(db, NT)[None, :], (128, 544))
    lam_init = 0.8 - 0.6 * math.exp(-0.3 * l)
    v[:, V_LAMI] = lam_init
    v[:, V_LAMI + 1] = 1.0 - lam_init
    return v


def _consts():
    i = np.arange(128)
    ident = np.eye(128, dtype=np.float32)
    tri_f = (i[:, None] <= i[None, :]).astype(np.float32)
    tri_b = (i[:, None] >= i[None, :]).astype(np.float32)
    msf = -(i[None, :] > i[:, None]).astype(np.float32)
    mif = (i[None, :] >= i[:, None]).astype(np.float32)
    msb = -(i[None, :] < i[:, None]).astype(np.float32)
    mib = (i[None, :] <= i[:, None]).astype(np.float32)
    def bd(sz):
        return ((i[:, None] // sz) == (i[None, :] // sz)).astype(np.float32)
    bd16, o32, o64, o128 = bd(16), bd(32) - bd(16), bd(64) - bd(32), 1.0 - bd(64)
    c128 = np.concatenate([ident, tri_f, tri_b, msf, mif, msb, mib, np.ones((128, 128), np.float32), bd16, o32, o64, o128], axis=1)
    sel = np.zeros((128, 32, 128), np.float32)
    for r in range(32):
        sel[r, r, :] = 1.0
    rows = 4096 // 64
    row_id = np.repeat(np.arange(rows), 64).astype(np.float32)
    col_id = np.tile(np.arange(64), rows).astype(np.float32)
    inv_freq = (np.float32(10000.0) ** (-np.arange(16, dtype=np.float32) / np.float32(16))).astype(np.float32)
    ang = np.concatenate([row_id[:, None] * inv_freq, col_id[:, None] * inv_freq], axis=-1).astype(np.float32)
    cos = np.cos(ang).astype(np.float32)
    sin = np.sin(ang).astype(np.float32)
    ctab = np.ones((128, T), np.float32)
    stab = np.zeros((128, T), np.float32)
    for r in range(128):
        ctab[r, NCTX:] = cos[:, r % 32]
        sgn = -1.0 if (r % 64) < 32 else 1.0
        stab[r, NCTX:] = sgn * sin[:, r % 32]
    return c128, sel.reshape(128, 32 * 128), ctab, stab


C_ID, C_TRIF, C_TRIB, C_MSF, C_MIF, C_MSB, C_MIB, C_ONE, C_BD16, C_O32, C_O64, C_O128 = [128 * i for i in range(12)]


class KB:
    def __init__(self, layers, stop_after=None, dbg=(), skip=()):
        self.skip = set(skip)
        self.layers = list(layers)
        self.stop_after = stop_after
        self.dbg = set(dbg)
        nc = self.nc = bass.Bass("TRN2", target_bir_lowering=False)
        self.P = Prog(nc)
        self._uid = 0
        self.xin = self.dram("xin", [D, T], F32, "ExternalInput")
        self.xout = self.dram("xout", [D, T], F32, "ExternalOutput")
        self.cT_d = self.dram("cT", [128, 32], F32, "ExternalInput")
        self.c128_d = self.dram("c128", [128, 1536], F32, "ExternalInput")
        self.ctab_d = self.dram("ctab", [128, T], F32, "ExternalInput")
        self.stab_d = self.dram("stab", [128, T], F32, "ExternalInput")
        self.W = {}
        wshapes = {"w_ada": [D, 6 * D], "w_in": [D, WIN_EXT], "mqup": [512, 2048], "mkvup": [512, 2048],
                   "wbr": [3072, D], "wout": [D, D], "wg": [D, FF], "wu": [D, FF], "wd": [FF, D]}
        for l in self.layers:
            for nm, shp in wshapes.items():
                self.W[(nm, l)] = self.dram(f"{nm}{l}", shp, F32, "ExternalInput")
            self.W[("vec", l)] = self.dram(f"vec{l}", [128, NV], F32, "ExternalInput")
        self.WB = {}
        self.rWB = {}
        for l in self.layers:
            for nm, shp in wshapes.items():
                if nm == "w_ada":
                    continue
                self.WB[(nm, l)] = self.dram(f"{nm}b{l}", shp, BF16)
                self.rWB[(nm, l)] = [Res() for _ in range(shp[0] // 128)]
        sd = self.sdram
        self.S_QD = sd("S_QD", [1024, T], BF16)
        self.S_KD = sd("S_KD", [1024, T], BF16)
        self.S_VD = sd("S_VD", [T, 1024], BF16)
        self.S_G = sd("S_G", [3072, T], F32)
        self.S_Z = sd("S_Z", [1024, T], BF16)
        self.S_AB = sd("S_AB", [T, 32], F32)
        self.S_MQ = sd("S_MQ", [512, T], BF16)
        self.S_MKV = sd("S_MKV", [512, T], BF16)
        self.S_KR = sd("S_KR", [64, T], BF16)
        self.S_GT = sd("S_GT", [6144, T], BF16)
        self.S_MQN = sd("S_MQN", [1024, T], BF16)
        self.S_MQR = sd("S_MQR", [512, T], BF16)
        self.S_MKN = sd("S_MKN", [1024, T], BF16)
        self.S_VM = sd("S_VM", [T, 1024], BF16)
        self.S_YA = sd("S_YA", [1024, T], BF16)
        self.S_YB = sd("S_YB", [1024, T], BF16)
        self.S_YC = sd("S_YC", [1024, T], BF16)
        self.S_OT = [sd("S_OT0", [1024, T], F32), sd("S_OT1", [1024, T], F32)]
        self.S_YS = sd("S_YS", [D, T], BF16)
        self.S_H2 = sd("S_H2", [D, T], BF16)
        self.S_ACT = sd("S_ACT", [FF, T], BF16)
        self.S_F = sd("S_F", [D, T], F32)
        self.ps = []
        self.rps = []
        for i in range(8):
            t = nc.alloc_psum_tensor(f"psb{i}", [128, 512], F32)
            self.ps.append(t)
            self.rps.append(Res(f"ps{i}"))
        self.bank_rr = 0
        self.c128 = nc.alloc_sbuf_tensor("c128s", [128, 1536], F32)
        self.ones_bf = nc.alloc_sbuf_tensor("ones_bf", [128, 128], BF16)
        self.cols = nc.alloc_sbuf_tensor("ccols", [128, 8], F32)
        self.r_const = Res("const")
        self.vec = nc.alloc_sbuf_tensor("vecs", [128, NV], F32)
        self.r_vec = Res("vec")
        self.mod = nc.alloc_sbuf_tensor("mods", [128, 96, 2], F32)
        self.r_mod = Res("mod")
        self.mv = nc.alloc_sbuf_tensor("mvs", [128, 6, 16, 2], F32)
        self.lamc = nc.alloc_sbuf_tensor("lamc", [128, 4], F32)
        P = self.P
        P.dma("sp", self.c128[:], self.c128_d, writes=[self.r_const])
        P.op("dve", lambda h: h.memset(self.ones_bf[:], 1.0), writes=[self.r_const])
        for i, val in enumerate((1.0, 1e-6, 1e-5, 0.0)):
            P.op("dve", lambda h: h.memset(self.cols[:, i:i + 1], val), writes=[self.r_const])

    def dram(self, name, shape, dt, kind="Internal"):
        return self.nc.dram_tensor(name, shape, dt, kind=kind).ap()

    def sdram(self, name, shape, dt):
        kind = "ExternalOutput" if name in self.dbg else "Internal"
        return self.nc.dram_tensor(name, shape, dt, kind=kind).ap()

    def sb(self, st, shape, dt, name=None):
        self._uid += 1
        t = st.enter_context(self.nc.sbuf_tensor(f"{name or 't'}_{self._uid}", shape, dt))
        return t, Res(name or "t")

    def pool(self, st, n, shape, dt, name=None):
        items = [self.sb(st, shape, dt, name) for _ in range(n)]
        state = {"i": 0}

        def nxt():
            it = items[state["i"] % n]
            state["i"] += 1
            return it
        return nxt

    def bank(self):
        i = self.bank_rr
        self.bank_rr = (self.bank_rr + 1) % 8
        return self.ps[i], self.rps[i]

    def ident(self):
        return self.c128[:, C_ID:C_ID + 128]

    def onesf(self):
        return self.c128[:, C_ONE:C_ONE + 128]

    def issue_casts(self, l):
        for nm in ("w_in", "mqup", "mkvup", "wbr", "wout", "wg", "wu", "wd"):
            src = self.W[(nm, l)]
            dst = self.WB[(nm, l)]
            rows = src.shape[0]
            for rb in range(rows // 128):
                self.P.dma("pool", dst[rb * 128:(rb + 1) * 128, :], src[rb * 128:(rb + 1) * 128, :],
                           writes=[self.rWB[(nm, l)][rb]], cast=True)

    def wload(self, q, dst_ap, nm, l, kcs, c0, c1, res):
        src = self.WB[(nm, l)]
        k0, k1 = kcs[0], kcs[-1] + 1
        self.P.dma(q, dst_ap, src[k0 * 128:k1 * 128, c0:c1].rearrange("(k p) n -> p k n", p=128),
                   reads=self.rWB[(nm, l)][k0:k1], writes=[res])

    def fm_rstd(self, chunks, n, scale, eps_idx, sqpool, out_ap, r_out, bank=None):
        P = self.P
        ps, rps = self.bank() if bank is None else (self.ps[bank], self.rps[bank])
        C = len(chunks)
        for c, (ap, r) in enumerate(chunks):
            sq, rsq = sqpool()
            P.op("act", lambda h: h.activation(sq[:, :n], ap, AF.Square), reads=[r], writes=[rsq])
            P.op("pe", lambda h: h.matmul(ps[:, :n], self.ones_bf[:], sq[:, :n], start=(c == 0), stop=(c == C - 1)),
                 reads=[rsq, self.r_const], writes=[rps], inc=(c == C - 1))
        P.op("act", lambda h: h.activation(out_ap, ps[:, :n], AF.Sqrt, bias=self.cols[:, eps_idx:eps_idx + 1], scale=scale),
             reads=[rps, self.r_const], writes=[r_out])
        P.op("dve", lambda h: h.reciprocal(out_ap, out_ap), reads=[r_out], writes=[r_out])

    def phase_mod(self, l):
        P = self.P
        with ExitStack() as st:
            sc, r_sc = self.sb(st, [128, 16, 2], F32, "sc")
            wp = self.pool(st, 2, [128, 16, 512], F32, "wada")
            P.dma("sp", self.vec[:], self.W[("vec", l)], writes=[self.r_vec])
            P.dma("sp", sc[:], self.cT_d.rearrange("p (k v) -> p k v", v=2), writes=[r_sc])
            P.op("act", lambda h: h.activation(sc[:], sc[:], AF.Silu), reads=[r_sc], writes=[r_sc])
            wsrc = self.W[("w_ada", l)]
            for jg in range(24):
                wt, rwt = wp()
                P.dma("sp", wt[:], wsrc[:, jg * 512:(jg + 1) * 512].rearrange("(k p) n -> p k n", p=128), writes=[rwt])
                for jb in range(4):
                    j = jg * 4 + jb
                    ps, rps = self.bank()
                    for k in range(16):
                        P.op("pe", lambda h: h.matmul(ps[:, 0:2], wt[:, k, jb * 128:(jb + 1) * 128], sc[:, k, :],
                                                      start=(k == 0), stop=(k == 15)),
                             reads=[rwt, r_sc], writes=[rps], inc=(k == 15))
                    P.op("dve", lambda h: h.tensor_single_scalar(self.mod[:, j, :], ps[:, 0:2], self.vec[:, V_BADA + j:V_BADA + j + 1], ALU.add),
                         reads=[rps, self.r_vec], writes=[self.r_mod])
            mv, mod, vec = self.mv, self.mod, self.vec
            for v in range(2):
                for (i_out, sc_off, g_off) in ((0, 16, 0), (3, 64, 2)):
                    P.op("dve", lambda h: h.tensor_single_scalar(mv[:, i_out, :, v], mod[:, sc_off:sc_off + 16, v], 1.0, ALU.add),
                         reads=[self.r_mod], writes=[self.r_mod])
                    P.op("dve", lambda h: h.tensor_tensor(mv[:, i_out, :, v], mv[:, i_out, :, v], vec[:, V_G + 16 * g_off:V_G + 16 * g_off + 16], ALU.mult),
                         reads=[self.r_mod, self.r_vec], writes=[self.r_mod])
                for (i_out, sh_off) in ((1, 0), (4, 48)):
                    P.op("dve", lambda h: h.tensor_copy(mv[:, i_out, :, v], mod[:, sh_off:sh_off + 16, v]),
                         reads=[self.r_mod], writes=[self.r_mod])
                for (i_out, gt_off, g_off) in ((2, 32, 1), (5, 80, 3)):
                    P.op("dve", lambda h: h.tensor_tensor(mv[:, i_out, :, v], mod[:, gt_off:gt_off + 16, v], vec[:, V_G + 16 * g_off:V_G + 16 * g_off + 16], ALU.mult),
                         reads=[self.r_mod, self.r_vec], writes=[self.r_mod])
            tmp, r_tmp = self.sb(st, [128, 128], F32, "lamtmp")
            s2, r_s2 = self.sb(st, [128, 2], F32, "lams")
            lamv = vec[:, V_LAM:V_LAM + 256]
            P.op("dve", lambda h: h.tensor_tensor(tmp[:, 0:64], vec[:, V_LAM:V_LAM + 64], vec[:, V_LAM + 64:V_LAM + 128], ALU.mult),
                 reads=[self.r_vec], writes=[r_tmp])
            P.op("dve", lambda h: h.tensor_tensor(tmp[:, 64:128], vec[:, V_LAM + 128:V_LAM + 192], vec[:, V_LAM + 192:V_LAM + 256], ALU.mult),
                 reads=[self.r_vec], writes=[r_tmp])
            P.op("dve", lambda h: h.reduce_sum(s2[:, 0:1], tmp[:, 0:64], mybir.AxisListType.X), reads=[r_tmp], writes=[r_s2])
            P.op("dve", lambda h: h.reduce_sum(s2[:, 1:2], tmp[:, 64:128], mybir.AxisListType.X), reads=[r_tmp], writes=[r_s2])
            P.op("act", lambda h: h.activation(s2[:], s2[:], AF.Exp), reads=[r_s2], writes=[r_s2])
            P.op("dve", lambda h: h.scalar_tensor_tensor(self.lamc[:, 0:1], s2[:, 1:2], vec[:, V_LAMI:V_LAMI + 1], s2[:, 0:1], ALU.subtract, ALU.subtract),
                 reads=[r_s2, self.r_vec], writes=[self.r_mod])
            P.op("dve", lambda h: h.tensor_tensor(self.lamc[:, 1:2], vec[:, V_SUBLN:V_SUBLN + 1], vec[:, V_LAMI + 1:V_LAMI + 2], ALU.mult),
                 reads=[self.r_vec], writes=[self.r_mod])
            P.barrier()

    def rope_epi(self, st_pools, psa, rpa, psb, rpb, m, n, ctab, stab, r_tab, toff, dst_ap):
        P = self.P
        f32p, bfp = st_pools
        t1, r1 = f32p()
        t2, r2 = f32p()
        o, ro = bfp()
        P.op("dve", lambda h: h.tensor_tensor(t1[0:m, :n], psa[0:m, :n], ctab[0:m, toff:toff + n], ALU.mult), reads=[rpa, r_tab], writes=[r1])
        P.op("dve", lambda h: h.tensor_tensor(t2[0:m, :n], psb[0:m, :n], stab[0:m, toff:toff + n], ALU.mult), reads=[rpb, r_tab], writes=[r2])
        P.op("pool", lambda h: h.tensor_tensor(o[0:m, :n], t1[0:m, :n], t2[0:m, :n], ALU.add), reads=[r1, r2], writes=[ro])
        P.dma("pool", dst_ap, o[0:m, :n], reads=[ro])

    def phase_p1(self, l, xsrc):
        P = self.P
        mv = self.mv
        for sci, blocks in enumerate(SCS):
            sc0 = blocks[0][0]
            ntok = sum(b[1] for b in blocks)
            with ExitStack() as st:
                H, r_H = self.sb(st, [128, 16, 2304], BF16, "H")
                ctab, r_tab = self.sb(st, [128, 2304], F32, "ctab")
                stab, _ = self.sb(st, [128, 2304], F32, "stab")
                P.dma("sp", ctab[:, :ntok], self.ctab_d[:, sc0:sc0 + ntok], writes=[r_tab])
                P.dma("sp", stab[:, :ntok], self.stab_d[:, sc0:sc0 + ntok], writes=[r_tab])
                with ExitStack() as st2:
                    xp = self.pool(st2, 1, [128, 16, 512], F32, "xt")
                    sqp = self.pool(st2, 3, [128, 512], BF16, "sq")
                    rsp = self.pool(st2, 2, [128, 512], F32, "rstd")
                    for (t0, n) in blocks:
                        v = 1 if t0 < NCTX else 0
                        off = t0 - sc0
                        xt, r_xt = xp()
                        P.dma("sp", xt[:, :, :n], xsrc[:, t0:t0 + n].rearrange("(k p) n -> p k n", p=128), writes=[r_xt])
                        rstd, r_rs = rsp()
                        self.fm_rstd([(xt[:, k, :n], r_xt) for k in range(16)], n, 1.0 / D, 1, sqp, rstd[:, :n], r_rs)
                        for k in range(16):
                            P.op("dve", lambda h: h.tensor_tensor(xt[:, k, :n], xt[:, k, :n], rstd[:, :n], ALU.mult), reads=[r_xt, r_rs], writes=[r_xt])
                            P.op("act", lambda h: h.activation(H[:, k, off:off + n], xt[:, k, :n], AF.Identity,
                                                               bias=mv[:, 1, k, v:v + 1], scale=mv[:, 0, k, v:v + 1]),
                                 reads=[r_xt, self.r_mod], writes=[r_H])
                    P.barrier()
                wp = self.pool(st, 2, [128, 16, 512], BF16, "win")
                f32p = self.pool(st, 4, [128, 512], F32, "stf")
                bfp = self.pool(st, 4, [128, 512], BF16, "stb")
                mqs, r_mqs = self.sb(st, [128, 4, 512], F32, "mqs")
                sqp = self.pool(st, 3, [128, 512], BF16, "sq")
                rsp = self.pool(st, 2, [128, 512], F32, "rstd")
                evi = [0]

                def mm_fm(ps, rps, wt, rwt, c0, m, off, n):
                    for k in range(16):
                        P.op("pe", lambda h: h.matmul(ps[0:m, :n], wt[:, k, c0:c0 + m], H[:, k, off:off + n], start=(k == 0), stop=(k == 15)),
                             reads=[rwt, r_H], writes=[rps], inc=(k == 15))

                for g in range(31):
                    wt, rwt = wp()
                    ncol = 512 if g < 30 else 128
                    self.wload("sp", wt[:, :, :ncol], "w_in", l, list(range(16)), g * 512, g * 512 + ncol, rwt)
                    for (t0, n) in blocks:
                        off = t0 - sc0
                        if g < 8:
                            for pr in range(2):
                                b = g * 4 + pr * 2
                                psa, rpa = self.bank()
                                mm_fm(psa, rpa, wt, rwt, pr * 256, 128, off, n)
                                psb, rpb = self.bank()
                                mm_fm(psb, rpb, wt, rwt, pr * 256 + 128, 128, off, n)
                                dst = self.S_QD if b < 16 else self.S_KD
                                hh = (b % 16) // 2
                                self.rope_epi((f32p, bfp), psa, rpa, psb, rpb, 128, n, ctab, stab, r_tab, off,
                                              dst[hh * 128:(hh + 1) * 128, t0:t0 + n])
                        elif g == 30:
                            psa, rpa = self.bank()
                            mm_fm(psa, rpa, wt, rwt, 0, 64, off, n)
                            psb, rpb = self.bank()
                            mm_fm(psb, rpb, wt, rwt, 64, 64, off, n)
                            self.rope_epi((f32p, bfp), psa, rpa, psb, rpb, 64, n, ctab, stab, r_tab, off, self.S_KR[:, t0:t0 + n])
                        elif g in (16, 17):
                            pss = []
                            for jb in range(4):
                                ps, rps = self.bank()
                                mm_fm(ps, rps, wt, rwt, jb * 128, 128, off, n)
                                P.op("act" if jb % 2 == 0 else "dve",
                                     (lambda h: h.activation(mqs[:, jb, :n], ps[:, :n], AF.Copy)) if jb % 2 == 0 else
                                     (lambda h: h.tensor_copy(mqs[:, jb, :n], ps[:, :n])),
                                     reads=[rps], writes=[r_mqs])
                            rstd, r_rs = rsp()
                            self.fm_rstd([(mqs[:, jb, :n], r_mqs) for jb in range(4)], n, 1.0 / 512, 1, sqp, rstd[:, :n], r_rs)
                            gofs = V_MQN if g == 16 else V_MKVN
                            dst = self.S_MQ if g == 16 else self.S_MKV
                            for jb in range(4):
                                o, ro = bfp()
                                P.op("dve", lambda h: h.tensor_tensor(mqs[:, jb, :n], mqs[:, jb, :n], rstd[:, :n], ALU.mult), reads=[r_mqs, r_rs], writes=[r_mqs])
                                P.op("act", lambda h: h.activation(o[:, :n], mqs[:, jb, :n], AF.Identity, scale=self.vec[:, gofs + jb:gofs + jb + 1]),
                                     reads=[r_mqs, self.r_vec], writes=[ro])
                                P.dma("pool", dst[jb * 128:(jb + 1) * 128, t0:t0 + n], o[:, :n], reads=[ro])
                        else:
                            for jb in range(4):
                                b = g * 4 + jb
                                ps, rps = self.bank()
                                mm_fm(ps, rps, wt, rwt, jb * 128, 128, off, n)
                                if b < 56:
                                    o, ro = f32p()
                                    evi[0] += 1
                                    if evi[0] % 2 == 0:
                                        P.op("act", lambda h: h.activation(o[:, :n], ps[:, :n], AF.Copy), reads=[rps], writes=[ro])
                                    else:
                                        P.op("dve", lambda h: h.tensor_copy(o[:, :n], ps[:, :n]), reads=[rps], writes=[ro])
                                    P.dma("pool", self.S_G[(b - 32) * 128:(b - 31) * 128, t0:t0 + n], o[:, :n], reads=[ro])
                                elif b < 64:
                                    o, ro = bfp()
                                    P.op("act", lambda h: h.activation(o[:, :n], ps[:, :n], AF.Silu), reads=[rps], writes=[ro])
                                    P.dma("pool", self.S_Z[(b - 56) * 128:(b - 55) * 128, t0:t0 + n], o[:, :n], reads=[ro])
                                else:
                                    o, ro = bfp()
                                    P.op("act", lambda h: h.activation(o[:, :n], ps[:, :n], AF.Sigmoid), reads=[rps], writes=[ro])
                                    P.dma("pool", self.S_GT[(b - 72) * 128:(b - 71) * 128, t0:t0 + n], o[:, :n], reads=[ro])
                for gi in range(3):
                    wt, rwt = wp()
                    c0 = 15488 + gi * 512
                    ncol = 512 if gi < 2 else 32
                    self.wload("sp", wt[:, :, :ncol], "w_in", l, list(range(16)), c0, c0 + ncol, rwt)
                    for tt in range(ntok // 128):
                        ps, rps = self.bank()
                        for k in range(16):
                            P.op("pe", lambda h: h.matmul(ps[:, :ncol], H[:, k, tt * 128:(tt + 1) * 128], wt[:, k, :ncol], start=(k == 0), stop=(k == 15)),
                                 reads=[rwt, r_H], writes=[rps], inc=(k == 15))
                        trow = sc0 + tt * 128
                        if gi < 2:
                            o, ro = bfp()
                            if tt % 2 == 0:
                                P.op("act", lambda h: h.activation(o[:, :], ps[:, :], AF.Copy), reads=[rps], writes=[ro])
                            else:
                                P.op("dve", lambda h: h.tensor_copy(o[:, :], ps[:, :]), reads=[rps], writes=[ro])
                            P.dma("pool", self.S_VD[trow:trow + 128, gi * 512:(gi + 1) * 512], o[:, :], reads=[ro])
                        else:
                            o, ro = f32p()
                            P.op("dve", lambda h: h.tensor_copy(o[:, :32], ps[:, :32]), reads=[rps], writes=[ro])
                            P.dma("pool", self.S_AB[trow:trow + 128, :], o[:, :32], reads=[ro])
                P.barrier()

    def phase_mla_up(self, l):
        P = self.P
        with ExitStack() as st:
            wq, r_wq = self.sb(st, [128, 4, 2048], BF16, "wq")
            wkv, r_wkv = self.sb(st, [128, 4, 2048], BF16, "wkv")
            self.wload("sp", wq[:], "mqup", l, [0, 1, 2, 3], 0, 2048, r_wq)
            self.wload("sp", wkv[:], "mkvup", l, [0, 1, 2, 3], 0, 2048, r_wkv)
            ap = self.pool(st, 2, [128, 8, 512], BF16, "mqkv")
            tabp = self.pool(st, 2, [128, 2, 512], F32, "tab")
            f32p = self.pool(st, 4, [128, 512], F32, "stf")
            bfp = self.pool(st, 6, [128, 512], BF16, "stb")
            ev = [0]

            def copy_out(ps, rps, n, dst):
                o, ro = bfp()
                ev[0] += 1
                if ev[0] % 2 == 0:
                    P.op("act", lambda h: h.activation(o[:, :n], ps[:, :n], AF.Copy), reads=[rps], writes=[ro])
                else:
                    P.op("dve", lambda h: h.tensor_copy(o[:, :n], ps[:, :n]), reads=[rps], writes=[ro])
                P.dma("pool", dst, o[:, :n], reads=[ro])

            for (t0, n) in TB:
                a, r_a = ap()
                P.dma("sp", a[:, 0:4, :n], self.S_MQ[:, t0:t0 + n].rearrange("(k p) n -> p k n", p=128), writes=[r_a])
                P.dma("sp", a[:, 4:8, :n], self.S_MKV[:, t0:t0 + n].rearrange("(k p) n -> p k n", p=128), writes=[r_a])
                tab, r_tab = tabp()
                P.dma("sp", tab[:, 0, :n], self.ctab_d[:, t0:t0 + n], writes=[r_tab])
                P.dma("sp", tab[:, 1, :n], self.stab_d[:, t0:t0 + n], writes=[r_tab])

                def mm(ps, rps, w, rw, c0, kofs):
                    for k in range(4):
                        P.op("pe", lambda h: h.matmul(ps[:, :n], w[:, k, c0:c0 + 128], a[:, kofs + k, :n], start=(k == 0), stop=(k == 3)),
                             reads=[rw, r_a], writes=[rps], inc=(k == 3))
                for hh in range(8):
                    ps, rps = self.bank()
                    mm(ps, rps, wq, r_wq, hh * 128, 0)
                    copy_out(ps, rps, n, self.S_MQN[hh * 128:(hh + 1) * 128, t0:t0 + n])
                for j in range(4):
                    psa, rpa = self.bank()
                    mm(psa, rpa, wq, r_wq, 1024 + j * 256, 0)
                    psb, rpb = self.bank()
                    mm(psb, rpb, wq, r_wq, 1024 + j * 256 + 128, 0)
                    self.rope_epi((f32p, bfp), psa, rpa, psb, rpb, 128, n, tab[:, 0, :], tab[:, 1, :], r_tab, 0,
                                  self.S_MQR[j * 128:(j + 1) * 128, t0:t0 + n])
                for hh in range(8):
                    ps, rps = self.bank()
                    mm(ps, rps, wkv, r_wkv, hh * 128, 4)
                    copy_out(ps, rps, n, self.S_MKN[hh * 128:(hh + 1) * 128, t0:t0 + n])
                for tt in range(n // 128):
                    for half in range(2):
                        ps, rps = self.bank()
                        for k in range(4):
                            P.op("pe", lambda h: h.matmul(ps[:, :], a[:, 4 + k, tt * 128:(tt + 1) * 128], wkv[:, k, 1024 + half * 512:1024 + (half + 1) * 512],
                                                          start=(k == 0), stop=(k == 3)),
                                 reads=[r_wkv, r_a], writes=[rps], inc=(k == 3))
                        copy_out(ps, rps, 512, self.S_VM[t0 + tt * 128:t0 + (tt + 1) * 128, half * 512:(half + 1) * 512])
            P.barrier()

    def attn_core(self, kts, n, s_ops, V, r_V, scale, ptp, sbanks, ol):
        P = self.P
        (O, rO), (L, rL) = ol
        nk = len(kts)
        look = 2
        pts = {}

        def emit_s(idx):
            kt = kts[idx]
            S, rS = sbanks()
            for j, (lf, rhs, rd) in enumerate(s_ops):
                P.op("pe", lambda h: h.matmul(S[:, :n], lf(kt), rhs, start=(j == 0), stop=(j == len(s_ops) - 1)),
                     reads=rd, writes=[rS], inc=(j == len(s_ops) - 1))
            Pt, rPt = ptp()
            P.op("act", lambda h: h.activation(Pt[:, :n], S[:, :n], AF.Exp, scale=scale), reads=[rS], writes=[rPt])
            pts[idx] = (Pt, rPt)

        def emit_pv(idx):
            kt = kts[idx]
            Pt, rPt = pts.pop(idx)
            P.op("pe", lambda h: h.matmul(O[:, :n], V[:, kt, :], Pt[:, :n], start=(idx == 0), stop=(idx == nk - 1)),
                 reads=[r_V, rPt], writes=[rO], inc=False)
            P.op("pe", lambda h: h.matmul(L[:, :n], self.ones_bf[:], Pt[:, :n], start=(idx == 0), stop=(idx == nk - 1)),
                 reads=[rPt, self.r_const], writes=[rL], inc=True)

        for idx in range(min(look, nk)):
            emit_s(idx)
        for idx in range(nk):
            if idx + look < nk:
                emit_s(idx + look)
            emit_pv(idx)

    def phase_attn(self, l):
        P = self.P
        with ExitStack() as st:
            ktp = self.pool(st, 2, [128, T], BF16, "kt")
            vp = self.pool(st, 2, [128, NT, 128], BF16, "v")
            kr, r_kr = self.sb(st, [64, T], BF16, "kr")
            qp = self.pool(st, 2, [128, 512], BF16, "q")
            qrp = self.pool(st, 2, [64, 512], BF16, "qr")
            ptp = self.pool(st, 5, [128, 512], BF16, "pt")
            omp = self.pool(st, 4, [128, 512], F32, "om")
            rlp = self.pool(st, 2, [128, 512], F32, "rl")
            bfp = self.pool(st, 3, [128, 512], BF16, "stb")
            sqp = self.pool(st, 2, [128, 512], BF16, "sq")
            rsp = self.pool(st, 2, [128, 512], F32, "rstd")
            srr = [0]

            def sbanks():
                i = srr[0] % 3
                srr[0] += 1
                return self.ps[i], self.rps[i]
            olr = [0]

            def olpair():
                i = olr[0] % 2
                olr[0] += 1
                return (self.ps[3 + i], self.rps[3 + i]), (self.ps[5 + i], self.rps[5 + i])

            P.dma("sp", kr[:], self.S_KR, writes=[r_kr])
            items = [(kind, hh, tb) for kind in ("diff", "mla") for hh in range(8) for tb in TB]
            loaded = {}

            def load(item):
                kind, hh, (t0, n) = item
                key = (kind, hh)
                if key not in loaded:
                    kt, r_kt = ktp()
                    v, r_v = vp()
                    ksrc = self.S_KD if kind == "diff" else self.S_MKN
                    vsrc = self.S_VD if kind == "diff" else self.S_VM
                    P.dma("sp", kt[:], ksrc[hh * 128:(hh + 1) * 128, :], writes=[r_kt])
                    P.dma("sp", v[:], vsrc[:, hh * 128:(hh + 1) * 128].rearrange("(t p) c -> p t c", p=128), writes=[r_v])
                    loaded.clear()
                    loaded[key] = (kt, r_kt, v, r_v)
                q, r_q = qp()
                qsrc = self.S_QD if kind == "diff" else self.S_MQN
                P.dma("sp", q[:, :n], qsrc[hh * 128:(hh + 1) * 128, t0:t0 + n], writes=[r_q])
                qr = r_qr = None
                if kind == "mla":
                    qr, r_qr = qrp()
                    P.dma("sp", qr[:, :n], self.S_MQR[hh * 64:(hh + 1) * 64, t0:t0 + n], writes=[r_qr])
                return loaded[key] + (q, r_q, qr, r_qr)

            nxt = load(items[0])
            for ii, item in enumerate(items):
                kind, hh, (t0, n) = item
                kt, r_kt, v, r_v, q, r_q, qr, r_qr = nxt
                if ii + 1 < len(items):
                    nxt = load(items[ii + 1])
                kts = [0, 1] if t0 < NCTX else list(range(NT))
                if kind == "diff":
                    oms = []
                    for m in range(2):
                        ol = olpair()
                        s_ops = [(lambda k_, m=m: kt[m * 64:(m + 1) * 64, k_ * 128:(k_ + 1) * 128], q[m * 64:(m + 1) * 64, :n], [r_kt, r_q])]
                        self.attn_core(kts, n, s_ops, v, r_v, 0.125, ptp, sbanks, ol)
                        (O, rO), (L, rL) = ol
                        rl, r_rl = rlp()
                        om, r_om = omp()
                        P.op("dve", lambda h: h.reciprocal(rl[:, :n], L[:, :n]), reads=[rL], writes=[r_rl])
                        P.op("dve", lambda h: h.tensor_tensor(om[:, :n], O[:, :n], rl[:, :n], ALU.mult), reads=[rO, r_rl], writes=[r_om])
                        oms.append((om, r_om))
                    (o0, r0), (o1, r1) = oms
                    P.op("dve", lambda h: h.scalar_tensor_tensor(o0[:, :n], o1[:, :n], self.lamc[:, 0:1], o0[:, :n], ALU.mult, ALU.add),
                         reads=[r0, r1, self.r_mod], writes=[r0])
                    rstd, r_rs = rsp()
                    self.fm_rstd([(o0[:, :n], r0)], n, 1.0 / 128, 2, sqp, rstd[:, :n], r_rs, bank=7)
                    P.op("dve", lambda h: h.tensor_tensor(o0[:, :n], o0[:, :n], rstd[:, :n], ALU.mult), reads=[r0, r_rs], writes=[r0])
                    o, ro = bfp()
                    P.op("act", lambda h: h.activation(o[:, :n], o0[:, :n], AF.Identity, scale=self.lamc[:, 1:2]), reads=[r0, self.r_mod], writes=[ro])
                    P.dma("pool", self.S_YA[hh * 128:(hh + 1) * 128, t0:t0 + n], o[:, :n], reads=[ro])
                else:
                    ol = olpair()
                    s_ops = [(lambda k_: kt[:, k_ * 128:(k_ + 1) * 128], q[:, :n], [r_kt, r_q]),
                             (lambda k_: kr[0:64, k_ * 128:(k_ + 1) * 128], qr[0:64, :n], [r_kr, r_qr])]
                    self.attn_core(kts, n, s_ops, v, r_v, 192.0 ** -0.5, ptp, sbanks, ol)
                    (O, rO), (L, rL) = ol
                    rl, r_rl = rlp()
                    P.op("dve", lambda h: h.reciprocal(rl[:, :n], L[:, :n]), reads=[rL], writes=[r_rl])
                    o, ro = bfp()
                    P.op("dve", lambda h: h.tensor_tensor(o[:, :n], O[:, :n], rl[:, :n], ALU.mult), reads=[rO, r_rl], writes=[ro])
                    P.dma("pool", self.S_YC[hh * 128:(hh + 1) * 128, t0:t0 + n], o[:, :n], reads=[ro])
            P.barrier()

    def phase_gdn(self, l):
        P = self.P
        c128 = self.c128
        ident = self.ident()
        with ExitStack() as st:
            be2, r_be2 = self.sb(st, [128, 2, NT * 8], F32, "be2")
            gcum, r_gc = self.sb(st, [128, 2, NT * 8], F32, "gcum")
            egc, r_egc = self.sb(st, [128, 2, NT * 8], F32, "egc")
            etl, r_etl = self.sb(st, [128, 2, NT * 8], F32, "etl")
            egl, r_egl = self.sb(st, [128, 2, NT * 8], F32, "egl")
            bege, r_bege = self.sb(st, [128, 2, NT * 8], F32, "bege")
            st0 = ExitStack()
            ab, r_ab = self.sb(st0, [128, NT, 32], F32, "ab")
            t1, r_t1 = self.sb(st0, [128, NT, 16], F32, "t1")
            t2, r_t2 = self.sb(st0, [128, NT, 16], F32, "t2")
            g2, r_g2 = self.sb(st0, [128, 2, NT * 8], F32, "g2")
            gtot, r_gt = self.sb(st0, [128, 2, NT * 8], F32, "gtot")
            vec = self.vec
            P.dma("sp", ab[:], self.S_AB.rearrange("(t p) c -> p t c", p=128), writes=[r_ab])
            dtb = vec[:, V_DTB:V_DTB + 544].rearrange("p (t c) -> p t c", c=16)
            alog = vec[:, V_ALOG:V_ALOG + 544].rearrange("p (t c) -> p t c", c=16)
            P.op("dve", lambda h: h.tensor_tensor(t1[:], ab[:, :, 0:16], dtb, ALU.add), reads=[r_ab, self.r_vec], writes=[r_t1])
            P.op("act", lambda h: h.activation(t1[:], t1[:], AF.Exp), reads=[r_t1], writes=[r_t1])
            P.op("act", lambda h: h.activation(t1[:], t1[:], AF.Ln, bias=self.cols[:, 0:1]), reads=[r_t1, self.r_const], writes=[r_t1])
            P.op("act", lambda h: h.activation(t2[:], alog, AF.Exp), reads=[self.r_vec], writes=[r_t2])
            P.op("dve", lambda h: h.scalar_tensor_tensor(t1[:], t1[:], -1.0, t2[:], ALU.mult, ALU.mult), reads=[r_t1, r_t2], writes=[r_t1])
            P.op("act", lambda h: h.activation(t2[:], ab[:, :, 16:32], AF.Sigmoid), reads=[r_ab, r_t1], writes=[r_t2])
            for d in range(2):
                P.op("dve", lambda h: h.tensor_copy(g2[:, d, :].rearrange("p (t c) -> p t c", c=8), t1[:, :, d * 8:(d + 1) * 8]), reads=[r_t1], writes=[r_g2])
                P.op("dve", lambda h: h.tensor_copy(be2[:, d, :].rearrange("p (t c) -> p t c", c=8), t2[:, :, d * 8:(d + 1) * 8]), reads=[r_t2], writes=[r_be2])
            for d in range(2):
                tri = c128[:, (C_TRIF if d == 0 else C_TRIB):(C_TRIF if d == 0 else C_TRIB) + 128]
                ps, rps = self.bank()
                P.op("pe", lambda h: h.matmul(ps[:, :272], tri, g2[:, d, :], start=True, stop=True), reads=[r_g2, self.r_const], writes=[rps])
                P.op("dve", lambda h: h.tensor_copy(gcum[:, d, :], ps[:, :272]), reads=[rps], writes=[r_gc])
                ps, rps = self.bank()
                P.op("pe", lambda h: h.matmul(ps[:, :272], self.onesf(), g2[:, d, :], start=True, stop=True), reads=[r_g2, self.r_const], writes=[rps])
                P.op("dve", lambda h: h.tensor_copy(gtot[:, d, :], ps[:, :272]), reads=[rps], writes=[r_gt])
            P.op("act", lambda h: h.activation(egc[:], gcum[:], AF.Exp), reads=[r_gc], writes=[r_egc])
            P.op("act", lambda h: h.activation(egl[:], gtot[:], AF.Exp), reads=[r_gt], writes=[r_egl])
            P.op("dve", lambda h: h.tensor_tensor(etl[:], gtot[:], gcum[:], ALU.subtract), reads=[r_gt, r_gc], writes=[r_etl])
            P.op("act", lambda h: h.activation(etl[:], etl[:], AF.Exp), reads=[r_etl], writes=[r_etl])
            P.op("dve", lambda h: h.tensor_tensor(bege[:], be2[:], egc[:], ALU.mult), reads=[r_be2, r_egc], writes=[r_bege])
            P.barrier()
            st0.close()
            stats = dict(gcum=gcum, egc=egc, etl=etl, egl=egl, beta=be2, bege=bege)

            B, r_B = self.sb(st, [128, 4360], F32, "cbuf")
            qT, r_qT = self.sb(st, [128, T], F32, "qT")
            kT, r_kT = self.sb(st, [128, T], F32, "kT")
            ktm, r_ktm = self.sb(st, [128, NT, 128], F32, "ktm")
            vtm, r_vtm = self.sb(st, [128, NT, 128], F32, "vtm")
            sqp = self.pool(st, 2, [128, 512], BF16, "sq")
            rsp = self.pool(st, 2, [128, 512], F32, "rstd")
            Sst = [self.sb(st, [128, 128], F32, "S0"), self.sb(st, [128, 128], F32, "S1")]
            ostg = [self.pool(st, 2, [128, 4, 128], F32, "ostg0"), self.pool(st, 2, [128, 4, 128], F32, "ostg1")]
            tmps = [{}, {}]

            def tmp(c, name, n=2):
                if name not in tmps[c]:
                    tmps[c][name] = self.pool(st, n, [128, 128], F32, f"{name}{c}")
                    if _os.environ.get('GDN_TRACE'):
                        print("TMP", name, c, "sbuf_base", self.nc.sbuf_base)
                return tmps[c][name]()
            qres = [[Res(f"q{c}_{i}") for i in range(4)] for c in range(2)]
            qrr = [0, 0]

            def qps(c):
                i = qrr[c] % 4
                qrr[c] += 1
                return self.ps[4 * c + i][:, 0:128], qres[c][i]

            P.op("dve", lambda h: h.memset(B[:], 0.0), writes=[r_B])
            import os as _os
            for c_ in range(2):
                for nm_, n_ in (("Dg", 1), ("Db", 1), ("dpre", 1), ("DT", 1), ("Er", 1), ("DTs", 1), ("DTi", 1), ("KKs", 1), ("Yt", 1), ("Y0", 2), ("AiT", 2), ("qd", 2), ("X", 2), ("Yd", 1), ("Xd", 1), ("R", 3), ("Xn", 2), ("Yn", 2), ("Inv", 3), ("Xb", 1), ("Qs", 1), ("Yb", 1), ("Q2s", 1), ("vb", 2), ("kbg", 2), ("ktl", 2), ("u", 2), ("wT", 2), ("vn", 2)):
                    tmp(c_, nm_, n_)
            if _os.environ.get('GDN_STOP') == 'g0':
                return
            for hh in range(int(_os.environ.get('GDN_HEADS', '8'))):
                for typ, (Y, r_Y) in ((2, (kT, r_kT)), (1, (kT, r_kT)), (0, (qT, r_qT))):
                    blk = typ * 8 + hh
                    src = self.S_G[blk * 128:(blk + 1) * 128, :]
                    P.dma("sp", B[:, 1:257], src[:, 0:256], writes=[r_B])
                    P.dma("sp", B[:, 259:4355], src[:, 256:T], writes=[r_B])
                    w = [vec[:, V_CONV + blk * 3 + tp:V_CONV + blk * 3 + tp + 1] for tp in range(3)]
                    for (s0, ln, y0) in ((1, 256, 0), (259, 4096, 256)):
                        e1 = "dve"
                        P.op(e1, lambda h: h.tensor_single_scalar(Y[:, y0:y0 + ln], B[:, s0:s0 + ln], w[1], ALU.mult), reads=[r_B, self.r_vec], writes=[r_Y])
                        P.op(e1, lambda h: h.scalar_tensor_tensor(Y[:, y0:y0 + ln], B[:, s0 - 1:s0 - 1 + ln], w[0], Y[:, y0:y0 + ln], ALU.mult, ALU.add), reads=[r_B, self.r_vec, r_Y], writes=[r_Y])
                        P.op(e1, lambda h: h.scalar_tensor_tensor(Y[:, y0:y0 + ln], B[:, s0 + 1:s0 + 1 + ln], w[2], Y[:, y0:y0 + ln], ALU.mult, ALU.add), reads=[r_B, self.r_vec, r_Y], writes=[r_Y])
                    P.op("act", lambda h: h.activation(Y[:], Y[:], AF.Silu), reads=[r_Y], writes=[r_Y])
                    if typ < 2:
                        for (t0, n) in TB:
                            rstd, r_rs = rsp()
                            self.fm_rstd([(Y[:, t0:t0 + n], r_Y)], n, 1.0, 1, sqp, rstd[:, :n], r_rs)
                            sc_ = (128.0 ** -0.5) if typ == 0 else 1.0
                            P.op("dve", lambda h: h.scalar_tensor_tensor(Y[:, t0:t0 + n], Y[:, t0:t0 + n], sc_, rstd[:, :n], ALU.mult, ALU.mult), reads=[r_Y, r_rs], writes=[r_Y])
                    if typ in (1, 2):
                        dst, r_d = (ktm, r_ktm) if typ == 1 else (vtm, r_vtm)
                        for t4 in range(0, NT, 4):
                            ps, rps = self.bank()
                            nn = min(4, NT - t4)
                            for j in range(nn):
                                P.op("pe", lambda h: h.transpose(ps[:, j * 128:(j + 1) * 128], Y[:, (t4 + j) * 128:(t4 + j + 1) * 128], ident), reads=[r_Y, self.r_const], writes=[rps], inc=(j == nn - 1))
                            if (t4 // 4) % 2 == 0:
                                P.op("act", lambda h: h.activation(dst[:, t4:t4 + nn, :], ps[:, :nn * 128].rearrange("p (t c) -> p t c", c=128), AF.Copy), reads=[rps], writes=[r_d])
                            else:
                                P.op("dve", lambda h: h.tensor_copy(dst[:, t4:t4 + nn, :], ps[:, :nn * 128].rearrange("p (t c) -> p t c", c=128)), reads=[rps], writes=[r_d])
                P.barrier()
                if _os.environ.get('GDN_STOP') == 'g1':
                    continue
                gens = [self.gdn_chain(c, hh, stats, qT, r_qT, kT, r_kT, ktm, r_ktm, vtm, r_vtm, Sst[c], ostg[c], tmp, qps) for c in range(2)]
                P.trace = bool(_os.environ.get('GDN_TRACE'))
                alive = [True, True]
                budget = int(_os.environ.get('GDN_OPS', '100000000'))
                while any(alive) and budget > 0:
                    for c in range(2):
                        if alive[c]:
                            try:
                                next(gens[c])
                                budget -= 1
                            except StopIteration:
                                alive[c] = False
                P.barrier()

    def gdn_chain(self, c, hh, stats, qT, r_qT, kT, r_kT, ktm, r_ktm, vtm, r_vtm, Sres, ostg, tmp, qps):
        P = self.P
        c128 = self.c128
        S, r_S = Sres
        order = list(range(NT)) if c == 0 else [1, 0] + list(range(NT - 1, 1, -1))
        mS = c128[:, (C_MSF if c == 0 else C_MSB):(C_MSF if c == 0 else C_MSB) + 128]
        mI = c128[:, (C_MIF if c == 0 else C_MIB):(C_MIF if c == 0 else C_MIB) + 128]
        ident = self.ident()
        rc = self.r_const
        P.op("pool", lambda h: h.memset(S[:], 0.0), writes=[r_S])
        yield
        ost = None
        import os as _os
        order = order[:int(_os.environ.get('GDN_STEPS', '34'))]
        for si, t in enumerate(order):
            ts = slice(t * 128, (t + 1) * 128)
            ci = t * 8 + hh

            def col(nm):
                return stats[nm][:, c, ci:ci + 1]
            r_st = [Res()]
            Dr, rDr = qps(c)
            Br, rBr = qps(c)
            Dg, r_Dg = tmp(c, "Dg", 1)
            Db, r_Db = tmp(c, "Db", 1)
            P.op("pool", lambda h: h.tensor_single_scalar(Dg[:], ident, col("gcum"), ALU.mult), reads=[rc], writes=[r_Dg]); yield
            P.op("pool", lambda h: h.tensor_single_scalar(Db[:], ident, col("beta"), ALU.mult), reads=[rc], writes=[r_Db]); yield
            P.op("pe", lambda h: h.matmul(Dr, self.onesf(), Dg[:], start=True, stop=True), reads=[r_Dg, rc], writes=[rDr]); yield
            P.op("pe", lambda h: h.matmul(Br, self.onesf(), Db[:], start=True, stop=True), reads=[r_Db, rc], writes=[rBr]); yield
            KK, rKK = qps(c)
            QK, rQK = qps(c)
            P.op("pe", lambda h: h.matmul(KK, kT[:, ts], kT[:, ts], start=True, stop=True), reads=[r_kT], writes=[rKK]); yield
            P.op("pe", lambda h: h.matmul(QK, kT[:, ts], qT[:, ts], start=True, stop=True), reads=[r_kT, r_qT], writes=[rQK]); yield
            dpre, r_dpre = tmp(c, "dpre", 1)
            P.op("dve", lambda h: h.tensor_scalar(dpre[:], Dr, col("gcum"), 0.0, ALU.subtract, ALU.min), reads=[], writes=[r_dpre, rDr]); yield
            DT, r_DT = tmp(c, "DT", 1)
            P.op("act", lambda h: h.activation(DT[:], dpre[:], AF.Exp), reads=[r_dpre], writes=[r_DT]); yield
            Er, r_Er = tmp(c, "Er", 1)
            import os as _o3
            if _o3.environ.get("GDN_NOER"):
                P.op("act", lambda h: h.activation(Er[:], dpre[:], AF.Exp), reads=[r_dpre], writes=[r_Er]); yield
            else:
                P.op("act", lambda h: h.activation(Er[:], Dr, AF.Exp), reads=[], writes=[r_Er, rDr]); yield
            DTs, r_DTs = tmp(c, "DTs", 1)
            P.op("pool", lambda h: h.tensor_tensor(DTs[:], DT[:], mS, ALU.mult), reads=[r_DT, rc], writes=[r_DTs]); yield
            DTi, r_DTi = tmp(c, "DTi", 1)
            P.op("pool", lambda h: h.tensor_tensor(DTi[:], DT[:], mI, ALU.mult), reads=[r_DT, rc], writes=[r_DTi]); yield
            KKs, r_KKs = tmp(c, "KKs", 1)
            P.op("act", lambda h: h.activation(KKs[:], KK, AF.Copy), reads=[], writes=[r_KKs, rKK]); yield
            Yt, r_Yt = tmp(c, "Yt", 1)
            P.op("dve", lambda h: h.tensor_tensor(Yt[:], KKs[:], Br, ALU.mult), reads=[r_KKs], writes=[r_Yt, rBr]); yield
            Y0, r_Y0 = tmp(c, "Y0")
            P.op("pool", lambda h: h.tensor_tensor(Y0[:], Yt[:], DTs[:], ALU.mult), reads=[r_Yt, r_DTs], writes=[r_Y0]); yield
            AiT, r_AiT = tmp(c, "AiT")
            P.op("dve", lambda h: h.tensor_tensor(AiT[:], DTi[:], QK, ALU.mult), reads=[r_DTi], writes=[r_AiT, rQK]); yield
            qd, r_qd = tmp(c, "qd")
            P.op("pool", lambda h: h.tensor_tensor(qd[:], qT[:, ts], Er[:], ALU.mult), reads=[r_qT, r_Er], writes=[r_qd]); yield
            Xp_ps, rXp_ps = qps(c)
            P.op("pe", lambda h: h.transpose(Xp_ps, Y0[:], ident), reads=[r_Y0, rc], writes=[rXp_ps]); yield
            Xp, r_Xp = tmp(c, "X", 2)
            P.op("act", lambda h: h.activation(Xp[:], Xp_ps, AF.Copy), reads=[], writes=[r_Xp, rXp_ps]); yield
            X0, r_X0 = Xp, r_Xp
            bd16 = c128[:, C_BD16:C_BD16 + 128]
            Yd, r_Yd = tmp(c, "Yd", 1)
            P.op("pool", lambda h: h.tensor_tensor(Yd[:], Y0[:], bd16, ALU.mult), reads=[r_Y0, rc], writes=[r_Yd]); yield
            Xd, r_Xd = tmp(c, "Xd", 1)
            P.op("pool", lambda h: h.tensor_tensor(Xd[:], X0[:], bd16, ALU.mult), reads=[r_X0, rc], writes=[r_Xd]); yield
            R, r_R = tmp(c, "R", 3)
            P.op("pool", lambda h: h.tensor_tensor(R[:], Yd[:], ident, ALU.add), reads=[r_Yd, rc], writes=[r_R]); yield
            Yp, r_Yp, Xp, r_Xp = Yd, r_Yd, Xd, r_Xd
            for k in range(1, 4):
                X2, rX2 = qps(c)
                P.op("pe", lambda h: h.matmul(X2, Yp[:], Xp[:], start=True, stop=True), reads=[r_Yp, r_Xp], writes=[rX2]); yield
                Xn, r_Xn = tmp(c, "Xn", 2)
                P.op("act", lambda h: h.activation(Xn[:], X2, AF.Copy), reads=[], writes=[r_Xn, rX2]); yield
                if k <= 2:
                    Y2, rY2 = qps(c)
                    P.op("pe", lambda h: h.matmul(Y2, Xp[:], Yp[:], start=True, stop=True), reads=[r_Yp, r_Xp], writes=[rY2]); yield
                    Yn, r_Yn = tmp(c, "Yn", 2)
                    P.op("dve", lambda h: h.tensor_copy(Yn[:], Y2), reads=[], writes=[r_Yn, rY2]); yield
                pr, rpr = qps(c)
                P.op("pe", lambda h: h.matmul(pr, Xn[:], R[:], start=True, stop=True), reads=[r_Xn, r_R], writes=[rpr]); yield
                P.op("dve", lambda h: h.tensor_tensor(R[:], R[:], pr, ALU.add), reads=[], writes=[r_R, rpr]); yield
                Xp, r_Xp = Xn, r_Xn
                if k <= 2:
                    Yp, r_Yp = Yn, r_Yn
            it_ps, rit_ps = qps(c)
            P.op("pe", lambda h: h.transpose(it_ps, R[:], ident), reads=[r_R, rc], writes=[rit_ps]); yield
            Inv, r_Inv = tmp(c, "Inv", 3)
            P.op("act", lambda h: h.activation(Inv[:], it_ps, AF.Copy), reads=[], writes=[r_Inv, rit_ps]); yield
            for lvl, mko in enumerate((C_O32, C_O64, C_O128)):
                mk = c128[:, mko:mko + 128]
                Xb, r_Xb = tmp(c, "Xb", 1)
                P.op("pool", lambda h: h.tensor_tensor(Xb[:], X0[:], mk, ALU.mult), reads=[r_X0, rc], writes=[r_Xb]); yield
                Q, rQ = qps(c)
                P.op("pe", lambda h: h.matmul(Q, Xb[:], R[:], start=True, stop=True), reads=[r_Xb, r_R], writes=[rQ]); yield
                Qs, r_Qs = tmp(c, "Qs", 1)
                P.op("dve", lambda h: h.tensor_copy(Qs[:], Q), reads=[], writes=[r_Qs, rQ]); yield
                if lvl < 2:
                    Yb, r_Yb = tmp(c, "Yb", 1)
                    P.op("pool", lambda h: h.tensor_tensor(Yb[:], Y0[:], mk, ALU.mult), reads=[r_Y0, rc], writes=[r_Yb]); yield
                    Q2, rQ2 = qps(c)
                    P.op("pe", lambda h: h.matmul(Q2, Yb[:], Inv[:], start=True, stop=True), reads=[r_Yb, r_Inv], writes=[rQ2]); yield
                    Q2s, r_Q2s = tmp(c, "Q2s", 1)
                    P.op("act", lambda h: h.activation(Q2s[:], Q2, AF.Copy), reads=[], writes=[r_Q2s, rQ2]); yield
                P1, rP1 = qps(c)
                P.op("pe", lambda h: h.matmul(P1, Inv[:], Qs[:], start=True, stop=True), reads=[r_Inv, r_Qs], writes=[rP1]); yield
                Rn, r_Rn = tmp(c, "R", 3)
                P.op("dve", lambda h: h.tensor_tensor(Rn[:], R[:], P1, ALU.add), reads=[r_R], writes=[r_Rn, rP1]); yield
                if lvl < 2:
                    P2, rP2 = qps(c)
                    P.op("pe", lambda h: h.matmul(P2, R[:], Q2s[:], start=True, stop=True), reads=[r_R, r_Q2s], writes=[rP2]); yield
                    Invn, r_Invn = tmp(c, "Inv", 3)
                    P.op("dve", lambda h: h.tensor_tensor(Invn[:], Inv[:], P2, ALU.add), reads=[r_Inv], writes=[r_Invn, rP2]); yield
                    Inv, r_Inv = Invn, r_Invn
                R, r_R = Rn, r_Rn
            vb, r_vb = tmp(c, "vb")
            P.op("pool", lambda h: h.tensor_single_scalar(vb[:], vtm[:, t, :], col("beta"), ALU.mult), reads=[r_vtm], writes=[r_vb]); yield
            kbg, r_kbg = tmp(c, "kbg")
            P.op("pool", lambda h: h.tensor_single_scalar(kbg[:], ktm[:, t, :], col("bege"), ALU.mult), reads=[r_ktm], writes=[r_kbg]); yield
            ktl, r_ktl = tmp(c, "ktl")
            P.op("pool", lambda h: h.tensor_single_scalar(ktl[:], ktm[:, t, :], col("etl"), ALU.mult), reads=[r_ktm], writes=[r_ktl]); yield
            u_ps, ru_ps = qps(c)
            P.op("pe", lambda h: h.matmul(u_ps, R[:], vb[:], start=True, stop=True), reads=[r_R, r_vb], writes=[ru_ps]); yield
            w_ps, rw_ps = qps(c)
            P.op("pe", lambda h: h.matmul(w_ps, kbg[:], R[:], start=True, stop=True), reads=[r_R, r_kbg], writes=[rw_ps]); yield
            u, r_u = tmp(c, "u")
            P.op("act", lambda h: h.activation(u[:], u_ps, AF.Copy), reads=[], writes=[r_u, ru_ps]); yield
            wT, r_wT = tmp(c, "wT")
            P.op("dve", lambda h: h.tensor_copy(wT[:], w_ps), reads=[], writes=[r_wT, rw_ps]); yield
            p1, rp1 = qps(c)
            P.op("pe", lambda h: h.matmul(p1, wT[:], S[:], start=True, stop=True), reads=[r_wT, r_S], writes=[rp1]); yield
            vn, r_vn = tmp(c, "vn")
            P.op("dve", lambda h: h.tensor_tensor(vn[:], u[:], p1, ALU.subtract), reads=[r_u], writes=[r_vn, rp1]); yield
            o_ps, ro_ps = qps(c)
            P.op("pe", lambda h: h.matmul(o_ps, S[:], qd[:], start=True, stop=False), reads=[r_S, r_qd], writes=[ro_ps], inc=False); yield
            P.op("pe", lambda h: h.matmul(o_ps, vn[:], AiT[:], start=False, stop=True), reads=[r_vn, r_AiT], writes=[ro_ps]); yield
            p3, rp3 = qps(c)
            P.op("pe", lambda h: h.matmul(p3, ktl[:], vn[:], start=True, stop=True), reads=[r_ktl, r_vn], writes=[rp3]); yield
            if si % 4 == 0:
                ost = ostg()
                ost_t0 = t
            P.op("act", lambda h: h.activation(ost[0][:, si % 4, :], o_ps, AF.Copy), reads=[], writes=[ost[1], ro_ps]); yield
            P.op("dve", lambda h: h.scalar_tensor_tensor(S[:], S[:], col("egl"), p3, ALU.mult, ALU.add), reads=[r_S], writes=[r_S, rp3]); yield
            if si % 4 == 3 or si == len(order) - 1:
                nst = si % 4 + 1
                for j in range(nst):
                    tt = order[si - nst + 1 + j]
                    P.dma("pool", self.S_OT[c][hh * 128:(hh + 1) * 128, tt * 128:(tt + 1) * 128], ost[0][:, j, :], reads=[ost[1]])
                yield

    def phase_gdn_out(self, l):
        P = self.P
        with ExitStack() as st:
            op0 = self.pool(st, 2, [128, 8, 512], F32, "o0")
            op1 = self.pool(st, 2, [128, 8, 512], F32, "o1")
            zp = self.pool(st, 2, [128, 8, 512], BF16, "z")
            yp = self.pool(st, 2, [128, 8, 512], BF16, "yb")
            sqp = self.pool(st, 2, [128, 512], BF16, "sq")
            rsp = self.pool(st, 2, [128, 512], F32, "rstd")
            for (t0, n) in TB:
                o0, r0 = op0()
                o1, r1 = op1()
                z, rz = zp()
                y, ry = yp()
                P.dma("sp", o0[:, :, :n], self.S_OT[0][:, t0:t0 + n].rearrange("(h p) n -> p h n", p=128), writes=[r0])
                P.dma("sp", o1[:, :, :n], self.S_OT[1][:, t0:t0 + n].rearrange("(h p) n -> p h n", p=128), writes=[r1])
                P.dma("sp", z[:, :, :n], self.S_Z[:, t0:t0 + n].rearrange("(h p) n -> p h n", p=128), writes=[rz])
                P.op("pool", lambda h: h.tensor_tensor(o0[:, :, :n], o0[:, :, :n], o1[:, :, :n], ALU.add), reads=[r0, r1], writes=[r0])
                for hh in range(8):
                    rstd, r_rs = rsp()
                    self.fm_rstd([(o0[:, hh, :n], r0)], n, 1.0 / 128, 1, sqp, rstd[:, :n], r_rs)
                    P.op("dve", lambda h: h.tensor_tensor(o0[:, hh, :n], o0[:, hh, :n], rstd[:, :n], ALU.mult), reads=[r0, r_rs], writes=[r0])
                    P.op("dve", lambda h: h.tensor_tensor(o0[:, hh, :n], o0[:, hh, :n], z[:, hh, :n], ALU.mult), reads=[r0, rz], writes=[r0])
                    P.op("act", lambda h: h.activation(y[:, hh, :n], o0[:, hh, :n], AF.Identity, scale=self.vec[:, V_ONORM:V_ONORM + 1]),
                         reads=[r0, self.r_vec], writes=[ry])
                P.dma("pool", self.S_YB[:, t0:t0 + n].rearrange("(h p) n -> p h n", p=128), y[:, :, :n], reads=[ry])
            P.barrier()

    def phase_merge1(self, l):
        P = self.P
        with ExitStack() as st:
            yps = [self.pool(st, 2, [128, 8, 512], BF16, f"y{i}") for i in range(3)]
            wp = self.pool(st, 2, [128, 24, 512], BF16, "wbr")
            gp = self.pool(st, 2, [128, 3, 4, 512], BF16, "gt")
            accp = self.pool(st, 3, [128, 512], F32, "acc")
            tp = self.pool(st, 3, [128, 512], F32, "tt")
            bfp = self.pool(st, 3, [128, 512], BF16, "stb")
            srcs = (self.S_YA, self.S_YB, self.S_YC)
            for (t0, n) in TB:
                ys = []
                for i in range(3):
                    y, ry = yps[i]()
                    P.dma("sp", y[:, :, :n], srcs[i][:, t0:t0 + n].rearrange("(k p) n -> p k n", p=128), writes=[ry])
                    ys.append((y, ry))
                for cg in range(4):
                    w, rw = wp()
                    self.wload("sp", w[:], "wbr", l, list(range(24)), cg * 512, (cg + 1) * 512, rw)
                    gt, rg = gp()
                    for i in range(3):
                        P.dma("sp", gt[:, i, :, :n], self.S_GT[i * 2048 + cg * 512:i * 2048 + (cg + 1) * 512, t0:t0 + n].rearrange("(j p) n -> p j n", p=128), writes=[rg])
                    for jb in range(4):
                        cb = cg * 4 + jb
                        pss = []
                        for i in range(3):
                            ps, rps = self.bank()
                            y, ry = ys[i]
                            for k in range(8):
                                P.op("pe", lambda h: h.matmul(ps[:, :n], w[:, i * 8 + k, jb * 128:(jb + 1) * 128], y[:, k, :n], start=(k == 0), stop=(k == 7)),
                                     reads=[rw, ry], writes=[rps], inc=(k == 7))
                            pss.append((ps, rps))
                        acc, racc = accp()
                        P.op("dve", lambda h: h.tensor_tensor(acc[:, :n], pss[0][0][:, :n], gt[:, 0, jb, :n], ALU.mult), reads=[pss[0][1], rg], writes=[racc])
                        t1, rt1 = tp()
                        P.op("dve", lambda h: h.tensor_tensor(t1[:, :n], pss[1][0][:, :n], gt[:, 1, jb, :n], ALU.mult), reads=[pss[1][1], rg], writes=[rt1])
                        P.op("pool", lambda h: h.tensor_tensor(acc[:, :n], acc[:, :n], t1[:, :n], ALU.add), reads=[rt1], writes=[racc])
                        t2, rt2 = tp()
                        P.op("dve", lambda h: h.tensor_tensor(t2[:, :n], pss[2][0][:, :n], gt[:, 2, jb, :n], ALU.mult), reads=[pss[2][1], rg], writes=[rt2])
                        o, ro = bfp()
                        P.op("pool", lambda h: h.tensor_tensor(o[:, :n], acc[:, :n], t2[:, :n], ALU.add), reads=[racc, rt2], writes=[ro])
                        P.dma("pool", self.S_YS[cb * 128:(cb + 1) * 128, t0:t0 + n], o[:, :n], reads=[ro])
            P.barrier()

    def phase_merge2(self, l, xsrc):
        P = self.P
        mv = self.mv
        with ExitStack() as st:
            ysp = self.pool(st, 1, [128, 16, 512], BF16, "ys")
            xp = self.pool(st, 1, [128, 16, 512], F32, "x")
            y2p = self.pool(st, 1, [128, 16, 512], F32, "y2")
            h2p = self.pool(st, 1, [128, 16, 512], BF16, "h2")
            wp = self.pool(st, 2, [128, 16, 512], BF16, "wout")
            sqp = self.pool(st, 2, [128, 512], BF16, "sq")
            rsp = self.pool(st, 2, [128, 512], F32, "rstd")
            for (t0, n) in TB:
                v = 1 if t0 < NCTX else 0
                ysb, rys = ysp()
                x, rx = xp()
                y2, ry2 = y2p()
                P.dma("sp", ysb[:, :, :n], self.S_YS[:, t0:t0 + n].rearrange("(k p) n -> p k n", p=128), writes=[rys])
                P.dma("sp", x[:, :, :n], xsrc[:, t0:t0 + n].rearrange("(k p) n -> p k n", p=128), writes=[rx])
                for cg in range(4):
                    w, rw = wp()
                    self.wload("sp", w[:], "wout", l, list(range(16)), cg * 512, (cg + 1) * 512, rw)
                    for jb in range(4):
                        cb = cg * 4 + jb
                        ps, rps = self.bank()
                        for k in range(16):
                            P.op("pe", lambda h: h.matmul(ps[:, :n], w[:, k, jb * 128:(jb + 1) * 128], ysb[:, k, :n], start=(k == 0), stop=(k == 15)),
                                 reads=[rw, rys], writes=[rps], inc=(k == 15))
                        if cb % 2 == 0:
                            P.op("act", lambda h: h.activation(y2[:, cb, :n], ps[:, :n], AF.Copy), reads=[rps], writes=[ry2])
                        else:
                            P.op("dve", lambda h: h.tensor_copy(y2[:, cb, :n], ps[:, :n]), reads=[rps], writes=[ry2])
                rstd, r_rs = rsp()
                self.fm_rstd([(y2[:, cb, :n], ry2) for cb in range(16)], n, 1.0 / D, 1, sqp, rstd[:, :n], r_rs)
                for cb in range(16):
                    P.op("dve", lambda h: h.tensor_tensor(y2[:, cb, :n], y2[:, cb, :n], rstd[:, :n], ALU.mult), reads=[ry2, r_rs], writes=[ry2])
                    P.op("dve", lambda h: h.scalar_tensor_tensor(x[:, cb, :n], y2[:, cb, :n], mv[:, 2, cb, v:v + 1], x[:, cb, :n], ALU.mult, ALU.add),
                         reads=[ry2, self.r_mod], writes=[rx])
                P.dma("pool", self.xout[:, t0:t0 + n].rearrange("(k p) n -> p k n", p=128), x[:, :, :n], reads=[rx])
                rstd2, r_rs2 = rsp()
                self.fm_rstd([(x[:, cb, :n], rx) for cb in range(16)], n, 1.0 / D, 1, sqp, rstd2[:, :n], r_rs2)
                h2, rh2 = h2p()
                for cb in range(16):
                    P.op("dve", lambda h: h.tensor_tensor(y2[:, cb, :n], x[:, cb, :n], rstd2[:, :n], ALU.mult), reads=[rx, r_rs2], writes=[ry2])
                    P.op("act", lambda h: h.activation(h2[:, cb, :n], y2[:, cb, :n], AF.Identity, bias=mv[:, 4, cb, v:v + 1], scale=mv[:, 3, cb, v:v + 1]),
                         reads=[ry2, self.r_mod], writes=[rh2])
                P.dma("pool", self.S_H2[:, t0:t0 + n].rearrange("(k p) n -> p k n", p=128), h2[:, :, :n], reads=[rh2])
            P.barrier()

    def phase_ffn1(self, l):
        P = self.P
        for blocks in SCS:
            sc0 = blocks[0][0]
            ntok = sum(b[1] for b in blocks)
            with ExitStack() as st:
                H, r_H = self.sb(st, [128, 16, 2304], BF16, "H2")
                for (t0, n) in blocks:
                    P.dma("sp", H[:, :, t0 - sc0:t0 - sc0 + n], self.S_H2[:, t0:t0 + n].rearrange("(k p) n -> p k n", p=128), writes=[r_H])
                wgp = self.pool(st, 2, [128, 16, 512], BF16, "wg")
                wup = self.pool(st, 2, [128, 16, 512], BF16, "wu")
                sgp = self.pool(st, 3, [128, 512], F32, "sg")
                bfp = self.pool(st, 3, [128, 512], BF16, "stb")
                for hg in range(11):
                    wg, rwg = wgp()
                    wu, rwu = wup()
                    self.wload("sp", wg[:], "wg", l, list(range(16)), hg * 512, (hg + 1) * 512, rwg)
                    self.wload("sp", wu[:], "wu", l, list(range(16)), hg * 512, (hg + 1) * 512, rwu)
                    for jb in range(4):
                        j = hg * 4 + jb
                        for (t0, n) in blocks:
                            off = t0 - sc0
                            pg, rpg = self.bank()
                            for k in range(16):
                                P.op("pe", lambda h: h.matmul(pg[:, :n], wg[:, k, jb * 128:(jb + 1) * 128], H[:, k, off:off + n], start=(k == 0), stop=(k == 15)),
                                     reads=[rwg, r_H], writes=[rpg], inc=(k == 15))
                            pu, rpu = self.bank()
                            for k in range(16):
                                P.op("pe", lambda h: h.matmul(pu[:, :n], wu[:, k, jb * 128:(jb + 1) * 128], H[:, k, off:off + n], start=(k == 0), stop=(k == 15)),
                                     reads=[rwu, r_H], writes=[rpu], inc=(k == 15))
                            sg, rsg = sgp()
                            P.op("act", lambda h: h.activation(sg[:, :n], pg[:, :n], AF.Silu), reads=[rpg], writes=[rsg])
                            o, ro = bfp()
                            P.op("dve", lambda h: h.tensor_tensor(o[:, :n], sg[:, :n], pu[:, :n], ALU.mult), reads=[rsg, rpu], writes=[ro])
                            P.dma("pool", self.S_ACT[j * 128:(j + 1) * 128, t0:t0 + n], o[:, :n], reads=[ro])
                P.barrier()

    def phase_ffn2(self, l):
        P = self.P
        with ExitStack() as st:
            ap_ = self.pool(st, 1, [128, FKC, 512], BF16, "act")
            wp = self.pool(st, 2, [128, FKC, 512], BF16, "wd")
            fp_ = self.pool(st, 4, [128, 512], F32, "stf")
            ev = 0
            for (t0, n) in TB:
                a, ra = ap_()
                P.dma("sp", a[:, 0:22, :n], self.S_ACT[0:22 * 128, t0:t0 + n].rearrange("(k p) n -> p k n", p=128), writes=[ra])
                P.dma("sp", a[:, 22:44, :n], self.S_ACT[22 * 128:44 * 128, t0:t0 + n].rearrange("(k p) n -> p k n", p=128), writes=[ra])
                for cg in range(4):
                    w, rw = wp()
                    self.wload("sp", w[:, 0:22, :], "wd", l, list(range(22)), cg * 512, (cg + 1) * 512, rw)
                    self.wload("sp", w[:, 22:44, :], "wd", l, list(range(22, 44)), cg * 512, (cg + 1) * 512, rw)
                    for jb in range(4):
                        cb = cg * 4 + jb
                        ps, rps = self.bank()
                        for k in range(FKC):
                            P.op("pe", lambda h: h.matmul(ps[:, :n], w[:, k, jb * 128:(jb + 1) * 128], a[:, k, :n], start=(k == 0), stop=(k == FKC - 1)),
                                 reads=[rw, ra], writes=[rps], inc=(k == FKC - 1))
                        o, ro = fp_()
                        ev += 1
                        if ev % 2 == 0:
                            P.op("act", lambda h: h.activation(o[:, :n], ps[:, :n], AF.Copy), reads=[rps], writes=[ro])
                        else:
                            P.op("dve", lambda h: h.tensor_copy(o[:, :n], ps[:, :n]), reads=[rps], writes=[ro])
                        P.dma("pool", self.S_F[cb * 128:(cb + 1) * 128, t0:t0 + n], o[:, :n], reads=[ro])
            P.barrier()

    def phase_ffn3(self, l):
        P = self.P
        mv = self.mv
        with ExitStack() as st:
            fp_ = self.pool(st, 2, [128, 16, 512], F32, "f")
            xp = self.pool(st, 2, [128, 16, 512], F32, "x")
            sqp = self.pool(st, 2, [128, 512], BF16, "sq")
            rsp = self.pool(st, 2, [128, 512], F32, "rstd")
            for (t0, n) in TB:
                v = 1 if t0 < NCTX else 0
                f, rf = fp_()
                x, rx = xp()
                P.dma("sp", f[:, :, :n], self.S_F[:, t0:t0 + n].rearrange("(k p) n -> p k n", p=128), writes=[rf])
                P.dma("sp", x[:, :, :n], self.xout[:, t0:t0 + n].rearrange("(k p) n -> p k n", p=128), writes=[rx])
                rstd, r_rs = rsp()
                self.fm_rstd([(f[:, cb, :n], rf) for cb in range(16)], n, 1.0 / D, 1, sqp, rstd[:, :n], r_rs)
                for cb in range(16):
                    P.op("dve", lambda h: h.tensor_tensor(f[:, cb, :n], f[:, cb, :n], rstd[:, :n], ALU.mult), reads=[rf, r_rs], writes=[rf])
                    P.op("dve", lambda h: h.scalar_tensor_tensor(x[:, cb, :n], f[:, cb, :n], mv[:, 5, cb, v:v + 1], x[:, cb, :n], ALU.mult, ALU.add),
                         reads=[rf, self.r_mod], writes=[rx])
                P.dma("pool", self.xout[:, t0:t0 + n].rearrange("(k p) n -> p k n", p=128), x[:, :, :n], reads=[rx])
            P.barrier()

    def build(self):
        P = self.P
        first = True
        for li, l in enumerate(self.layers):
            if li == 0:
                self.issue_casts(l)
            if li + 1 < len(self.layers):
                self.issue_casts(self.layers[li + 1])
            xsrc = self.xin if first else self.xout
            first = False
            self.phase_mod(l)
            if self.stop_after == "mod":
                break
            self.phase_p1(l, xsrc)
            self.phase_mla_up(l)
            if self.stop_after == "p1":
                break
            if "attn" not in self.skip:
                self.phase_attn(l)
            if self.stop_after == "attn":
                break
            if "gdn" not in self.skip:
                self.phase_gdn(l)
                import os as _os2
                if not _os2.environ.get('GDN_NOOUT'):
                    self.phase_gdn_out(l)
            if self.stop_after == "gdn":
                break
            self.phase_merge1(l)
            self.phase_merge2(l, xsrc)
            if self.stop_after == "merge":
                break
            self.phase_ffn1(l)
            self.phase_ffn2(l)
            self.phase_ffn3(l)
        P.barrier()
        return self.nc


_CONSTS = None


def _layer_inputs(inp, l, tag):
    d = {}
    d[f"w_ada{tag}"] = np.ascontiguousarray(inp["w_ada"][l])
    d[f"w_in{tag}"] = np.ascontiguousarray(inp["w_in"][l][:, _win_cols()])
    d[f"mqup{tag}"] = np.ascontiguousarray(inp["mla_q_up"][l][:, _mqup_cols()])
    d[f"mkvup{tag}"] = np.ascontiguousarray(inp["mla_kv_up"][l][:, _mkvup_cols()])
    d[f"wbr{tag}"] = np.ascontiguousarray(inp["w_branch"][l].reshape(3072, D))
    d[f"wout{tag}"] = np.ascontiguousarray(inp["w_out"][l])
    d[f"wg{tag}"] = np.ascontiguousarray(inp["w_ffn_gate"][l])
    d[f"wu{tag}"] = np.ascontiguousarray(inp["w_ffn_up"][l])
    d[f"wd{tag}"] = np.ascontiguousarray(inp["w_ffn_down"][l])
    d[f"vec{tag}"] = _pack_vec(inp, l)
    return d


def _core_inputs(inp, b):
    global _CONSTS
    if _CONSTS is None:
        _CONSTS = _consts()
    c128, sel, ctab, stab = _CONSTS
    cT = np.stack([inp["c"][b].reshape(16, 128).T, inp["c_ctx"].reshape(16, 128).T], axis=-1).reshape(128, 32)
    return {"cT": np.ascontiguousarray(cT, dtype=np.float32), "c128": c128, "ctab": ctab, "stab": stab}


FUSED = True
_PROG_CACHE = {}


def _get_prog(layers):
    key = tuple(layers)
    if key not in _PROG_CACHE:
        kb = KB(list(layers))
        _PROG_CACHE[key] = kb.build()
    return _PROG_CACHE[key]


def kernel(**inputs):
    inp = {k: np.asarray(v) for k, v in inputs.items()}
    B = inp["x"].shape[0]
    cores = list(range(B))
    xT = [np.ascontiguousarray(np.concatenate([inp["ctx"][b], inp["x"][b]], axis=0).T.astype(np.float32)) for b in range(B)]
    base = [_core_inputs(inp, b) for b in range(B)]
    if FUSED:
        nc = _get_prog(range(DEPTH))
        wl = {}
        for l in range(DEPTH):
            wl.update(_layer_inputs(inp, l, str(l)))
        in_maps = []
        for b in range(B):
            m = dict(base[b])
            m.update(wl)
            m["xin"] = xT[b]
            in_maps.append(m)
        res = run_bass_kernel_spmd(nc, in_maps, core_ids=cores)
        xT = [np.asarray(res.results[b]["xout"]) for b in range(B)]
    else:
        nc = _get_prog([0])
        for l in range(DEPTH):
            wl = _layer_inputs(inp, l, "0")
            in_maps = []
            for b in range(B):
                m = dict(base[b])
                m.update(wl)
                m["xin"] = xT[b]
                in_maps.append(m)
            res = run_bass_kernel_spmd(nc, in_maps, core_ids=cores)
            xT = [np.ascontiguousarray(np.asarray(res.results[b]["xout"])) for b in range(B)]
    out = np.stack([xT[b][:, NCTX:].T for b in range(B)], axis=0)
    return np.ascontiguousarray(out.astype(np.float32))
```
